# Optimizing a Trainium2 kernel written in Bass

```python
import jax, jax.numpy as jnp
from jax import lax
import numpy as np

D_MODEL = 4096
BATCH = 4
SEQ = 2048
DEPTH = 1
DEC_BATCH = 128
DEC_SEQ = 4
PAST_LEN = 16384
PAGE_SIZE = 128

D_SC = D_MODEL // 2
SC_CONV_W = 3
SSD_HEADDIM = 64
D_SSD = D_MODEL
SSD_HEADS = D_SSD // SSD_HEADDIM
SSD_GROUPS = 8
SSD_D_STATE = 128
SSD_CONV_W = 4
SSD_CHUNK = 128
D_XBC = D_SSD + 2 * SSD_GROUPS * SSD_D_STATE
D_MIX = D_SC + D_SSD
D_IN = 3 * D_SC + D_SSD + D_XBC + SSD_HEADS
D_FF = ((8 * D_MODEL // 3 + 255) // 256) * 256
FFN_CONV_W = 3
N_MOD = 6
EPS = 1e-6

kernel_name = 'hybrid_shortconv_ssd_convffn_adaln_step'


def rmsnorm(x, g):
    xf = x.astype(jnp.float32)
    r = lax.rsqrt(jnp.mean(xf * xf, axis=-1, keepdims=True) + EPS)
    return (xf * r).astype(x.dtype) * g


def causal_dwconv(u, buf, w, b=None):
    K = w.shape[0]
    L = u.shape[1]
    cat = jnp.concatenate([buf.astype(u.dtype), u], axis=1)
    out = cat[:, 0:L] * w[0]
    for k in range(1, K):
        out = out + cat[:, k:k + L] * w[k]
    if b is not None:
        out = out + b
    return out, cat[:, L:]


def ssd_scan(x, dt, A, Bm, Cm, init_state):
    b, L, H, P = x.shape
    G, N = Bm.shape[2], Bm.shape[3]
    R = H // G
    Q = min(SSD_CHUNK, L)
    nc = -(-L // Q)
    pad = nc * Q - L

    def chunk(t):
        t = t.astype(jnp.float32)
        t = jnp.pad(t, [(0, 0), (0, pad)] + [(0, 0)] * (t.ndim - 2))
        return t.reshape((b, nc, Q) + t.shape[2:])

    xc = chunk(x).reshape(b, nc, Q, G, R, P)
    dtc = chunk(dt).reshape(b, nc, Q, G, R)
    Bc = chunk(Bm)
    Cc = chunk(Cm)
    xdt = xc * dtc[..., None]
    ac = jnp.moveaxis(jnp.cumsum(dtc * A.reshape(G, R), axis=2), 2, -1)
    seg = ac[..., :, None] - ac[..., None, :]
    causal = jnp.tril(jnp.ones((Q, Q), dtype=bool))
    decay_in = jnp.exp(jnp.where(causal, seg, -jnp.inf))
    cb = jnp.einsum('bcign,bcjgn->bcgij', Cc, Bc)
    y_diag = jnp.einsum('bcgij,bcgrij,bcjgrp->bcigrp', cb, decay_in, xdt)
    decay_to_end = jnp.exp(ac[..., -1:] - ac)
    states = jnp.einsum('bcjgn,bcgrj,bcjgrp->bcgrpn', Bc, decay_to_end, xdt)
    chunk_decay = jnp.exp(ac[..., -1])

    def step(carry, inp):
        st, dec = inp
        return carry * dec[..., None, None] + st, carry

    s0 = init_state.astype(jnp.float32).reshape(b, G, R, P, N)
    final, prev = lax.scan(step, s0, (jnp.moveaxis(states, 1, 0), jnp.moveaxis(chunk_decay, 1, 0)))
    prev = jnp.moveaxis(prev, 0, 1)
    y_off = jnp.einsum('bcign,bcgri,bcgrpn->bcigrp', Cc, jnp.exp(ac), prev)
    y = (y_diag + y_off).reshape(b, nc * Q, H, P)[:, :L]
    return y.astype(x.dtype), final.reshape(b, H, P, N).astype(x.dtype)


def layer(x, c, sc_buf, ssd_buf, ssm0, ffn_buf,
          w_ada, b_ada, g_norm1, w_in, w_sc_conv, w_ssd_conv, b_ssd_conv,
          dt_bias, a_log, d_skip, g_ssd_norm, w_out, g_norm2, w_up, w_ffn_conv,
          b_ffn_conv, w_down):
    b, L, _ = x.shape
    mod = (jax.nn.silu(c) @ w_ada + b_ada).reshape(b, N_MOD, D_MODEL)[:, :, None, :]
    shift1, scale1, gate1, shift2, scale2, gate2 = [mod[:, i] for i in range(N_MOD)]

    h = rmsnorm(x, g_norm1) * (1.0 + scale1) + shift1
    proj = h @ w_in
    cuts = [int(v) for v in np.cumsum([D_SC, D_SC, D_SC, D_SSD, D_XBC])]
    sc_b, sc_c, sc_x, z, xbc, dt_raw = jnp.split(proj, cuts, axis=-1)
    u = sc_c * sc_x
    uc, new_sc_buf = causal_dwconv(u, sc_buf, w_sc_conv)
    y_sc = sc_b * uc
    xbc_c, new_ssd_buf = causal_dwconv(xbc, ssd_buf, w_ssd_conv, b_ssd_conv)
    xbc_c = jax.nn.silu(xbc_c)
    xs, Bm, Cm = jnp.split(xbc_c, [D_SSD, D_SSD + SSD_GROUPS * SSD_D_STATE], axis=-1)
    xs = xs.reshape(b, L, SSD_HEADS, SSD_HEADDIM)
    Bm = Bm.reshape(b, L, SSD_GROUPS, SSD_D_STATE)
    Cm = Cm.reshape(b, L, SSD_GROUPS, SSD_D_STATE)
    dt = jax.nn.softplus(dt_raw.astype(jnp.float32) + dt_bias.astype(jnp.float32))
    A = -jnp.exp(a_log.astype(jnp.float32))
    y_ssd, new_ssm = ssd_scan(xs, dt, A, Bm, Cm, ssm0)
    y_ssd = (y_ssd + d_skip[:, None] * xs).reshape(b, L, D_SSD)
    y_ssd = rmsnorm(y_ssd * jax.nn.silu(z), g_ssd_norm)
    mix = jnp.concatenate([y_sc, y_ssd], axis=-1) @ w_out
    x = x + gate1 * mix

    h2 = rmsnorm(x, g_norm2) * (1.0 + scale2) + shift2
    up = h2 @ w_up
    upc, new_ffn_buf = causal_dwconv(up, ffn_buf, w_ffn_conv, b_ffn_conv)
    g, v = jnp.split(upc, 2, axis=-1)
    x = x + gate2 * ((jax.nn.silu(g) * v) @ w_down)
    return x, new_sc_buf, new_ssd_buf, new_ssm, new_ffn_buf


def setup_inputs(seed: int = 0) -> dict:
    key = jax.random.key(seed)
    ks = jax.random.split(key, 32)
    f = jnp.float32
    nrm = lambda k, s, sc: jax.random.normal(k, s, f) * sc
    dt0 = jnp.exp(jax.random.uniform(ks[20], (DEPTH, SSD_HEADS), f, np.log(1e-3), np.log(1e-1)))
    return {
        'x_prompt': nrm(ks[0], (BATCH, SEQ, D_MODEL), 1.0),
        'x_sample': nrm(ks[1], (DEC_BATCH, DEC_SEQ, D_MODEL), 1.0),
        'c_prompt': nrm(ks[2], (BATCH, D_MODEL), 1.0),
        'c_sample': nrm(ks[3], (DEC_BATCH, D_MODEL), 1.0),
        'state_sc_conv': nrm(ks[4], (DEPTH, DEC_BATCH, SC_CONV_W - 1, D_SC), 1.0),
        'state_ssd_conv': nrm(ks[5], (DEPTH, DEC_BATCH, SSD_CONV_W - 1, D_XBC), 1.0),
        'state_ssm': nrm(ks[6], (DEPTH, DEC_BATCH, SSD_HEADS, SSD_HEADDIM, SSD_D_STATE), 0.1),
        'state_ffn_conv': nrm(ks[7], (DEPTH, DEC_BATCH, FFN_CONV_W - 1, 2 * D_FF), 1.0),
        'w_ada': nrm(ks[8], (DEPTH, D_MODEL, N_MOD * D_MODEL), 0.5 * D_MODEL ** -0.5),
        'b_ada': nrm(ks[9], (DEPTH, N_MOD * D_MODEL), 0.02),
        'g_norm1': 1.0 + nrm(ks[10], (DEPTH, D_MODEL), 0.02),
        'w_in': nrm(ks[11], (DEPTH, D_MODEL, D_IN), D_MODEL ** -0.5),
        'w_sc_conv': nrm(ks[12], (DEPTH, SC_CONV_W, D_SC), SC_CONV_W ** -0.5),
        'w_ssd_conv': nrm(ks[13], (DEPTH, SSD_CONV_W, D_XBC), SSD_CONV_W ** -0.5),
        'b_ssd_conv': nrm(ks[14], (DEPTH, D_XBC), 0.02),
        'dt_bias': dt0 + jnp.log(-jnp.expm1(-dt0)),
        'a_log': jnp.log(jax.random.uniform(ks[15], (DEPTH, SSD_HEADS), f, 1.0, 16.0)),
        'd_skip': 1.0 + nrm(ks[16], (DEPTH, SSD_HEADS), 0.1),
        'g_ssd_norm': 1.0 + nrm(ks[17], (DEPTH, D_SSD), 0.02),
        'w_out': nrm(ks[18], (DEPTH, D_MIX, D_MODEL), D_MIX ** -0.5),
        'g_norm2': 1.0 + nrm(ks[19], (DEPTH, D_MODEL), 0.02),
        'w_up': nrm(ks[21], (DEPTH, D_MODEL, 2 * D_FF), D_MODEL ** -0.5),
        'w_ffn_conv': nrm(ks[22], (DEPTH, FFN_CONV_W, 2 * D_FF), FFN_CONV_W ** -0.5),
        'b_ffn_conv': nrm(ks[23], (DEPTH, 2 * D_FF), 0.02),
        'w_down': nrm(ks[24], (DEPTH, D_FF, D_MODEL), D_FF ** -0.5),
        'g_final': 1.0 + nrm(ks[25], (D_MODEL,), 0.02),
    }


def reference(x_prompt, x_sample, c_prompt, c_sample, state_sc_conv, state_ssd_conv,
              state_ssm, state_ffn_conv, w_ada, b_ada, g_norm1, w_in, w_sc_conv,
              w_ssd_conv, b_ssd_conv, dt_bias, a_log, d_skip, g_ssd_norm, w_out,
              g_norm2, w_up, w_ffn_conv, b_ffn_conv, w_down, g_final):
    bp = x_prompt.shape[0]
    dtp = x_prompt.dtype
    xp, xs = x_prompt, x_sample
    p_sc, p_ssd, p_ssm, p_ffn = [], [], [], []
    s_sc, s_ssd, s_ssm, s_ffn = [], [], [], []
    for l in range(DEPTH):
        params = (w_ada[l], b_ada[l], g_norm1[l], w_in[l], w_sc_conv[l], w_ssd_conv[l],
                  b_ssd_conv[l], dt_bias[l], a_log[l], d_skip[l], g_ssd_norm[l], w_out[l],
                  g_norm2[l], w_up[l], w_ffn_conv[l], b_ffn_conv[l], w_down[l])
        xp, a, b_, c_, d_ = layer(
            xp, c_prompt,
            jnp.zeros((bp, SC_CONV_W - 1, D_SC), dtp),
            jnp.zeros((bp, SSD_CONV_W - 1, D_XBC), dtp),
            jnp.zeros((bp, SSD_HEADS, SSD_HEADDIM, SSD_D_STATE), dtp),
            jnp.zeros((bp, FFN_CONV_W - 1, 2 * D_FF), dtp),
            *params)
        p_sc.append(a); p_ssd.append(b_); p_ssm.append(c_); p_ffn.append(d_)
        xs, a, b_, c_, d_ = layer(xs, c_sample, state_sc_conv[l], state_ssd_conv[l],
                                  state_ssm[l], state_ffn_conv[l], *params)
        s_sc.append(a); s_ssd.append(b_); s_ssm.append(c_); s_ffn.append(d_)
    y_prompt = rmsnorm(xp, g_final)
    y_sample = rmsnorm(xs, g_final)
    return (y_prompt, y_sample,
            jnp.stack(p_sc), jnp.stack(p_ssd), jnp.stack(p_ssm), jnp.stack(p_ffn),
            jnp.stack(s_sc), jnp.stack(s_ssd), jnp.stack(s_ssm), jnp.stack(s_ffn))
```

```python
import os
import contextlib
import numpy as np
import concourse.bass as bass
import concourse.mybir as mybir
from concourse.bass_utils import run_bass_kernel_spmd

F32 = mybir.dt.float32
BF16 = mybir.dt.bfloat16
AF = mybir.ActivationFunctionType
ALU = mybir.AluOpType

D = 4096
D_SC = 2048
D_SSD = 4096
NH = 64
HP = 64
NG = 8
NS = 128
D_XBC = 6144
D_IN = 16448
D_FF = 11008
EPS = 1e-6
NCORE = 8
NPRE = int(os.environ.get("K_NPRE", "7"))
NMAIN = int(os.environ.get("K_NMAIN", "9"))
DO_SAMPLE = int(os.environ.get("K_SAMPLE", "1"))
DEBUG = int(os.environ.get("K_DEBUG", "0"))

C_SC_B, C_SC_C, C_SC_X, C_Z, C_XBC, C_DT = 0, 2048, 4096, 6144, 10240, 16384


class Buf:
    __slots__ = ("w", "r", "name")

    def __init__(self, name=""):
        self.w = None
        self.r = {}
        self.name = name


class TT:
    def __init__(self, h, name, nb=0):
        self.h = h
        self.name = name
        self.all = Buf(name)
        self.subs = [Buf(f"{name}.{i}") for i in range(nb)]

    def B(self, i=None):
        if not self.subs:
            return [self.all]
        if i is None:
            return list(self.subs)
        if isinstance(i, (list, tuple, range)):
            return [self.subs[j] for j in i]
        return [self.subs[i]]


class Eng:
    def __init__(self, name, h, sem):
        self.name, self.h, self.sem = name, h, sem
        self.count = 0
        self.waited = {}


class Slot:
    def __init__(self, sem):
        self.sem = sem
        self.count = 0


class Ker:
    def __init__(self, nc, es):
        self.nc, self.es = nc, es
        self.eng = {}
        for n, h in (("pe", nc.tensor), ("act", nc.scalar), ("dve", nc.vector),
                     ("pool", nc.gpsimd), ("sp", nc.sync)):
            self.eng[n] = Eng(n, h, es.enter_context(nc.semaphore("s_" + n)))
        self.slots = {q: [Slot(es.enter_context(nc.semaphore(f"d_{q}{i}"))) for i in range(10)]
                      for q in ("sp", "pool")}
        self.slot_i = {"sp": 0, "pool": 0}
        self.nt = 0
        self.dbg_outs = []

    def sb(self, name, shape, dt, nb=0):
        h = self.es.enter_context(self.nc.sbuf_tensor("sb_" + name, list(shape), dt))
        return TT(h, name, nb)

    def ps(self, name):
        h = self.es.enter_context(self.nc.psum_tensor(name, [128, 512], F32))
        return TT(h, name)

    def _deps(self, E, R, W):
        deps = {}
        for b in R:
            if b.w is not None:
                s, v = b.w
                if deps.get(s, (None, 0))[1] < v:
                    deps[s] = (s, v)
        for b in W:
            if b.w is not None:
                s, v = b.w
                if deps.get(s, (None, 0))[1] < v:
                    deps[s] = (s, v)
            for s, v in b.r.items():
                if deps.get(s, (None, 0))[1] < v:
                    deps[s] = (s, v)
        for s, v in deps.values():
            if s is E.sem and E.name == "pe":
                continue
            if E.waited.get(s, 0) < v:
                E.h.wait_ge(s, v)
                E.waited[s] = v

    def op(self, e, fn, R=(), W=(), inc=True):
        E = self.eng[e]
        self._deps(E, R, W)
        ins = fn(E.h)
        if inc:
            E.count += 1
            ins.then_inc(E.sem, 1)
            val = E.count
        else:
            val = E.count + 1
        for b in R:
            if b.r.get(E.sem, 0) < val:
                b.r[E.sem] = val
        for b in W:
            b.w = (E.sem, val)
            b.r = {}
        return ins

    def dma(self, q, out, in_, R=(), W=()):
        E = self.eng[q]
        self._deps(E, R, W)
        sl = self.slots[q][self.slot_i[q] % len(self.slots[q])]
        self.slot_i[q] += 1
        if sl.count and E.waited.get(sl.sem, 0) < sl.count:
            E.h.wait_ge(sl.sem, sl.count)
            E.waited[sl.sem] = sl.count
        E.h.dma_start(out=out, in_=in_).then_inc(sl.sem, 16)
        sl.count += 16
        for b in R:
            b.r[sl.sem] = sl.count
        for b in W:
            b.w = (sl.sem, sl.count)
            b.r = {}

    def barrier(self):
        for E in self.eng.values():
            for Fe in self.eng.values():
                if Fe is E or Fe.count == 0:
                    continue
                if E.waited.get(Fe.sem, 0) < Fe.count:
                    E.h.wait_ge(Fe.sem, Fe.count)
                    E.waited[Fe.sem] = Fe.count
            for q in ("sp", "pool"):
                for sl in self.slots[q]:
                    if sl.count and E.waited.get(sl.sem, 0) < sl.count:
                        E.h.wait_ge(sl.sem, sl.count)
                        E.waited[sl.sem] = sl.count

    def finish(self):
        E = self.eng["sp"]
        for q in ("sp", "pool"):
            for sl in self.slots[q]:
                if sl.count and E.waited.get(sl.sem, 0) < sl.count:
                    E.h.wait_ge(sl.sem, sl.count)
                    E.waited[sl.sem] = sl.count

    def dbg(self, name, tt, ap, shape, dt=F32):
        if not DEBUG:
            return
        d = self.nc.dram_tensor("dbg_" + name, list(shape), dt, kind="ExternalOutput").ap()
        self.dma("sp", d, ap, R=tt.B())
        self.dbg_outs.append("dbg_" + name)


def build(nc):
    es = contextlib.ExitStack()
    with es:
        _build(nc, es)
    return nc


def _build(nc, es):
    k = Ker(nc, es)

    def din(name, shape):
        return nc.dram_tensor(name, list(shape), F32, kind="ExternalInput").ap()

    def dout(name, shape):
        return nc.dram_tensor(name, list(shape), F32, kind="ExternalOutput").ap()

    NTOK_M = NMAIN * 128
    xm = din("xm", [NTOK_M, D])
    xp = din("xp", [max(NPRE, 1) * 128, D])
    flag_d = din("flag", [128, 1])
    cvec = din("cvec", [17, D])
    xs_in = din("xs_in", [64, D])
    st_sc = din("st_sc", [128, 16, 32])
    st_ssd = din("st_ssd", [128, 48, 48])
    st_ffn = din("st_ffn", [128, 172, 32])
    ssm_in = din("ssm_in", [16, 4096, 128])
    w_ada = din("w_ada", [D, 6 * D])
    b_adaT = din("b_adaT", [128, 192])
    g1T_d = din("g1T", [128, 32])
    g2T_d = din("g2T", [128, 32])
    gsT_d = din("gsT", [128, 32])
    w_in = din("w_in", [D, D_IN])
    wsc_d = din("wsc", [128, 16, 3])
    wssd_d = din("wssd", [128, 48, 4])
    bssd_d = din("bssd", [128, 48])
    dtb_d = din("dtb", [1, 64])
    alog_d = din("alog", [1, 64])
    dsk_d = din("dsk", [1, 64])
    w_out = din("w_out", [D_SC + D_SSD, D])
    w_up = din("w_up", [D, 2 * D_FF])
    wffn_d = din("wffn", [128, 172, 3])
    bffn_d = din("bffn", [128, 172])
    w_down = din("w_down", [D_FF, D])
    gfin_d = din("gfin", [1, D])
    consts_d = din("consts", [128, 640])

    y_main = dout("y_main", [NTOK_M, D])
    y_samp = dout("y_samp", [64, D])
    o_sc_p = dout("o_sc_p", [128, 16, 2])
    o_ssd_p = dout("o_ssd_p", [128, 48, 3])
    o_ffn_p = dout("o_ffn_p", [128, 172, 2])
    o_ssm_p = dout("o_ssm_p", [4096, 128])
    o_sc_s = dout("o_sc_s", [128, 16, 32])
    o_ssd_s = dout("o_ssd_s", [128, 48, 48])
    o_ffn_s = dout("o_ffn_s", [128, 172, 32])
    o_ssm_s = dout("o_ssm_s", [16, 4096, 128])

    cst = k.sb("cst", [128, 640], F32)
    identb = k.sb("identb", [128, 128], BF16)
    flag = k.sb("flag", [128, 1], F32)
    modT = k.sb("modT", [128, 192, 17], F32)
    gm1 = k.sb("gm1", [128, 32, 17], F32)
    gm2 = k.sb("gm2", [128, 32, 17], F32)
    g1T = k.sb("g1T", [128, 32], F32)
    g2T = k.sb("g2T", [128, 32], F32)
    gsT = k.sb("gsT", [128, 32], F32)
    badaT = k.sb("badaT", [128, 192], F32)
    wsc = k.sb("wsc", [128, 16, 3], F32)
    wssd = k.sb("wssd", [128, 48, 4], F32)
    bssd = k.sb("bssd", [128, 48], F32)
    wffn = k.sb("wffn", [128, 172, 3], F32)
    bffn = k.sb("bffn", [128, 172], F32)
    dtb = k.sb("dtb", [128, 64], F32)
    Abc = k.sb("Abc", [128, 64], F32)
    Dbc = k.sb("Dbc", [128, 64], F32)
    cT = k.sb("cT", [128, 32, 17], BF16)

    NWB = 3
    wst = [k.sb(f"wst{i}", [128, 43, 128], BF16) for i in range(NWB)]
    xres = [k.sb(f"xres{i}", [128, D], F32) for i in range(1)]
    xnb = k.sb("xnb", [128, D], BF16)
    hT = k.sb("hT", [128, 32, 128], BF16, nb=32)
    arena = k.sb("arena", [128, 86, 128], BF16)
    actT = TT(arena.h[:, :, :], "actT", nb=86)
    ymT = TT(arena.h[:, 0:48, :], "ymT", nb=48)
    BT = TT(arena.h[:, 48:56, :], "BT", nb=8)
    CT = TT(arena.h[:, 56:64, :], "CT", nb=8)
    B_tok = TT(arena.h[:, 64:72, :].rearrange("p a b -> p (a b)"), "B_tok")
    xs_tok = k.sb("xs_tok", [128, D], BF16)
    sz_tok = k.sb("sz_tok", [128, D], BF16)
    S = k.sb("S", [128, D], F32, nb=0)
    sbg = [k.sb(f"sbg{i}", [128, 512], BF16) for i in range(2)]
    tl_sc = k.sb("tl_sc", [128, 16, 2], F32, nb=16)
    tl_ssd = k.sb("tl_ssd", [128, 48, 3], F32, nb=48)
    tl_ffn = k.sb("tl_ffn", [128, 172, 2], F32, nb=172)
    cvb = [k.sb(f"cvb{i}", [128, 128 + 48], F32) for i in range(3)]
    cva = [k.sb(f"cva{i}", [128, 128], F32) for i in range(3)]
    cvc = [k.sb(f"cvc{i}", [128, 128], F32) for i in range(2)]
    cvh = [k.sb(f"cvh{i}", [128, 128], BF16) for i in range(3)]
    mixg = [k.sb(f"mixg{i}", [128, 4, 128], F32) for i in range(2)]
    sm = {n: k.sb("sm_" + n, [128, 64], F32) for n in
          ("v", "av", "e", "l", "dt", "a", "ac", "dte", "w1", "eac", "dec", "tot")}
    epsb = k.sb("epsb", [128, 1], F32)
    k.op("dve", lambda e: e.memset(epsb.h[:], EPS), W=epsb.B())
    st1 = k.sb("st1", [128, 8], F32)
    st2 = k.sb("st2", [128, 8], F32)
    xdt = xnb
    xdte = k.sb("xdte", [128, D], BF16)
    xsD = xs_tok
    cbm = k.sb("cbm", [128, 8, 128], F32)
    sgw = [k.sb(f"sgw{i}", [128, 512], F32) for i in range(2)]
    MT = [k.sb(f"MT{i}", [128, 4, 128], BF16) for i in range(2)]
    gall = k.sb("gall", [128, D], F32)
    gnb = xs_tok
    gfin = gall
    yw = [k.sb(f"yw{i}", [128, 512], F32) for i in range(2)]

    PSP = [k.ps(f"psp{i}") for i in range(3)]
    PSW = [k.ps(f"psw{i}") for i in range(5)]
    ring = {"p": 0, "w": 0, "wst": 0, "cvb": 0, "cva": 0, "cvc": 0, "cvh": 0, "mixg": 0,
            "sgw": 0, "sgx": 0, "MT": 0, "yw": 0, "xres": 0, "sbg": 0}

    def nxt(lst, key):
        t = lst[ring[key] % len(lst)]
        ring[key] += 1
        return t

    ident = cst.h[:, 0:128]
    triU = cst.h[:, 128:256]
    ones = cst.h[:, 256:384]
    triS = cst.h[0:64, 384:448]
    onesS = cst.h[0:64, 448:512]
    rowmask = cst.h[0:64, 512:528]

    def ld(tt, src):
        k.dma("sp", tt.h[:], src, W=tt.B())

    ld(cst, consts_d)
    ld(flag, flag_d)
    ld(g1T, g1T_d); ld(g2T, g2T_d); ld(gsT, gsT_d); ld(badaT, b_adaT)
    ld(wsc, wsc_d); ld(wssd, wssd_d); ld(bssd, bssd_d); ld(wffn, wffn_d); ld(bffn, bffn_d)
    k.dma("sp", dtb.h[:], dtb_d.partition_broadcast(128), W=dtb.B())
    k.dma("sp", Abc.h[:], alog_d.partition_broadcast(128), W=Abc.B())
    k.dma("sp", Dbc.h[:], dsk_d.partition_broadcast(128), W=Dbc.B())
    k.op("dve", lambda e: e.tensor_copy(out=identb.h[:], in_=ident), R=cst.B(), W=identb.B())
    k.op("act", lambda e: e.activation(out=Abc.h[:], in_=Abc.h[:], func=AF.Exp), R=Abc.B(), W=Abc.B())
    k.op("dve", lambda e: e.tensor_scalar(out=Abc.h[:], in0=Abc.h[:], scalar1=-1.0, scalar2=None,
                                          op0=ALU.mult), R=Abc.B(), W=Abc.B())

    def proj(Wd, blocks, kcs, rhs_fn, rhs_bufs_fn, NT, consumer):
        kparts = [(0, kcs)] if kcs <= 43 else [(0, kcs // 2), (kcs // 2, kcs)]
        jobs = [(bi, kp) for bi in range(len(blocks)) for kp in kparts]
        loaded = {}

        def issue(j):
            bi, (k0, k1) = jobs[j]
            c0, ncol, _ = blocks[bi]
            t = nxt(wst, "wst")
            src = Wd[k0 * 128:k1 * 128, c0:c0 + ncol].rearrange("(c p) n -> p c n", p=128)
            k.dma("pool", t.h[:, 0:k1 - k0, 0:ncol], src, W=t.B())
            loaded[j] = t

        PF = NWB - 1
        for j in range(min(PF, len(jobs))):
            issue(j)
        cur = None
        for j, (bi, (k0, k1)) in enumerate(jobs):
            c0, ncol, tag = blocks[bi]
            t = loaded.pop(j)
            if k0 == 0:
                cur = nxt(PSP, "p")
            for kc in range(k0, k1):
                last = (kc == kcs - 1)
                k.op("pe", lambda e, kc=kc, t=t, last=last: e.matmul(
                    cur.h[0:ncol, 0:NT], lhsT=t.h[:, kc - k0, 0:ncol], rhs=rhs_fn(kc),
                    start=(kc == 0), stop=last),
                    R=t.B() + rhs_bufs_fn(kc), W=cur.B(), inc=last)
            if j + PF < len(jobs):
                issue(j + PF)
            if k1 == kcs:
                consumer(tag, cur, cur.h[0:ncol, 0:NT])

    cv = nxt(xres, "xres")
    k.dma("sp", cv.h[0:17, :], cvec, W=cv.B())
    k.op("act", lambda e: e.activation(out=xnb.h[0:17, :], in_=cv.h[0:17, :], func=AF.Silu),
         R=cv.B(), W=xnb.B())
    for j0 in range(0, 32, 8):
        pt = nxt(PSW, "w")
        pv = pt.h[:].bitcast(BF16)
        for j in range(8):
            k.op("pe", lambda e, j=j: e.transpose(out=pv[:, j * 18:j * 18 + 17],
                                                  in_=xnb.h[0:17, (j0 + j) * 128:(j0 + j + 1) * 128],
                                                  identity=identb.h[0:17, 0:17]),
                 R=xnb.B() + identb.B(), W=pt.B(), inc=(j == 7))
        k.op("dve", lambda e: e.tensor_copy(
            out=cT.h[:, j0:j0 + 8, :], in_=pv[:, 0:8 * 18].rearrange("p (j r) -> p j r", r=18)[:, :, 0:17]),
            R=pt.B(), W=cT.B())

    def ada_cons(tag, pt, pap):
        k.op("act", lambda e: e.activation(out=modT.h[:, tag, :], in_=pap, func=AF.Identity,
                                           bias=badaT.h[:, tag:tag + 1], scale=1.0),
             R=pt.B() + badaT.B(), W=modT.B())

    proj(w_ada, [(i * 128, 128, i) for i in range(192)], 32,
         lambda kc: cT.h[:, kc, :], lambda kc: cT.B(), 17, ada_cons)
    for (gm, gT, off) in ((gm1, g1T, 32), (gm2, g2T, 128)):
        k.op("dve", lambda e, gm=gm, off=off: e.tensor_scalar(
            out=gm.h[:], in0=modT.h[:, off:off + 32, :], scalar1=1.0, scalar2=None, op0=ALU.add),
            R=modT.B(), W=gm.B())
        k.op("dve", lambda e, gm=gm, gT=gT: e.tensor_tensor(
            out=gm.h[:], in0=gm.h[:], in1=gT.h[:].unsqueeze(2).to_broadcast([128, 32, 17]),
            op=ALU.mult), R=gm.B() + gT.B(), W=gm.B())
    SH1, GT1, SH2, GT2 = 0, 64, 96, 160

    class Cfg:
        pass

    def mk_cfg(sample):
        c = Cfg()
        c.sample = sample
        c.NT = 64 if sample else 128
        c.shift = 16 if sample else 1

        def mod_ap(tt, off, j0, n):
            if sample:
                return tt.h[:, off + j0:off + j0 + n, 1:17].unsqueeze(2).to_broadcast([128, n, 4, 16])
            return tt.h[:, off + j0:off + j0 + n, 0:1].to_broadcast([128, n, 128])

        def view(ap):
            if sample:
                return ap.rearrange("p j (t b) -> p j t b", b=16)
            return ap
        c.mod_ap, c.view = mod_ap, view
        return c

    def norm_to_hT(c, xr, gm, shoff):
        NT = c.NT
        k.op("dve", lambda e: e.memset(st1.h[:, 0:1], 0.0), W=st1.B())
        k.op("act", lambda e: e.activation(out=xnb.h[0:NT, :], in_=xr.h[0:NT, :], func=AF.Square,
                                           accum_out=st1.h[0:NT, 0:1]),
             R=xr.B() + st1.B(), W=xnb.B() + st1.B())
        k.op("act", lambda e: e.activation(out=st1.h[0:NT, 1:2], in_=st1.h[0:NT, 0:1], func=AF.Ln,
                                           scale=1.0 / D, bias=epsb.h[0:NT, 0:1]), R=st1.B() + epsb.B(), W=st1.B())
        k.op("act", lambda e: e.activation(out=st1.h[0:NT, 2:3], in_=st1.h[0:NT, 1:2], func=AF.Exp,
                                           scale=-0.5), R=st1.B(), W=st1.B())
        k.op("act", lambda e: e.activation(out=xnb.h[0:NT, :], in_=xr.h[0:NT, :], func=AF.Copy,
                                           scale=st1.h[0:NT, 2:3]),
             R=xr.B() + st1.B(), W=xnb.B())
        for j0 in range(0, 32, 4):
            pt = nxt(PSW, "w")
            pv = pt.h[:].bitcast(BF16)
            for j in range(4):
                k.op("pe", lambda e, j=j: e.transpose(
                    out=pv[:, j * NT:(j + 1) * NT], in_=xnb.h[0:NT, (j0 + j) * 128:(j0 + j + 1) * 128],
                    identity=identb.h[0:NT, 0:NT]), R=xnb.B() + identb.B(), W=pt.B(), inc=(j == 3))
            src = pv[:, 0:4 * NT].rearrange("p (j t) -> p j t", t=NT)
            dst = hT.h[:, j0:j0 + 4, 0:NT]
            k.op("dve", lambda e: e.tensor_tensor(out=c.view(dst), in0=c.view(src),
                                                  in1=c.mod_ap(gm, 0, j0, 4), op=ALU.mult),
                 R=pt.B() + gm.B(), W=hT.B(range(j0, j0 + 4)))
            k.op("dve", lambda e: e.tensor_tensor(out=c.view(dst), in0=c.view(dst),
                                                  in1=c.mod_ap(modT, shoff, j0, 4), op=ALU.add),
                 R=hT.B(range(j0, j0 + 4)) + modT.B(), W=hT.B(range(j0, j0 + 4)))

    TLS = {}

    def tail_load(c, cb, TL, tl, blk):
        if c.sample:
            k.dma("sp", cb.h[:, 0:TL], TLS[tl.name][0][:, blk, :], W=cb.B())
        else:
            k.op("dve", lambda e: e.tensor_copy(out=cb.h[:, 0:TL], in_=tl.h[:, blk, 0:TL]),
                 R=tl.B(blk), W=cb.B())

    def tail_store(c, cb, TL, tl, blk):
        NT = c.NT
        if c.sample:
            k.dma("sp", TLS[tl.name][1][:, blk, :], cb.h[:, NT:NT + TL], R=cb.B())
        else:
            k.op("dve", lambda e: e.tensor_copy(out=tl.h[:, blk, 0:TL], in_=cb.h[:, NT:NT + TL]),
                 R=cb.B(), W=tl.B(blk))

    def conv_block(c, pt, pap, tl, blk, wt, Kw, bias_ap):
        NT, sh = c.NT, c.shift
        TL = (Kw - 1) * sh
        cb = nxt(cvb, "cvb")
        ca = nxt(cva, "cva")
        k.op("act", lambda e: e.activation(out=cb.h[:, TL:TL + NT], in_=pap, func=AF.Copy),
             R=pt.B(), W=cb.B())
        tail_load(c, cb, TL, tl, blk)
        if bias_ap is not None:
            k.op("dve", lambda e: e.tensor_scalar(out=ca.h[:, 0:NT], in0=cb.h[:, TL:TL + NT],
                                                  scalar1=wt.h[:, blk, Kw - 1:Kw], scalar2=bias_ap,
                                                  op0=ALU.mult, op1=ALU.add),
                 R=cb.B() + wt.B(), W=ca.B())
        else:
            k.op("dve", lambda e: e.tensor_scalar(out=ca.h[:, 0:NT], in0=cb.h[:, TL:TL + NT],
                                                  scalar1=wt.h[:, blk, Kw - 1:Kw], scalar2=None,
                                                  op0=ALU.mult), R=cb.B() + wt.B(), W=ca.B())
        for kk in range(Kw - 1):
            o = kk * sh
            k.op("dve", lambda e, kk=kk, o=o: e.scalar_tensor_tensor(
                out=ca.h[:, 0:NT], in0=cb.h[:, o:o + NT], scalar=wt.h[:, blk, kk:kk + 1],
                in1=ca.h[:, 0:NT], op0=ALU.mult, op1=ALU.add), R=cb.B() + wt.B() + ca.B(), W=ca.B())
        tail_store(c, cb, TL, tl, blk)
        return ca

    tq = {"pt": None, "n": 0, "dst": None, "c0": 0}

    def tr_push(c, src_tt, src_ap, dst_tt, col0):
        NT = c.NT
        if tq["n"] == 0:
            tq["pt"] = nxt(PSW, "w")
            tq["dst"], tq["c0"] = dst_tt, col0
        pt = tq["pt"]
        pv = pt.h[:].bitcast(BF16)
        n = tq["n"]
        k.op("pe", lambda e: e.transpose(out=pv[0:NT, n * 128:(n + 1) * 128], in_=src_ap,
                                         identity=identb.h[:, :]),
             R=src_tt.B() + identb.B(), W=pt.B())
        tq["n"] += 1
        if tq["n"] == 4:
            tr_flush(c)

    def tr_flush(c):
        if tq["n"] == 0:
            return
        NT = c.NT
        pt, n, dst, c0 = tq["pt"], tq["n"], tq["dst"], tq["c0"]
        pv = pt.h[:].bitcast(BF16)
        k.op("act", lambda e: e.activation(out=dst.h[0:NT, c0:c0 + n * 128], in_=pv[0:NT, 0:n * 128],
                                           func=AF.Copy), R=pt.B(), W=dst.B())
        tq["n"] = 0

    def in_proj(c, prefix):
        NT = c.NT
        st = {}

        def cons(tag, pt, pap):
            kind, i = tag
            if kind == "c":
                t = nxt(cvc, "cvc")
                k.op("act", lambda e: e.activation(out=t.h[:, 0:NT], in_=pap, func=AF.Copy),
                     R=pt.B(), W=t.B())
                st["c"] = t
            elif kind == "x":
                t = st["c"]
                cb = nxt(cvb, "cvb")
                ca = nxt(cva, "cva")
                TL = 2 * c.shift
                k.op("dve", lambda e: e.tensor_tensor(out=cb.h[:, TL:TL + NT], in0=t.h[:, 0:NT],
                                                      in1=pap, op=ALU.mult), R=t.B() + pt.B(), W=cb.B())
                tail_load(c, cb, TL, tl_sc, i)
                k.op("dve", lambda e: e.tensor_scalar(out=ca.h[:, 0:NT], in0=cb.h[:, TL:TL + NT],
                                                      scalar1=wsc.h[:, i, 2:3], scalar2=None,
                                                      op0=ALU.mult), R=cb.B() + wsc.B(), W=ca.B())
                for kk in range(2):
                    o = kk * c.shift
                    k.op("dve", lambda e, kk=kk, o=o: e.scalar_tensor_tensor(
                        out=ca.h[:, 0:NT], in0=cb.h[:, o:o + NT], scalar=wsc.h[:, i, kk:kk + 1],
                        in1=ca.h[:, 0:NT], op0=ALU.mult, op1=ALU.add),
                        R=cb.B() + wsc.B() + ca.B(), W=ca.B())
                tail_store(c, cb, TL, tl_sc, i)
                st["uc"] = ca
            elif kind == "b":
                ca = st["uc"]
                k.op("dve", lambda e: e.tensor_tensor(out=ymT.h[:, i, 0:NT], in0=ca.h[:, 0:NT],
                                                      in1=pap, op=ALU.mult),
                     R=ca.B() + pt.B(), W=ymT.B(i))
            elif kind == "z":
                hb = nxt(cvh, "cvh")
                k.op("act", lambda e: e.activation(out=hb.h[:, 0:NT], in_=pap, func=AF.Silu),
                     R=pt.B(), W=hb.B())
                tr_push(c, hb, hb.h[:, 0:NT], sz_tok, i * 128)
            elif kind == "xbc":
                ca = conv_block(c, pt, pap, tl_ssd, i, wssd, 4, bssd.h[:, i:i + 1])
                if i < 32:
                    hb = nxt(cvh, "cvh")
                    k.op("act", lambda e: e.activation(out=hb.h[:, 0:NT], in_=ca.h[:, 0:NT],
                                                       func=AF.Silu), R=ca.B(), W=hb.B())
                    tr_push(c, hb, hb.h[:, 0:NT], xs_tok, i * 128)
                elif i < 40:
                    g = i - 32
                    k.op("act", lambda e: e.activation(out=BT.h[:, g, 0:NT], in_=ca.h[:, 0:NT],
                                                       func=AF.Silu), R=ca.B(), W=BT.B(g))
                    tr_push(c, BT, BT.h[:, g, 0:NT], B_tok, g * 128)
                else:
                    g = i - 40
                    k.op("act", lambda e: e.activation(out=CT.h[:, g, 0:NT], in_=ca.h[:, 0:NT],
                                                       func=AF.Silu), R=ca.B(), W=CT.B(g))

        blocks = []
        if not prefix:
            for i in range(16):
                blocks += [(C_SC_C + i * 128, 128, ("c", i)), (C_SC_X + i * 128, 128, ("x", i)),
                           (C_SC_B + i * 128, 128, ("b", i))]
            blocks += [(C_Z + i * 128, 128, ("z", i)) for i in range(32)]
        blocks += [(C_XBC + i * 128, 128, ("xbc", i)) for i in range(48)]
        proj(w_in, blocks, 32, lambda kc: hT.h[:, kc, 0:NT], lambda kc: hT.B(kc), NT, cons)
        tr_flush(c)
        t = nxt(wst, "wst")
        k.dma("pool", t.h[:, 0:32, 0:64], w_in[:, C_DT:C_DT + 64].rearrange("(c p) n -> p c n", p=128),
              W=t.B())
        pt = nxt(PSW, "w")
        for kc in range(32):
            k.op("pe", lambda e, kc=kc: e.matmul(pt.h[0:NT, 0:64], lhsT=hT.h[:, kc, 0:NT],
                                                 rhs=t.h[:, kc, 0:64], start=(kc == 0), stop=(kc == 31)),
                 R=t.B() + hT.B(kc), W=pt.B(), inc=(kc == 31))
        v, av, ee, ll, dt_, a_ = (sm[n] for n in ("v", "av", "e", "l", "dt", "a"))
        k.op("dve", lambda e: e.tensor_tensor(out=v.h[0:NT, :], in0=pt.h[0:NT, 0:64], in1=dtb.h[0:NT, :],
                                              op=ALU.add), R=pt.B() + dtb.B(), W=v.B())
        k.op("act", lambda e: e.activation(out=av.h[0:NT, :], in_=v.h[0:NT, :], func=AF.Abs),
             R=v.B(), W=av.B())
        k.op("act", lambda e: e.activation(out=ee.h[0:NT, :], in_=av.h[0:NT, :], func=AF.Exp, scale=-1.0),
             R=av.B(), W=ee.B())
        k.op("act", lambda e: e.activation(out=ll.h[0:NT, :], in_=ee.h[0:NT, :], func=AF.Ln, bias=1.0),
             R=ee.B(), W=ll.B())
        k.op("dve", lambda e: e.scalar_tensor_tensor(out=dt_.h[0:NT, :], in0=v.h[0:NT, :], scalar=0.0,
                                                     in1=ll.h[0:NT, :], op0=ALU.max, op1=ALU.add),
             R=v.B() + ll.B(), W=dt_.B())
        k.op("dve", lambda e: e.tensor_tensor(out=a_.h[0:NT, :], in0=dt_.h[0:NT, :], in1=Abc.h[0:NT, :],
                                              op=ALU.mult), R=dt_.B() + Abc.B(), W=a_.B())

    def hb3(ap64, NT):
        return ap64.unsqueeze(2).to_broadcast([NT, 64, 64])

    def v3(ap, NT):
        return ap.rearrange("t (h p) -> t h p", p=64)

    def ssd_decays(c):
        NT = c.NT
        tri_, ones_ = (triS, onesS) if c.sample else (triU, ones)
        a_, ac, dte, w1, eac, dec, dt_ = (sm[n] for n in ("a", "ac", "dte", "w1", "eac", "dec", "dt"))
        p1 = nxt(PSW, "w")
        k.op("pe", lambda e: e.matmul(p1.h[0:NT, 0:64], lhsT=tri_, rhs=a_.h[0:NT, :], start=True, stop=True),
             R=cst.B() + a_.B(), W=p1.B())
        k.op("pe", lambda e: e.matmul(p1.h[0:NT, 64:128], lhsT=ones_, rhs=a_.h[0:NT, :], start=True, stop=True),
             R=cst.B() + a_.B(), W=p1.B())
        k.op("act", lambda e: e.activation(out=ac.h[0:NT, :], in_=p1.h[0:NT, 0:64], func=AF.Copy),
             R=p1.B(), W=ac.B())
        k.op("act", lambda e: e.activation(out=eac.h[0:NT, :], in_=p1.h[0:NT, 0:64], func=AF.Exp),
             R=p1.B(), W=eac.B())
        k.op("act", lambda e: e.activation(out=dec.h[0:NT, :], in_=p1.h[0:NT, 64:128], func=AF.Exp),
             R=p1.B(), W=dec.B())
        k.op("dve", lambda e: e.tensor_tensor(out=dte.h[0:NT, :], in0=p1.h[0:NT, 64:128], in1=ac.h[0:NT, :],
                                              op=ALU.subtract), R=p1.B() + ac.B(), W=dte.B())
        k.op("act", lambda e: e.activation(out=dte.h[0:NT, :], in_=dte.h[0:NT, :], func=AF.Exp),
             R=dte.B(), W=dte.B())
        k.op("dve", lambda e: e.tensor_tensor(out=w1.h[0:NT, :], in0=dte.h[0:NT, :], in1=dt_.h[0:NT, :],
                                              op=ALU.mult), R=dte.B() + dt_.B(), W=w1.B())
        k.op("dve", lambda e: e.tensor_tensor(out=v3(xdte.h[0:NT, :], NT), in0=v3(xs_tok.h[0:NT, :], NT),
                                              in1=hb3(w1.h[0:NT, :], NT), op=ALU.mult),
             R=xs_tok.B() + w1.B(), W=xdte.B())

    def ssd_intra_group_prep(c):
        NT = c.NT
        tri_ = triS if c.sample else triU
        dt_ = sm["dt"]
        k.op("dve", lambda e: e.tensor_tensor(out=v3(xdt.h[0:NT, :], NT), in0=v3(xs_tok.h[0:NT, :], NT),
                                              in1=hb3(dt_.h[0:NT, :], NT), op=ALU.mult),
             R=xs_tok.B() + dt_.B(), W=xdt.B())
        k.op("dve", lambda e: e.tensor_tensor(out=v3(xsD.h[0:NT, :], NT), in0=v3(xs_tok.h[0:NT, :], NT),
                                              in1=hb3(Dbc.h[0:NT, :], NT), op=ALU.mult),
             R=xs_tok.B() + Dbc.B(), W=xsD.B())
        for g0 in (0, 4):
            pc = nxt(PSW, "w")
            for g in range(g0, g0 + 4):
                k.op("pe", lambda e, g=g: e.matmul(pc.h[0:NT, (g - g0) * NT:(g - g0 + 1) * NT],
                                                   lhsT=BT.h[:, g, 0:NT], rhs=CT.h[:, g, 0:NT],
                                                   start=True, stop=True),
                     R=BT.B(g) + CT.B(g), W=pc.B(), inc=(g == g0 + 3))
            k.op("dve", lambda e: e.tensor_tensor(
                out=cbm.h[0:NT, g0:g0 + 4, 0:NT],
                in0=pc.h[0:NT, 0:4 * NT].rearrange("p (g t) -> p g t", t=NT),
                in1=tri_.unsqueeze(1).to_broadcast([NT, 4, NT]), op=ALU.mult),
                R=pc.B() + cst.B(), W=cbm.B())

    def ssd_intra_group(c, g, pd):
        NT = c.NT
        tri_ = triS if c.sample else triU
        a_, ac = sm["a"], sm["ac"]
        k.op("pe", lambda e: e.matmul(pd.h[0:NT, 0:512], lhsT=identb.h[0:NT, 0:NT],
                                      rhs=xsD.h[0:NT, g * 512:(g + 1) * 512], start=True, stop=False),
             R=identb.B() + xsD.B(), W=pd.B(), inc=False)
        for hh in (0, 4):
            h0 = g * 8 + hh
            pa = nxt(PSW, "w")
            for q in range(4):
                k.op("pe", lambda e, q=q: e.matmul(
                    pa.h[0:NT, q * NT:(q + 1) * NT],
                    lhsT=a_.h[0:NT, h0 + q:h0 + q + 1].to_broadcast([NT, NT]), rhs=tri_,
                    start=True, stop=True), R=a_.B() + cst.B(), W=pa.B(), inc=(q == 3))
            w = nxt(sgw, "sgw")
            wv = w.h[0:NT, 0:4 * NT].rearrange("p (q t) -> p q t", t=NT)
            k.op("dve", lambda e: e.tensor_tensor(
                out=wv, in0=pa.h[0:NT, 0:4 * NT].rearrange("p (q t) -> p q t", t=NT),
                in1=ac.h[0:NT, h0:h0 + 4].unsqueeze(2).to_broadcast([NT, 4, NT]), op=ALU.subtract),
                R=pa.B() + ac.B(), W=w.B())
            k.op("dve", lambda e: e.tensor_scalar(out=w.h[0:NT, 0:4 * NT], in0=w.h[0:NT, 0:4 * NT],
                                                  scalar1=0.0, scalar2=None, op0=ALU.min),
                 R=w.B(), W=w.B())
            k.op("act", lambda e: e.activation(out=w.h[0:NT, 0:4 * NT], in_=w.h[0:NT, 0:4 * NT],
                                               func=AF.Exp), R=w.B(), W=w.B())
            m = nxt(MT, "MT")
            k.op("dve", lambda e: e.tensor_tensor(
                out=m.h[0:NT, :, 0:NT], in0=wv,
                in1=cbm.h[0:NT, g:g + 1, 0:NT].to_broadcast([NT, 4, NT]), op=ALU.mult),
                R=w.B() + cbm.B(), W=m.B())
            for q in range(4):
                hq = hh + q
                last = (hq == 7)
                k.op("pe", lambda e, q=q, hq=hq, last=last: e.matmul(
                    pd.h[0:NT, hq * 64:(hq + 1) * 64], lhsT=m.h[0:NT, q, 0:NT],
                    rhs=xdt.h[0:NT, (h0 + q) * 64:(h0 + q + 1) * 64], start=False, stop=last),
                    R=m.B() + xdt.B(), W=pd.B(), inc=last)

    def gate_accum(c, g, ysrc_tt, ysrc_ap):
        NT = c.NT
        k.op("dve", lambda e: e.tensor_tensor(out=gall.h[0:NT, g * 512:(g + 1) * 512], in0=ysrc_ap,
                                              in1=sz_tok.h[0:NT, g * 512:(g + 1) * 512], op=ALU.mult),
             R=ysrc_tt.B() + sz_tok.B(), W=gall.B())

    def gate_finish(c):
        NT = c.NT
        k.op("dve", lambda e: e.memset(st2.h[:, 0:1], 0.0), W=st2.B())
        k.op("act", lambda e: e.activation(out=gnb.h[0:NT, :], in_=gall.h[0:NT, :], func=AF.Square,
                                           accum_out=st2.h[0:NT, 0:1]),
             R=gall.B() + st2.B(), W=gnb.B() + st2.B())
        k.op("act", lambda e: e.activation(out=st2.h[0:NT, 1:2], in_=st2.h[0:NT, 0:1], func=AF.Ln,
                                           scale=1.0 / D_SSD, bias=epsb.h[0:NT, 0:1]), R=st2.B() + epsb.B(), W=st2.B())
        k.op("act", lambda e: e.activation(out=st2.h[0:NT, 2:3], in_=st2.h[0:NT, 1:2], func=AF.Exp,
                                           scale=-0.5), R=st2.B(), W=st2.B())
        k.op("act", lambda e: e.activation(out=gnb.h[0:NT, :], in_=gall.h[0:NT, :], func=AF.Copy,
                                           scale=st2.h[0:NT, 2:3]), R=gall.B() + st2.B(), W=gnb.B())
        for j0 in range(0, 32, 4):
            pt = nxt(PSW, "w")
            pv = pt.h[:].bitcast(BF16)
            for j in range(4):
                k.op("pe", lambda e, j=j: e.transpose(
                    out=pv[:, j * NT:(j + 1) * NT], in_=gnb.h[0:NT, (j0 + j) * 128:(j0 + j + 1) * 128],
                    identity=identb.h[0:NT, 0:NT]), R=gnb.B() + identb.B(), W=pt.B(), inc=(j == 3))
            k.op("dve", lambda e: e.tensor_tensor(
                out=ymT.h[:, 16 + j0:16 + j0 + 4, 0:NT],
                in0=pv[:, 0:4 * NT].rearrange("p (j t) -> p j t", t=NT),
                in1=gsT.h[:, j0:j0 + 4].unsqueeze(2).to_broadcast([128, 4, NT]), op=ALU.mult),
                R=pt.B() + gsT.B(), W=ymT.B(range(16 + j0, 16 + j0 + 4)))

    def ssd_prompt(c, prefix):
        NT = 128
        ssd_decays(c)
        dec, eac = sm["dec"], sm["eac"]
        if not prefix:
            ssd_intra_group_prep(c)
        for g in range(8):
            gs = slice(g * 512, (g + 1) * 512)
            if not prefix:
                pd = nxt(PSW, "w")
                ssd_intra_group(c, g, pd)
                po = nxt(PSW, "w")
                sb_ = nxt(sbg, "sbg")
                k.op("act", lambda e: e.activation(out=sb_.h[:, :], in_=S.h[:, gs], func=AF.Copy),
                     R=S.B(), W=sb_.B())
                k.op("pe", lambda e: e.matmul(po.h[:, 0:512], lhsT=CT.h[:, g, :], rhs=sb_.h[:, :],
                                              start=True, stop=True), R=CT.B(g) + sb_.B(), W=po.B())
                y1 = nxt(yw, "yw")
                k.op("dve", lambda e: e.tensor_tensor(
                    out=y1.h[:, :].rearrange("t (h p) -> t h p", p=64),
                    in0=po.h[:, 0:512].rearrange("t (h p) -> t h p", p=64),
                    in1=eac.h[:, g * 8:(g + 1) * 8].unsqueeze(2).to_broadcast([128, 8, 64]), op=ALU.mult),
                    R=po.B() + eac.B(), W=y1.B())
                k.op("dve", lambda e: e.tensor_tensor(out=y1.h[:, :], in0=y1.h[:, :], in1=pd.h[:, 0:512],
                                                      op=ALU.add), R=y1.B() + pd.B(), W=y1.B())
                gate_accum(c, g, y1, y1.h[:, :])
            pst = nxt(PSW, "w")
            k.op("pe", lambda e: e.matmul(pst.h[:, 0:512], lhsT=B_tok.h[:, g * 128:(g + 1) * 128],
                                          rhs=xdte.h[:, gs], start=True, stop=True),
                 R=B_tok.B() + xdte.B(), W=pst.B())
            k.op("dve", lambda e: e.tensor_tensor(
                out=S.h[:, gs].rearrange("n (h p) -> n h p", p=64),
                in0=S.h[:, gs].rearrange("n (h p) -> n h p", p=64),
                in1=dec.h[:, g * 8:(g + 1) * 8].unsqueeze(2).to_broadcast([128, 8, 64]), op=ALU.mult),
                R=S.B() + dec.B(), W=S.B())
            k.op("dve", lambda e: e.tensor_tensor(out=S.h[:, gs], in0=S.h[:, gs], in1=pst.h[:, 0:512],
                                                  op=ALU.add), R=S.B() + pst.B(), W=S.B())
        if not prefix:
            gate_finish(c)

    def ssd_sample(c):
        NT = 64
        ssd_decays(c)
        ssd_intra_group_prep(c)
        eac, a_ = sm["eac"], sm["a"]
        dP = k.sb("dP", [128, 32, 16], F32)
        aexp = gall
        k.op("dve", lambda e: e.tensor_copy(out=v3(aexp.h[0:64, :], 64), in_=hb3(a_.h[0:64, :], 64)),
             R=a_.B(), W=aexp.B())
        for hp0 in range(0, 32, 8):
            pp = nxt(PSW, "w")
            for q in range(8):
                hp = hp0 + q
                k.op("pe", lambda e, q=q, hp=hp: e.matmul(pp.h[:, q * 16:(q + 1) * 16],
                                                          lhsT=aexp.h[0:64, hp * 128:(hp + 1) * 128],
                                                          rhs=onesS[:, 0:16], start=True, stop=True),
                     R=aexp.B() + cst.B(), W=pp.B(), inc=(q == 7))
            k.op("act", lambda e: e.activation(out=dP.h[:, hp0:hp0 + 8, :],
                                               in_=pp.h[:, 0:128].rearrange("p (q b) -> p q b", b=16),
                                               func=AF.Exp), R=pp.B(), W=dP.B())
        for g in range(8):
            pd = nxt(PSW, "w")
            ssd_intra_group(c, g, pd)
            k.op("act", lambda e, g=g, pd=pd: e.activation(out=gall.h[0:NT, g * 512:(g + 1) * 512],
                                                           in_=pd.h[0:NT, 0:512], func=AF.Copy),
                 R=pd.B(), W=gall.B())
        s0 = TT(S.h[:, :].rearrange("p (a b) -> p a b", b=128), "s0")
        s0.all = S.all
        sn = s0
        s0t = TT(hT.h[:, :, :].rearrange("p a b -> p (a b)"), "s0t")
        s0t.B = lambda i=None: hT.B()
        xmb = TT(xnb.h[0:64, :], "xmb")
        xmb.all = xnb.all
        for b in range(16):
            k.dma("sp", s0.h[:], ssm_in[b].rearrange("(hp q) n -> q hp n", q=128), W=s0.B())
            for hp0 in range(0, 32, 4):
                pt = nxt(PSW, "w")
                for q in range(4):
                    k.op("pe", lambda e, q=q: e.transpose(out=pt.h[:, q * 128:(q + 1) * 128],
                                                          in_=s0.h[:, hp0 + q, :], identity=ident),
                         R=s0.B() + cst.B(), W=pt.B(), inc=(q == 3))
                k.op("act", lambda e: e.activation(out=s0t.h[:, hp0 * 128:(hp0 + 4) * 128],
                                                   in_=pt.h[:, 0:512], func=AF.Copy), R=pt.B(), W=s0t.B())
            for g in range(8):
                po = nxt(PSW, "w")
                k.op("pe", lambda e, g=g: e.matmul(po.h[0:64, 0:512], lhsT=CT.h[:, g, 0:64],
                                                   rhs=s0t.h[:, g * 512:(g + 1) * 512], start=True, stop=True),
                     R=CT.B(g) + s0t.B(), W=po.B())
                y1 = nxt(yw, "yw")
                k.op("dve", lambda e, g=g, y1=y1, po=po: e.tensor_tensor(
                    out=y1.h[0:64, :].rearrange("t (h p) -> t h p", p=64),
                    in0=po.h[0:64, 0:512].rearrange("t (h p) -> t h p", p=64),
                    in1=eac.h[0:64, g * 8:(g + 1) * 8].unsqueeze(2).to_broadcast([64, 8, 64]), op=ALU.mult),
                    R=po.B() + eac.B(), W=y1.B())
                k.op("dve", lambda e, g=g, y1=y1: e.scalar_tensor_tensor(
                    out=gall.h[0:64, g * 512:(g + 1) * 512], in0=y1.h[0:64, :], scalar=rowmask[:, b:b + 1],
                    in1=gall.h[0:64, g * 512:(g + 1) * 512], op0=ALU.mult, op1=ALU.add),
                    R=y1.B() + cst.B() + gall.B(), W=gall.B())
            k.op("dve", lambda e: e.tensor_scalar(out=xmb.h[:, :], in0=xdte.h[0:64, :],
                                                  scalar1=rowmask[:, b:b + 1], scalar2=None, op0=ALU.mult),
                 R=xdte.B() + cst.B(), W=xmb.B())
            for hp0 in range(0, 32, 4):
                g = hp0 // 4
                pn = nxt(PSW, "w")
                for q in range(4):
                    hp = hp0 + q
                    k.op("pe", lambda e, q=q, hp=hp: e.matmul(pn.h[:, q * 128:(q + 1) * 128],
                                                              lhsT=xmb.h[:, hp * 128:(hp + 1) * 128],
                                                              rhs=B_tok.h[0:64, g * 128:(g + 1) * 128],
                                                              start=True, stop=True),
                         R=xmb.B() + B_tok.B(), W=pn.B(), inc=(q == 3))
                for q in range(4):
                    hp = hp0 + q
                    k.op("dve", lambda e, q=q, hp=hp: e.scalar_tensor_tensor(
                        out=sn.h[:, hp, :], in0=s0.h[:, hp, :], scalar=dP.h[:, hp, b:b + 1],
                        in1=pn.h[:, q * 128:(q + 1) * 128], op0=ALU.mult, op1=ALU.add),
                        R=s0.B() + dP.B() + pn.B(), W=sn.B())
            k.dma("sp", o_ssm_s[b].rearrange("(hp q) n -> q hp n", q=128), sn.h[:], R=sn.B())
        for g in range(8):
            gate_accum(c, g, gall, gall.h[0:64, g * 512:(g + 1) * 512])
        gate_finish(c)

    def proj_residual(c, Wd, kcs, rhs_tt, goff, xr):
        NT = c.NT
        stt = {"mg": None}

        def cons(tag, pt, pap):
            j = tag
            q = j % 4
            if q == 0:
                stt["mg"] = nxt(mixg, "mixg")
            mg = stt["mg"]
            dst = mg.h[:, q:q + 1, 0:NT]
            k.op("dve", lambda e: e.tensor_tensor(
                out=c.view(dst), in0=c.view(pap.unsqueeze(1)), in1=c.mod_ap(modT, goff, j, 1), op=ALU.mult),
                R=pt.B() + modT.B(), W=mg.B())
            if q == 3:
                j0 = j - 3
                p2 = nxt(PSW, "w")
                for qq in range(4):
                    k.op("pe", lambda e, qq=qq: e.transpose(out=p2.h[0:NT, qq * 128:(qq + 1) * 128],
                                                            in_=mg.h[:, qq, 0:NT], identity=ident),
                         R=mg.B() + cst.B(), W=p2.B(), inc=(qq == 3))
                k.op("dve", lambda e: e.tensor_tensor(out=xr.h[0:NT, j0 * 128:(j0 + 4) * 128],
                                                      in0=xr.h[0:NT, j0 * 128:(j0 + 4) * 128],
                                                      in1=p2.h[0:NT, 0:512], op=ALU.add),
                     R=xr.B() + p2.B(), W=xr.B())

        proj(Wd, [(j * 128, 128, j) for j in range(32)], kcs,
             lambda kc: rhs_tt.h[:, kc, 0:NT], lambda kc: rhs_tt.B(kc), NT, cons)

    def ffn_up(c):
        NT = c.NT
        stt = {}

        def cons(tag, pt, pap):
            kind, i = tag
            blk = i if kind == "g" else 86 + i
            ca = conv_block(c, pt, pap, tl_ffn, blk, wffn, 3, bffn.h[:, blk:blk + 1])
            if kind == "g":
                t = nxt(cvc, "cvc")
                k.op("act", lambda e: e.activation(out=t.h[:, 0:NT], in_=ca.h[:, 0:NT], func=AF.Silu),
                     R=ca.B(), W=t.B())
                stt["g"] = t
            else:
                t = stt["g"]
                k.op("dve", lambda e: e.tensor_tensor(out=actT.h[:, i, 0:NT], in0=t.h[:, 0:NT],
                                                      in1=ca.h[:, 0:NT], op=ALU.mult),
                     R=t.B() + ca.B(), W=actT.B(i))

        blocks = []
        for i in range(86):
            blocks += [(i * 128, 128, ("g", i)), (D_FF + i * 128, 128, ("v", i))]
        proj(w_up, blocks, 32, lambda kc: hT.h[:, kc, 0:NT], lambda kc: hT.B(kc), NT, cons)

    def final_out(c, xr, dst):
        NT = c.NT
        k.op("dve", lambda e: e.memset(st1.h[:, 4:5], 0.0), W=st1.B())
        k.op("act", lambda e: e.activation(out=xnb.h[0:NT, :], in_=xr.h[0:NT, :], func=AF.Square,
                                           accum_out=st1.h[0:NT, 4:5]),
             R=xr.B() + st1.B(), W=xnb.B() + st1.B())
        k.op("act", lambda e: e.activation(out=st1.h[0:NT, 5:6], in_=st1.h[0:NT, 4:5], func=AF.Ln,
                                           scale=1.0 / D, bias=epsb.h[0:NT, 0:1]), R=st1.B() + epsb.B(), W=st1.B())
        k.op("act", lambda e: e.activation(out=st1.h[0:NT, 6:7], in_=st1.h[0:NT, 5:6], func=AF.Exp,
                                           scale=-0.5), R=st1.B(), W=st1.B())
        k.dma("sp", gfin.h[:], gfin_d.partition_broadcast(128), W=gfin.B())
        k.op("dve", lambda e: e.scalar_tensor_tensor(out=xr.h[0:NT, :], in0=xr.h[0:NT, :],
                                                     scalar=st1.h[0:NT, 6:7], in1=gfin.h[0:NT, :],
                                                     op0=ALU.mult, op1=ALU.mult),
             R=xr.B() + st1.B() + gfin.B(), W=xr.B())
        k.dma("sp", dst, xr.h[0:NT, :], R=xr.B())

    for tl in (tl_sc, tl_ssd, tl_ffn):
        k.op("dve", lambda e, tl=tl: e.memset(tl.h[:], 0.0), W=tl.B())
    k.op("dve", lambda e: e.memset(S.h[:], 0.0), W=S.B())

    cP = mk_cfg(False)
    cS = mk_cfg(True)

    def full_tile(c, src, dst, ssd_fn):
        xr = nxt(xres, "xres")
        k.dma("sp", xr.h[0:c.NT, :], src, W=xr.B())
        norm_to_hT(c, xr, gm1, SH1)
        in_proj(c, False)
        ssd_fn()
        proj_residual(c, w_out, 48, ymT, GT1, xr)
        norm_to_hT(c, xr, gm2, SH2)
        k.barrier()
        ffn_up(c)
        proj_residual(c, w_down, 86, actT, GT2, xr)
        k.barrier()
        final_out(c, xr, dst)

    for t in range(NPRE):
        xr = nxt(xres, "xres")
        k.dma("sp", xr.h[:, :], xp[t * 128:(t + 1) * 128, :], W=xr.B())
        norm_to_hT(cP, xr, gm1, SH1)
        in_proj(cP, True)
        ssd_prompt(cP, True)

    for t in range(NMAIN):
        full_tile(cP, xm[t * 128:(t + 1) * 128, :], y_main[t * 128:(t + 1) * 128, :],
                  lambda: ssd_prompt(cP, False))
        if t == 0:
            k.op("dve", lambda e: e.tensor_scalar(out=S.h[:], in0=S.h[:], scalar1=flag.h[:, 0:1],
                                                  scalar2=None, op0=ALU.mult), R=S.B() + flag.B(), W=S.B())
            for tl in (tl_sc, tl_ssd, tl_ffn):
                k.op("dve", lambda e, tl=tl: e.tensor_scalar(out=tl.h[:], in0=tl.h[:], scalar1=flag.h[:, 0:1],
                                                             scalar2=None, op0=ALU.mult),
                     R=tl.B() + flag.B(), W=tl.B())
    k.dma("sp", o_sc_p, tl_sc.h[:, :, 0:2], R=tl_sc.B())
    k.dma("sp", o_ssd_p, tl_ssd.h[:, :, 0:3], R=tl_ssd.B())
    k.dma("sp", o_ffn_p, tl_ffn.h[:, :, 0:2], R=tl_ffn.B())
    for hp0 in range(0, 32, 4):
        pt = nxt(PSW, "w")
        for q in range(4):
            k.op("pe", lambda e, q=q: e.transpose(out=pt.h[:, q * 128:(q + 1) * 128],
                                                  in_=S.h[:, (hp0 + q) * 128:(hp0 + q + 1) * 128], identity=ident),
                 R=S.B() + cst.B(), W=pt.B(), inc=(q == 3))
        k.op("act", lambda e: e.activation(out=gall.h[:, hp0 * 128:(hp0 + 4) * 128], in_=pt.h[:, 0:512],
                                           func=AF.Copy), R=pt.B(), W=gall.B())
    k.dma("sp", o_ssm_p.rearrange("(hp q) n -> q hp n", q=128),
          gall.h[:, :].rearrange("q (hp n) -> q hp n", n=128), R=gall.B())

    if DO_SAMPLE:
        TLS["tl_sc"] = (st_sc, o_sc_s)
        TLS["tl_ssd"] = (st_ssd, o_ssd_s)
        TLS["tl_ffn"] = (st_ffn, o_ffn_s)
        full_tile(cS, xs_in, y_samp, lambda: ssd_sample(cS))
    k.finish()


def _consts():
    c = np.zeros((128, 640), np.float32)
    c[:, 0:128] = np.eye(128, dtype=np.float32)
    c[:, 128:256] = np.triu(np.ones((128, 128), np.float32))
    c[:, 256:384] = 1.0
    idx = np.arange(64)
    tt, bb = idx // 16, idx % 16
    same = (bb[:, None] == bb[None, :])
    c[0:64, 384:448] = (same & (tt[:, None] <= tt[None, :])).astype(np.float32)
    c[0:64, 448:512] = same.astype(np.float32)
    c[0:64, 512:528] = (bb[:, None] == np.arange(16)[None, :]).astype(np.float32)
    return c


def _fm(v, nblk):
    return np.ascontiguousarray(v.reshape(nblk, 128).T)


def _fmk(w, nblk):
    K = w.shape[0]
    return np.ascontiguousarray(w.T.reshape(nblk, 128, K).transpose(1, 0, 2))


def _st_in(st, nblk):
    R = st.shape[1]
    a = st.transpose(2, 1, 0).reshape(nblk, 128, R * 16)
    return np.ascontiguousarray(a.transpose(1, 0, 2))


def _st_out(a, nblk, R):
    return a.transpose(1, 0, 2).reshape(nblk * 128, R, 16).transpose(2, 1, 0)


def kernel(x_prompt, x_sample, c_prompt, c_sample, state_sc_conv, state_ssd_conv,
           state_ssm, state_ffn_conv, w_ada, b_ada, g_norm1, w_in, w_sc_conv,
           w_ssd_conv, b_ssd_conv, dt_bias, a_log, d_skip, g_ssd_norm, w_out,
           g_norm2, w_up, w_ffn_conv, b_ffn_conv, w_down, g_final):
    f = lambda a: np.ascontiguousarray(np.asarray(a, dtype=np.float32))
    x_prompt, x_sample, c_prompt, c_sample = f(x_prompt), f(x_sample), f(c_prompt), f(c_sample)
    nc = bass.Bass("TRN2", target_bir_lowering=False)
    build(nc)
    shared = {
        "w_ada": f(w_ada[0]), "b_adaT": _fm(f(b_ada[0]), 192), "g1T": _fm(f(g_norm1[0]), 32),
        "g2T": _fm(f(g_norm2[0]), 32), "gsT": _fm(f(g_ssd_norm[0]), 32), "w_in": f(w_in[0]),
        "wsc": _fmk(f(w_sc_conv[0]), 16), "wssd": _fmk(f(w_ssd_conv[0]), 48),
        "bssd": _fm(f(b_ssd_conv[0]), 48), "dtb": f(dt_bias[0]).reshape(1, 64),
        "alog": f(a_log[0]).reshape(1, 64), "dsk": f(d_skip[0]).reshape(1, 64),
        "w_out": f(w_out[0]), "w_up": f(w_up[0]), "wffn": _fmk(f(w_ffn_conv[0]), 172),
        "bffn": _fm(f(b_ffn_conv[0]), 172), "w_down": f(w_down[0]),
        "gfin": f(g_final).reshape(1, D), "consts": _consts(),
    }
    in_maps = []
    npre = max(NPRE, 1) * 128
    for c in range(NCORE):
        s, half = c // 2, c % 2
        m = dict(shared)
        xm = np.zeros((NMAIN * 128, D), np.float32)
        xp = np.zeros((npre, D), np.float32)
        if half == 0:
            xm[128:] = x_prompt[s, 0:(NMAIN - 1) * 128]
        else:
            xm[:] = x_prompt[s, 896:896 + NMAIN * 128]
            xp[:NPRE * 128] = x_prompt[s, 0:NPRE * 128]
        m["xm"], m["xp"] = xm, xp
        m["flag"] = np.full((128, 1), float(half), np.float32)
        bs = slice(16 * c, 16 * c + 16)
        m["cvec"] = np.ascontiguousarray(np.concatenate([c_prompt[s:s + 1], c_sample[bs]], 0))
        m["xs_in"] = np.ascontiguousarray(x_sample[bs].transpose(1, 0, 2).reshape(64, D))
        m["st_sc"] = _st_in(f(state_sc_conv[0, bs]), 16)
        m["st_ssd"] = _st_in(f(state_ssd_conv[0, bs]), 48)
        m["st_ffn"] = _st_in(f(state_ffn_conv[0, bs]), 172)
        m["ssm_in"] = f(state_ssm[0, bs]).reshape(16, 4096, 128)
        in_maps.append(m)
    res = run_bass_kernel_spmd(nc, in_maps, core_ids=list(range(NCORE)))
    R = res.results
    kernel.last = R
    B, L = x_prompt.shape[0], x_prompt.shape[1]
    y_prompt = np.zeros((B, L, D), np.float32)
    y_sample = np.zeros((128, 4, D), np.float32)
    p_sc = np.zeros((1, B, 2, D_SC), np.float32)
    p_ssd = np.zeros((1, B, 3, D_XBC), np.float32)
    p_ssm = np.zeros((1, B, NH, HP, NS), np.float32)
    p_ffn = np.zeros((1, B, 2, 2 * D_FF), np.float32)
    s_sc = np.zeros((1, 128, 2, D_SC), np.float32)
    s_ssd = np.zeros((1, 128, 3, D_XBC), np.float32)
    s_ssm = np.zeros((1, 128, NH, HP, NS), np.float32)
    s_ffn = np.zeros((1, 128, 2, 2 * D_FF), np.float32)
    for c in range(NCORE):
        s, half = c // 2, c % 2
        r = R[c]
        nm = (NMAIN - 1) * 128
        y_prompt[s, half * 1024:half * 1024 + nm] = r["y_main"][128:]
        bs = slice(16 * c, 16 * c + 16)
        y_sample[bs] = r["y_samp"].reshape(4, 16, D).transpose(1, 0, 2)
        if half == 1:
            p_sc[0, s] = r["o_sc_p"].transpose(1, 0, 2).reshape(D_SC, 2).T
            p_ssd[0, s] = r["o_ssd_p"].transpose(1, 0, 2).reshape(D_XBC, 3).T
            p_ffn[0, s] = r["o_ffn_p"].transpose(1, 0, 2).reshape(2 * D_FF, 2).T
            p_ssm[0, s] = r["o_ssm_p"].reshape(NH, HP, NS)
        s_sc[0, bs] = _st_out(r["o_sc_s"], 16, 2)
        s_ssd[0, bs] = _st_out(r["o_ssd_s"], 48, 3)
        s_ffn[0, bs] = _st_out(r["o_ffn_s"], 172, 2)
        s_ssm[0, bs] = r["o_ssm_s"].reshape(16, NH, HP, NS)
    return (y_prompt, y_sample, p_sc, p_ssd, p_ssm, p_ffn, s_sc, s_ssd, s_ssm, s_ffn)
```

```python
import os
import contextlib
import numpy as np
import concourse.bass as bass
import concourse.mybir as mybir
from concourse.bass_utils import run_bass_kernel_spmd

F32 = mybir.dt.float32
BF16 = mybir.dt.bfloat16
AF = mybir.ActivationFunctionType
ALU = mybir.AluOpType

D = 4096
D_SC = 2048
D_SSD = 4096
NH = 64
HP = 64
NG = 8
NS = 128
D_XBC = 6144
D_IN = 16448
D_FF = 11008
EPS = 1e-6
NCORE = 8
NPRE = int(os.environ.get("K_NPRE", "7"))
NMAIN = int(os.environ.get("K_NMAIN", "9"))
DO_SAMPLE = int(os.environ.get("K_SAMPLE", "1"))
DEBUG = int(os.environ.get("K_DEBUG", "0"))

C_SC_B, C_SC_C, C_SC_X, C_Z, C_XBC, C_DT = 0, 2048, 4096, 6144, 10240, 16384


class Buf:
    __slots__ = ("w", "r", "name")

    def __init__(self, name=""):
        self.w = None
        self.r = {}
        self.name = name


class TT:
    def __init__(self, h, name, nb=0):
        self.h = h
        self.name = name
        self.all = Buf(name)
        self.subs = [Buf(f"{name}.{i}") for i in range(nb)]

    def B(self, i=None):
        if not self.subs:
            return [self.all]
        if i is None:
            return list(self.subs)
        if isinstance(i, (list, tuple, range)):
            return [self.subs[j] for j in i]
        return [self.subs[i]]


class Eng:
    def __init__(self, name, h, sem):
        self.name, self.h, self.sem = name, h, sem
        self.count = 0
        self.waited = {}


class Slot:
    def __init__(self, sem):
        self.sem = sem
        self.count = 0


class Ker:
    def __init__(self, nc, es):
        self.nc, self.es = nc, es
        self.eng = {}
        for n, h in (("pe", nc.tensor), ("act", nc.scalar), ("dve", nc.vector),
                     ("pool", nc.gpsimd), ("sp", nc.sync)):
            self.eng[n] = Eng(n, h, es.enter_context(nc.semaphore("s_" + n)))
        self.slots = {q: [Slot(es.enter_context(nc.semaphore(f"d_{q}{i}"))) for i in range(10)]
                      for q in ("sp", "pool")}
        self.slot_i = {"sp": 0, "pool": 0}
        self.nt = 0
        self.dbg_outs = []

    def sb(self, name, shape, dt, nb=0):
        h = self.es.enter_context(self.nc.sbuf_tensor("sb_" + name, list(shape), dt))
        return TT(h, name, nb)

    def ps(self, name):
        h = self.es.enter_context(self.nc.psum_tensor(name, [128, 512], F32))
        return TT(h, name)

    def _deps(self, E, R, W):
        deps = {}
        for b in R:
            if b.w is not None:
                s, v = b.w
                if deps.get(s, (None, 0))[1] < v:
                    deps[s] = (s, v)
        for b in W:
            if b.w is not None:
                s, v = b.w
                if deps.get(s, (None, 0))[1] < v:
                    deps[s] = (s, v)
            for s, v in b.r.items():
                if deps.get(s, (None, 0))[1] < v:
                    deps[s] = (s, v)
        for s, v in deps.values():
            if s is E.sem and E.name == "pe":
                continue
            if E.waited.get(s, 0) < v:
                E.h.wait_ge(s, v)
                E.waited[s] = v

    def op(self, e, fn, R=(), W=(), inc=True):
        E = self.eng[e]
        self._deps(E, R, W)
        ins = fn(E.h)
        if inc:
            E.count += 1
            ins.then_inc(E.sem, 1)
            val = E.count
        else:
            val = E.count + 1
        for b in R:
            if b.r.get(E.sem, 0) < val:
                b.r[E.sem] = val
        for b in W:
            b.w = (E.sem, val)
            b.r = {}
        return ins

    def dma(self, q, out, in_, R=(), W=()):
        E = self.eng[q]
        self._deps(E, R, W)
        sl = self.slots[q][self.slot_i[q] % len(self.slots[q])]
        self.slot_i[q] += 1
        if sl.count and E.waited.get(sl.sem, 0) < sl.count:
            E.h.wait_ge(sl.sem, sl.count)
            E.waited[sl.sem] = sl.count
        E.h.dma_start(out=out, in_=in_).then_inc(sl.sem, 16)
        sl.count += 16
        for b in R:
            b.r[sl.sem] = sl.count
        for b in W:
            b.w = (sl.sem, sl.count)
            b.r = {}

    def barrier(self):
        for E in self.eng.values():
            for Fe in self.eng.values():
                if Fe is E or Fe.count == 0:
                    continue
                if E.waited.get(Fe.sem, 0) < Fe.count:
                    E.h.wait_ge(Fe.sem, Fe.count)
                    E.waited[Fe.sem] = Fe.count
            for q in ("sp", "pool"):
                for sl in self.slots[q]:
                    if sl.count and E.waited.get(sl.sem, 0) < sl.count:
                        E.h.wait_ge(sl.sem, sl.count)
                        E.waited[sl.sem] = sl.count

    def finish(self):
        E = self.eng["sp"]
        for q in ("sp", "pool"):
            for sl in self.slots[q]:
                if sl.count and E.waited.get(sl.sem, 0) < sl.count:
                    E.h.wait_ge(sl.sem, sl.count)
                    E.waited[sl.sem] = sl.count

    def dbg(self, name, tt, ap, shape, dt=F32):
        if not DEBUG:
            return
        d = self.nc.dram_tensor("dbg_" + name, list(shape), dt, kind="ExternalOutput").ap()
        self.dma("sp", d, ap, R=tt.B())
        self.dbg_outs.append("dbg_" + name)


def build(nc):
    es = contextlib.ExitStack()
    with es:
        _build(nc, es)
    return nc


def _build(nc, es):
    k = Ker(nc, es)

    def din(name, shape):
        return nc.dram_tensor(name, list(shape), F32, kind="ExternalInput").ap()

    def dout(name, shape):
        return nc.dram_tensor(name, list(shape), F32, kind="ExternalOutput").ap()

    NTOK_M = NMAIN * 128
    xm = din("xm", [NTOK_M, D])
    xp = din("xp", [max(NPRE, 1) * 128, D])
    flag_d = din("flag", [128, 1])
    cvec = din("cvec", [17, D])
    xs_in = din("xs_in", [64, D])
    st_sc = din("st_sc", [128, 16, 32])
    st_ssd = din("st_ssd", [128, 48, 48])
    st_ffn = din("st_ffn", [128, 172, 32])
    ssm_in = din("ssm_in", [16, 4096, 128])
    w_ada = din("w_ada", [D, 6 * D])
    b_adaT = din("b_adaT", [128, 192])
    g1T_d = din("g1T", [128, 32])
    g2T_d = din("g2T", [128, 32])
    gsT_d = din("gsT", [128, 32])
    w_in = din("w_in", [D, D_IN])
    wsc_d = din("wsc", [128, 16, 3])
    wssd_d = din("wssd", [128, 48, 4])
    bssd_d = din("bssd", [128, 48])
    dtb_d = din("dtb", [1, 64])
    alog_d = din("alog", [1, 64])
    dsk_d = din("dsk", [1, 64])
    w_out = din("w_out", [D_SC + D_SSD, D])
    w_up = din("w_up", [D, 2 * D_FF])
    wffn_d = din("wffn", [128, 172, 3])
    bffn_d = din("bffn", [128, 172])
    w_down = din("w_down", [D_FF, D])
    gfin_d = din("gfin", [1, D])
    consts_d = din("consts", [128, 640])

    y_main = dout("y_main", [NTOK_M, D])
    y_samp = dout("y_samp", [64, D])
    o_sc_p = dout("o_sc_p", [128, 16, 2])
    o_ssd_p = dout("o_ssd_p", [128, 48, 3])
    o_ffn_p = dout("o_ffn_p", [128, 172, 2])
    o_ssm_p = dout("o_ssm_p", [4096, 128])
    o_sc_s = dout("o_sc_s", [128, 16, 32])
    o_ssd_s = dout("o_ssd_s", [128, 48, 48])
    o_ffn_s = dout("o_ffn_s", [128, 172, 32])
    o_ssm_s = dout("o_ssm_s", [16, 4096, 128])

    cst = k.sb("cst", [128, 640], F32)
    identb = k.sb("identb", [128, 128], BF16)
    flag = k.sb("flag", [128, 1], F32)
    modT = k.sb("modT", [128, 192, 17], F32)
    gm1 = k.sb("gm1", [128, 32, 17], F32)
    gm2 = k.sb("gm2", [128, 32, 17], F32)
    g1T = k.sb("g1T", [128, 32], F32)
    g2T = k.sb("g2T", [128, 32], F32)
    gsT = k.sb("gsT", [128, 32], F32)
    badaT = k.sb("badaT", [128, 192], F32)
    wsc = k.sb("wsc", [128, 16, 3], F32)
    wssd = k.sb("wssd", [128, 48, 4], F32)
    bssd = k.sb("bssd", [128, 48], F32)
    wffn = k.sb("wffn", [128, 172, 3], F32)
    bffn = k.sb("bffn", [128, 172], F32)
    dtb = k.sb("dtb", [128, 64], F32)
    Abc = k.sb("Abc", [128, 64], F32)
    Dbc = k.sb("Dbc", [128, 64], F32)
    cT = k.sb("cT", [128, 32, 17], BF16)

    NWB = 6
    KP = 16
    wst = [k.sb(f"wst{i}", [128, KP, 128], BF16) for i in range(NWB)]
    xres = [k.sb(f"xres{i}", [128, D], F32) for i in range(1)]
    xnb = k.sb("xnb", [128, D], BF16)
    hT = k.sb("hT", [128, 32, 128], BF16, nb=32)
    arena = k.sb("arena", [128, 86, 128], BF16)
    actT = TT(arena.h[:, :, :], "actT", nb=86)
    ymT = TT(arena.h[:, 0:48, :], "ymT", nb=48)
    BT = TT(arena.h[:, 48:56, :], "BT", nb=8)
    CT = TT(arena.h[:, 56:64, :], "CT", nb=8)
    B_tok = TT(arena.h[:, 64:72, :].rearrange("p a b -> p (a b)"), "B_tok")
    xs_tok = k.sb("xs_tok", [128, D], BF16)
    sz_tok = k.sb("sz_tok", [128, D], BF16)
    S = k.sb("S", [128, D], F32, nb=0)
    sbg = [k.sb(f"sbg{i}", [128, 512], BF16) for i in range(2)]
    tl_sc = k.sb("tl_sc", [128, 16, 2], F32, nb=16)
    tl_ssd = k.sb("tl_ssd", [128, 48, 3], F32, nb=48)
    tl_ffn = k.sb("tl_ffn", [128, 172, 2], F32, nb=172)
    cvb = [k.sb(f"cvb{i}", [128, 128 + 48], F32) for i in range(3)]
    cva = [k.sb(f"cva{i}", [128, 128], F32) for i in range(3)]
    cvc = [k.sb(f"cvc{i}", [128, 128], F32) for i in range(2)]
    cvh = [k.sb(f"cvh{i}", [128, 128], BF16) for i in range(3)]
    mixg = [k.sb(f"mixg{i}", [128, 4, 128], F32) for i in range(2)]
    sm = {n: k.sb("sm_" + n, [128, 64], F32) for n in
          ("v", "av", "e", "l", "dt", "a", "ac", "dte", "w1", "eac", "dec", "tot")}
    epsb = k.sb("epsb", [128, 1], F32)
    k.op("dve", lambda e: e.memset(epsb.h[:], EPS), W=epsb.B())
    st1 = k.sb("st1", [128, 8], F32)
    st2 = k.sb("st2", [128, 8], F32)
    xdt = xnb
    xdte = k.sb("xdte", [128, D], BF16)
    xsD = xs_tok
    cbm = k.sb("cbm", [128, 8, 128], F32)
    sgw = [k.sb(f"sgw{i}", [128, 512], F32) for i in range(2)]
    MT = [k.sb(f"MT{i}", [128, 4, 128], BF16) for i in range(2)]
    gall = k.sb("gall", [128, D], F32)
    gnb = xs_tok
    gfin = gall
    yw = [k.sb(f"yw{i}", [128, 512], F32) for i in range(2)]

    PSP = [k.ps(f"psp{i}") for i in range(3)]
    PSW = [k.ps(f"psw{i}") for i in range(5)]
    ring = {"p": 0, "w": 0, "wst": 0, "cvb": 0, "cva": 0, "cvc": 0, "cvh": 0, "mixg": 0,
            "sgw": 0, "sgx": 0, "MT": 0, "yw": 0, "xres": 0, "sbg": 0}

    def nxt(lst, key):
        t = lst[ring[key] % len(lst)]
        ring[key] += 1
        return t

    ident = cst.h[:, 0:128]
    triU = cst.h[:, 128:256]
    ones = cst.h[:, 256:384]
    triS = cst.h[0:64, 384:448]
    onesS = cst.h[0:64, 448:512]
    rowmask = cst.h[0:64, 512:528]

    def ld(tt, src):
        k.dma("sp", tt.h[:], src, W=tt.B())

    ld(cst, consts_d)
    ld(flag, flag_d)
    ld(g1T, g1T_d); ld(g2T, g2T_d); ld(gsT, gsT_d); ld(badaT, b_adaT)
    ld(wsc, wsc_d); ld(wssd, wssd_d); ld(bssd, bssd_d); ld(wffn, wffn_d); ld(bffn, bffn_d)
    k.dma("sp", dtb.h[:], dtb_d.partition_broadcast(128), W=dtb.B())
    k.dma("sp", Abc.h[:], alog_d.partition_broadcast(128), W=Abc.B())
    k.dma("sp", Dbc.h[:], dsk_d.partition_broadcast(128), W=Dbc.B())
    k.op("dve", lambda e: e.tensor_copy(out=identb.h[:], in_=ident), R=cst.B(), W=identb.B())
    k.op("act", lambda e: e.activation(out=Abc.h[:], in_=Abc.h[:], func=AF.Exp), R=Abc.B(), W=Abc.B())
    k.op("dve", lambda e: e.tensor_scalar(out=Abc.h[:], in0=Abc.h[:], scalar1=-1.0, scalar2=None,
                                          op0=ALU.mult), R=Abc.B(), W=Abc.B())

    scr_map = {}
    SCR = {}
    for nm, nblk in (("w_in", 129 * 2), ("w_up", 172 * 2), ("w_out", 32 * 3), ("w_down", 32 * 6)):
        SCR[nm] = {"n": 0, "t": nc.dram_tensor("scr_" + nm, [nblk, 128, KP * 128], BF16).ap()}

    def proj(Wd, blocks, kcs, rhs_fn, rhs_bufs_fn, NT, consumer):
        npart = -(-kcs // KP)
        bnd = [round(i * kcs / npart) for i in range(npart + 1)]
        kparts = [(bnd[i], bnd[i + 1]) for i in range(npart)]
        jobs = [(bi, kp) for bi in range(len(blocks)) for kp in kparts]
        loaded = {}
        pend_wr = {}
        wname = Wd.tensor.name
        use_scr = wname in SCR

        def issue(j):
            bi, (k0, k1) = jobs[j]
            c0, ncol, _ = blocks[bi]
            t = nxt(wst, "wst")
            key = (wname, c0, k0)
            if key in scr_map:
                sbuf_, ap = scr_map[key]
                k.dma("sp", t.h[:, 0:k1 - k0, 0:ncol], ap, R=[sbuf_], W=t.B())
            else:
                src = Wd[k0 * 128:k1 * 128, c0:c0 + ncol].rearrange("(c p) n -> p c n", p=128)
                k.dma("pool", t.h[:, 0:k1 - k0, 0:ncol], src, W=t.B())
                if use_scr:
                    st_ = SCR[wname]
                    idx = st_["n"]
                    st_["n"] += 1
                    ap = st_["t"][idx].rearrange("p (c n) -> p c n", n=128)[:, 0:k1 - k0, 0:ncol]
                    sbuf_ = Buf("scr")
                    scr_map[key] = (sbuf_, ap)
                    pend_wr[j] = (ap, t.h[:, 0:k1 - k0, 0:ncol], sbuf_)
            loaded[j] = t

        PF = NWB - 1
        for j in range(min(PF, len(jobs))):
            issue(j)
        cur = None
        for j, (bi, (k0, k1)) in enumerate(jobs):
            c0, ncol, tag = blocks[bi]
            t = loaded.pop(j)
            if k0 == 0:
                cur = nxt(PSP, "p")
            for kc in range(k0, k1):
                last = (kc == kcs - 1)
                k.op("pe", lambda e, kc=kc, t=t, last=last: e.matmul(
                    cur.h[0:ncol, 0:NT], lhsT=t.h[:, kc - k0, 0:ncol], rhs=rhs_fn(kc),
                    start=(kc == 0), stop=last),
                    R=t.B() + rhs_bufs_fn(kc), W=cur.B(), inc=last)
            if j in pend_wr:
                ap_, src_, sbuf_ = pend_wr.pop(j)
                k.dma("pool", ap_, src_, R=t.B(), W=[sbuf_])
            if j + PF < len(jobs):
                issue(j + PF)
            if k1 == kcs:
                consumer(tag, cur, cur.h[0:ncol, 0:NT])

    cv = nxt(xres, "xres")
    k.dma("sp", cv.h[0:17, :], cvec, W=cv.B())
    k.op("act", lambda e: e.activation(out=xnb.h[0:17, :], in_=cv.h[0:17, :], func=AF.Silu),
         R=cv.B(), W=xnb.B())
    for j0 in range(0, 32, 8):
        pt = nxt(PSW, "w")
        pv = pt.h[:].bitcast(BF16)
        for j in range(8):
            k.op("pe", lambda e, j=j: e.transpose(out=pv[:, j * 18:j * 18 + 17],
                                                  in_=xnb.h[0:17, (j0 + j) * 128:(j0 + j + 1) * 128],
                                                  identity=identb.h[0:17, 0:17]),
                 R=xnb.B() + identb.B(), W=pt.B(), inc=(j == 7))
        k.op("dve", lambda e: e.tensor_copy(
            out=cT.h[:, j0:j0 + 8, :], in_=pv[:, 0:8 * 18].rearrange("p (j r) -> p j r", r=18)[:, :, 0:17]),
            R=pt.B(), W=cT.B())

    def ada_cons(tag, pt, pap):
        k.op("act", lambda e: e.activation(out=modT.h[:, tag, :], in_=pap, func=AF.Identity,
                                           bias=badaT.h[:, tag:tag + 1], scale=1.0),
             R=pt.B() + badaT.B(), W=modT.B())

    proj(w_ada, [(i * 128, 128, i) for i in range(192)], 32,
         lambda kc: cT.h[:, kc, :], lambda kc: cT.B(), 17, ada_cons)
    for (gm, gT, off) in ((gm1, g1T, 32), (gm2, g2T, 128)):
        k.op("dve", lambda e, gm=gm, off=off: e.tensor_scalar(
            out=gm.h[:], in0=modT.h[:, off:off + 32, :], scalar1=1.0, scalar2=None, op0=ALU.add),
            R=modT.B(), W=gm.B())
        k.op("dve", lambda e, gm=gm, gT=gT: e.tensor_tensor(
            out=gm.h[:], in0=gm.h[:], in1=gT.h[:].unsqueeze(2).to_broadcast([128, 32, 17]),
            op=ALU.mult), R=gm.B() + gT.B(), W=gm.B())
    SH1, GT1, SH2, GT2 = 0, 64, 96, 160

    class Cfg:
        pass

    def mk_cfg(sample):
        c = Cfg()
        c.sample = sample
        c.NT = 64 if sample else 128
        c.shift = 16 if sample else 1

        def mod_ap(tt, off, j0, n):
            if sample:
                return tt.h[:, off + j0:off + j0 + n, 1:17].unsqueeze(2).to_broadcast([128, n, 4, 16])
            return tt.h[:, off + j0:off + j0 + n, 0:1].to_broadcast([128, n, 128])

        def view(ap):
            if sample:
                return ap.rearrange("p j (t b) -> p j t b", b=16)
            return ap
        c.mod_ap, c.view = mod_ap, view
        return c

    def norm_to_hT(c, xr, gm, shoff):
        NT = c.NT
        k.op("dve", lambda e: e.memset(st1.h[:, 0:1], 0.0), W=st1.B())
        k.op("act", lambda e: e.activation(out=xnb.h[0:NT, :], in_=xr.h[0:NT, :], func=AF.Square,
                                           accum_out=st1.h[0:NT, 0:1]),
             R=xr.B() + st1.B(), W=xnb.B() + st1.B())
        k.op("act", lambda e: e.activation(out=st1.h[0:NT, 1:2], in_=st1.h[0:NT, 0:1], func=AF.Ln,
                                           scale=1.0 / D, bias=epsb.h[0:NT, 0:1]), R=st1.B() + epsb.B(), W=st1.B())
        k.op("act", lambda e: e.activation(out=st1.h[0:NT, 2:3], in_=st1.h[0:NT, 1:2], func=AF.Exp,
                                           scale=-0.5), R=st1.B(), W=st1.B())
        k.op("act", lambda e: e.activation(out=xnb.h[0:NT, :], in_=xr.h[0:NT, :], func=AF.Copy,
                                           scale=st1.h[0:NT, 2:3]),
             R=xr.B() + st1.B(), W=xnb.B())
        for j0 in range(0, 32, 4):
            pt = nxt(PSW, "w")
            pv = pt.h[:].bitcast(BF16)
            for j in range(4):
                k.op("pe", lambda e, j=j: e.transpose(
                    out=pv[:, j * NT:(j + 1) * NT], in_=xnb.h[0:NT, (j0 + j) * 128:(j0 + j + 1) * 128],
                    identity=identb.h[0:NT, 0:NT]), R=xnb.B() + identb.B(), W=pt.B(), inc=(j == 3))
            src = pv[:, 0:4 * NT].rearrange("p (j t) -> p j t", t=NT)
            dst = hT.h[:, j0:j0 + 4, 0:NT]
            k.op("dve", lambda e: e.tensor_tensor(out=c.view(dst), in0=c.view(src),
                                                  in1=c.mod_ap(gm, 0, j0, 4), op=ALU.mult),
                 R=pt.B() + gm.B(), W=hT.B(range(j0, j0 + 4)))
            k.op("dve", lambda e: e.tensor_tensor(out=c.view(dst), in0=c.view(dst),
                                                  in1=c.mod_ap(modT, shoff, j0, 4), op=ALU.add),
                 R=hT.B(range(j0, j0 + 4)) + modT.B(), W=hT.B(range(j0, j0 + 4)))

    TLS = {}

    def tail_load(c, cb, TL, tl, blk):
        if c.sample:
            k.dma("sp", cb.h[:, 0:TL], TLS[tl.name][0][:, blk, :], W=cb.B())
        else:
            k.op("dve", lambda e: e.tensor_copy(out=cb.h[:, 0:TL], in_=tl.h[:, blk, 0:TL]),
                 R=tl.B(blk), W=cb.B())

    def tail_store(c, cb, TL, tl, blk):
        NT = c.NT
        if c.sample:
            k.dma("sp", TLS[tl.name][1][:, blk, :], cb.h[:, NT:NT + TL], R=cb.B())
        else:
            k.op("dve", lambda e: e.tensor_copy(out=tl.h[:, blk, 0:TL], in_=cb.h[:, NT:NT + TL]),
                 R=cb.B(), W=tl.B(blk))

    def conv_block(c, pt, pap, tl, blk, wt, Kw, bias_ap):
        NT, sh = c.NT, c.shift
        TL = (Kw - 1) * sh
        cb = nxt(cvb, "cvb")
        ca = nxt(cva, "cva")
        k.op("act", lambda e: e.activation(out=cb.h[:, TL:TL + NT], in_=pap, func=AF.Copy),
             R=pt.B(), W=cb.B())
        tail_load(c, cb, TL, tl, blk)
        if bias_ap is not None:
            k.op("dve", lambda e: e.tensor_scalar(out=ca.h[:, 0:NT], in0=cb.h[:, TL:TL + NT],
                                                  scalar1=wt.h[:, blk, Kw - 1:Kw], scalar2=bias_ap,
                                                  op0=ALU.mult, op1=ALU.add),
                 R=cb.B() + wt.B(), W=ca.B())
        else:
            k.op("dve", lambda e: e.tensor_scalar(out=ca.h[:, 0:NT], in0=cb.h[:, TL:TL + NT],
                                                  scalar1=wt.h[:, blk, Kw - 1:Kw], scalar2=None,
                                                  op0=ALU.mult), R=cb.B() + wt.B(), W=ca.B())
        for kk in range(Kw - 1):
            o = kk * sh
            k.op("dve", lambda e, kk=kk, o=o: e.scalar_tensor_tensor(
                out=ca.h[:, 0:NT], in0=cb.h[:, o:o + NT], scalar=wt.h[:, blk, kk:kk + 1],
                in1=ca.h[:, 0:NT], op0=ALU.mult, op1=ALU.add), R=cb.B() + wt.B() + ca.B(), W=ca.B())
        tail_store(c, cb, TL, tl, blk)
        return ca

    tq = {"pt": None, "n": 0, "dst": None, "c0": 0}

    def tr_push(c, src_tt, src_ap, dst_tt, col0):
        NT = c.NT
        if tq["n"] == 0:
            tq["pt"] = nxt(PSW, "w")
            tq["dst"], tq["c0"] = dst_tt, col0
        pt = tq["pt"]
        pv = pt.h[:].bitcast(BF16)
        n = tq["n"]
        k.op("pe", lambda e: e.transpose(out=pv[0:NT, n * 128:(n + 1) * 128], in_=src_ap,
                                         identity=identb.h[:, :]),
             R=src_tt.B() + identb.B(), W=pt.B())
        tq["n"] += 1
        if tq["n"] == 4:
            tr_flush(c)

    def tr_flush(c):
        if tq["n"] == 0:
            return
        NT = c.NT
        pt, n, dst, c0 = tq["pt"], tq["n"], tq["dst"], tq["c0"]
        pv = pt.h[:].bitcast(BF16)
        k.op("act", lambda e: e.activation(out=dst.h[0:NT, c0:c0 + n * 128], in_=pv[0:NT, 0:n * 128],
                                           func=AF.Copy), R=pt.B(), W=dst.B())
        tq["n"] = 0

    def in_proj(c, prefix):
        NT = c.NT
        st = {}

        def cons(tag, pt, pap):
            kind, i = tag
            if kind == "c":
                t = nxt(cvc, "cvc")
                k.op("act", lambda e: e.activation(out=t.h[:, 0:NT], in_=pap, func=AF.Copy),
                     R=pt.B(), W=t.B())
                st["c"] = t
            elif kind == "x":
                t = st["c"]
                cb = nxt(cvb, "cvb")
                ca = nxt(cva, "cva")
                TL = 2 * c.shift
                k.op("dve", lambda e: e.tensor_tensor(out=cb.h[:, TL:TL + NT], in0=t.h[:, 0:NT],
                                                      in1=pap, op=ALU.mult), R=t.B() + pt.B(), W=cb.B())
                tail_load(c, cb, TL, tl_sc, i)
                k.op("dve", lambda e: e.tensor_scalar(out=ca.h[:, 0:NT], in0=cb.h[:, TL:TL + NT],
                                                      scalar1=wsc.h[:, i, 2:3], scalar2=None,
                                                      op0=ALU.mult), R=cb.B() + wsc.B(), W=ca.B())
                for kk in range(2):
                    o = kk * c.shift
                    k.op("dve", lambda e, kk=kk, o=o: e.scalar_tensor_tensor(
                        out=ca.h[:, 0:NT], in0=cb.h[:, o:o + NT], scalar=wsc.h[:, i, kk:kk + 1],
                        in1=ca.h[:, 0:NT], op0=ALU.mult, op1=ALU.add),
                        R=cb.B() + wsc.B() + ca.B(), W=ca.B())
                tail_store(c, cb, TL, tl_sc, i)
                st["uc"] = ca
            elif kind == "b":
                ca = st["uc"]
                k.op("dve", lambda e: e.tensor_tensor(out=ymT.h[:, i, 0:NT], in0=ca.h[:, 0:NT],
                                                      in1=pap, op=ALU.mult),
                     R=ca.B() + pt.B(), W=ymT.B(i))
            elif kind == "z":
                hb = nxt(cvh, "cvh")
                k.op("act", lambda e: e.activation(out=hb.h[:, 0:NT], in_=pap, func=AF.Silu),
                     R=pt.B(), W=hb.B())
                tr_push(c, hb, hb.h[:, 0:NT], sz_tok, i * 128)
            elif kind == "xbc":
                ca = conv_block(c, pt, pap, tl_ssd, i, wssd, 4, bssd.h[:, i:i + 1])
                if i < 32:
                    hb = nxt(cvh, "cvh")
                    k.op("act", lambda e: e.activation(out=hb.h[:, 0:NT], in_=ca.h[:, 0:NT],
                                                       func=AF.Silu), R=ca.B(), W=hb.B())
                    tr_push(c, hb, hb.h[:, 0:NT], xs_tok, i * 128)
                elif i < 40:
                    g = i - 32
                    k.op("act", lambda e: e.activation(out=BT.h[:, g, 0:NT], in_=ca.h[:, 0:NT],
                                                       func=AF.Silu), R=ca.B(), W=BT.B(g))
                    tr_push(c, BT, BT.h[:, g, 0:NT], B_tok, g * 128)
                else:
                    g = i - 40
                    k.op("act", lambda e: e.activation(out=CT.h[:, g, 0:NT], in_=ca.h[:, 0:NT],
                                                       func=AF.Silu), R=ca.B(), W=CT.B(g))

        blocks = []
        if not prefix:
            for i in range(16):
                blocks += [(C_SC_C + i * 128, 128, ("c", i)), (C_SC_X + i * 128, 128, ("x", i)),
                           (C_SC_B + i * 128, 128, ("b", i))]
            blocks += [(C_Z + i * 128, 128, ("z", i)) for i in range(32)]
        blocks += [(C_XBC + i * 128, 128, ("xbc", i)) for i in range(48)]
        proj(w_in, blocks, 32, lambda kc: hT.h[:, kc, 0:NT], lambda kc: hT.B(kc), NT, cons)
        tr_flush(c)
        t = nxt(wst, "wst")
        tdt = t.h[:, :, :].rearrange("p a b -> p (a b)").rearrange("p (c n) -> p c n", n=64)
        k.dma("pool", tdt, w_in[:, C_DT:C_DT + 64].rearrange("(c p) n -> p c n", p=128), W=t.B())
        pt = nxt(PSW, "w")
        for kc in range(32):
            k.op("pe", lambda e, kc=kc: e.matmul(pt.h[0:NT, 0:64], lhsT=hT.h[:, kc, 0:NT],
                                                 rhs=tdt[:, kc, :], start=(kc == 0), stop=(kc == 31)),
                 R=t.B() + hT.B(kc), W=pt.B(), inc=(kc == 31))
        v, av, ee, ll, dt_, a_ = (sm[n] for n in ("v", "av", "e", "l", "dt", "a"))
        k.op("dve", lambda e: e.tensor_tensor(out=v.h[0:NT, :], in0=pt.h[0:NT, 0:64], in1=dtb.h[0:NT, :],
                                              op=ALU.add), R=pt.B() + dtb.B(), W=v.B())
        k.op("act", lambda e: e.activation(out=av.h[0:NT, :], in_=v.h[0:NT, :], func=AF.Abs),
             R=v.B(), W=av.B())
        k.op("act", lambda e: e.activation(out=ee.h[0:NT, :], in_=av.h[0:NT, :], func=AF.Exp, scale=-1.0),
             R=av.B(), W=ee.B())
        k.op("act", lambda e: e.activation(out=ll.h[0:NT, :], in_=ee.h[0:NT, :], func=AF.Ln, bias=1.0),
             R=ee.B(), W=ll.B())
        k.op("dve", lambda e: e.scalar_tensor_tensor(out=dt_.h[0:NT, :], in0=v.h[0:NT, :], scalar=0.0,
                                                     in1=ll.h[0:NT, :], op0=ALU.max, op1=ALU.add),
             R=v.B() + ll.B(), W=dt_.B())
        k.op("dve", lambda e: e.tensor_tensor(out=a_.h[0:NT, :], in0=dt_.h[0:NT, :], in1=Abc.h[0:NT, :],
                                              op=ALU.mult), R=dt_.B() + Abc.B(), W=a_.B())

    def hb3(ap64, NT):
        return ap64.unsqueeze(2).to_broadcast([NT, 64, 64])

    def v3(ap, NT):
        return ap.rearrange("t (h p) -> t h p", p=64)

    def ssd_decays(c):
        NT = c.NT
        tri_, ones_ = (triS, onesS) if c.sample else (triU, ones)
        a_, ac, dte, w1, eac, dec, dt_ = (sm[n] for n in ("a", "ac", "dte", "w1", "eac", "dec", "dt"))
        p1 = nxt(PSW, "w")
        k.op("pe", lambda e: e.matmul(p1.h[0:NT, 0:64], lhsT=tri_, rhs=a_.h[0:NT, :], start=True, stop=True),
             R=cst.B() + a_.B(), W=p1.B())
        k.op("pe", lambda e: e.matmul(p1.h[0:NT, 64:128], lhsT=ones_, rhs=a_.h[0:NT, :], start=True, stop=True),
             R=cst.B() + a_.B(), W=p1.B())
        k.op("act", lambda e: e.activation(out=ac.h[0:NT, :], in_=p1.h[0:NT, 0:64], func=AF.Copy),
             R=p1.B(), W=ac.B())
        k.op("act", lambda e: e.activation(out=eac.h[0:NT, :], in_=p1.h[0:NT, 0:64], func=AF.Exp),
             R=p1.B(), W=eac.B())
        k.op("act", lambda e: e.activation(out=dec.h[0:NT, :], in_=p1.h[0:NT, 64:128], func=AF.Exp),
             R=p1.B(), W=dec.B())
        k.op("dve", lambda e: e.tensor_tensor(out=dte.h[0:NT, :], in0=p1.h[0:NT, 64:128], in1=ac.h[0:NT, :],
                                              op=ALU.subtract), R=p1.B() + ac.B(), W=dte.B())
        k.op("act", lambda e: e.activation(out=dte.h[0:NT, :], in_=dte.h[0:NT, :], func=AF.Exp),
             R=dte.B(), W=dte.B())
        k.op("dve", lambda e: e.tensor_tensor(out=w1.h[0:NT, :], in0=dte.h[0:NT, :], in1=dt_.h[0:NT, :],
                                              op=ALU.mult), R=dte.B() + dt_.B(), W=w1.B())
        k.op("dve", lambda e: e.tensor_tensor(out=v3(xdte.h[0:NT, :], NT), in0=v3(xs_tok.h[0:NT, :], NT),
                                              in1=hb3(w1.h[0:NT, :], NT), op=ALU.mult),
             R=xs_tok.B() + w1.B(), W=xdte.B())

    def ssd_intra_group_prep(c):
        NT = c.NT
        tri_ = triS if c.sample else triU
        dt_ = sm["dt"]
        k.op("dve", lambda e: e.tensor_tensor(out=v3(xdt.h[0:NT, :], NT), in0=v3(xs_tok.h[0:NT, :], NT),
                                              in1=hb3(dt_.h[0:NT, :], NT), op=ALU.mult),
             R=xs_tok.B() + dt_.B(), W=xdt.B())
        k.op("dve", lambda e: e.tensor_tensor(out=v3(xsD.h[0:NT, :], NT), in0=v3(xs_tok.h[0:NT, :], NT),
                                              in1=hb3(Dbc.h[0:NT, :], NT), op=ALU.mult),
             R=xs_tok.B() + Dbc.B(), W=xsD.B())
        for g0 in (0, 4):
            pc = nxt(PSW, "w")
            for g in range(g0, g0 + 4):
                k.op("pe", lambda e, g=g: e.matmul(pc.h[0:NT, (g - g0) * NT:(g - g0 + 1) * NT],
                                                   lhsT=BT.h[:, g, 0:NT], rhs=CT.h[:, g, 0:NT],
                                                   start=True, stop=True),
                     R=BT.B(g) + CT.B(g), W=pc.B(), inc=(g == g0 + 3))
            k.op("dve", lambda e: e.tensor_tensor(
                out=cbm.h[0:NT, g0:g0 + 4, 0:NT],
                in0=pc.h[0:NT, 0:4 * NT].rearrange("p (g t) -> p g t", t=NT),
                in1=tri_.unsqueeze(1).to_broadcast([NT, 4, NT]), op=ALU.mult),
                R=pc.B() + cst.B(), W=cbm.B())

    def ssd_intra_group(c, g, pd):
        NT = c.NT
        tri_ = triS if c.sample else triU
        a_, ac = sm["a"], sm["ac"]
        k.op("pe", lambda e: e.matmul(pd.h[0:NT, 0:512], lhsT=identb.h[0:NT, 0:NT],
                                      rhs=xsD.h[0:NT, g * 512:(g + 1) * 512], start=True, stop=False),
             R=identb.B() + xsD.B(), W=pd.B(), inc=False)
        for hh in (0, 4):
            h0 = g * 8 + hh
            pa = nxt(PSW, "w")
            for q in range(4):
                k.op("pe", lambda e, q=q: e.matmul(
                    pa.h[0:NT, q * NT:(q + 1) * NT],
                    lhsT=a_.h[0:NT, h0 + q:h0 + q + 1].to_broadcast([NT, NT]), rhs=tri_,
                    start=True, stop=True), R=a_.B() + cst.B(), W=pa.B(), inc=(q == 3))
            w = nxt(sgw, "sgw")
            wv = w.h[0:NT, 0:4 * NT].rearrange("p (q t) -> p q t", t=NT)
            k.op("dve", lambda e: e.tensor_tensor(
                out=wv, in0=pa.h[0:NT, 0:4 * NT].rearrange("p (q t) -> p q t", t=NT),
                in1=ac.h[0:NT, h0:h0 + 4].unsqueeze(2).to_broadcast([NT, 4, NT]), op=ALU.subtract),
                R=pa.B() + ac.B(), W=w.B())
            k.op("dve", lambda e: e.tensor_scalar(out=w.h[0:NT, 0:4 * NT], in0=w.h[0:NT, 0:4 * NT],
                                                  scalar1=0.0, scalar2=None, op0=ALU.min),
                 R=w.B(), W=w.B())
            k.op("act", lambda e: e.activation(out=w.h[0:NT, 0:4 * NT], in_=w.h[0:NT, 0:4 * NT],
                                               func=AF.Exp), R=w.B(), W=w.B())
            m = nxt(MT, "MT")
            k.op("dve", lambda e: e.tensor_tensor(
                out=m.h[0:NT, :, 0:NT], in0=wv,
                in1=cbm.h[0:NT, g:g + 1, 0:NT].to_broadcast([NT, 4, NT]), op=ALU.mult),
                R=w.B() + cbm.B(), W=m.B())
            for q in range(4):
                hq = hh + q
                last = (hq == 7)
                k.op("pe", lambda e, q=q, hq=hq, last=last: e.matmul(
                    pd.h[0:NT, hq * 64:(hq + 1) * 64], lhsT=m.h[0:NT, q, 0:NT],
                    rhs=xdt.h[0:NT, (h0 + q) * 64:(h0 + q + 1) * 64], start=False, stop=last),
                    R=m.B() + xdt.B(), W=pd.B(), inc=last)

    def gate_accum(c, g, ysrc_tt, ysrc_ap):
        NT = c.NT
        k.op("dve", lambda e: e.tensor_tensor(out=gall.h[0:NT, g * 512:(g + 1) * 512], in0=ysrc_ap,
                                              in1=sz_tok.h[0:NT, g * 512:(g + 1) * 512], op=ALU.mult),
             R=ysrc_tt.B() + sz_tok.B(), W=gall.B())

    def gate_finish(c):
        NT = c.NT
        k.op("dve", lambda e: e.memset(st2.h[:, 0:1], 0.0), W=st2.B())
        k.op("act", lambda e: e.activation(out=gnb.h[0:NT, :], in_=gall.h[0:NT, :], func=AF.Square,
                                           accum_out=st2.h[0:NT, 0:1]),
             R=gall.B() + st2.B(), W=gnb.B() + st2.B())
        k.op("act", lambda e: e.activation(out=st2.h[0:NT, 1:2], in_=st2.h[0:NT, 0:1], func=AF.Ln,
                                           scale=1.0 / D_SSD, bias=epsb.h[0:NT, 0:1]), R=st2.B() + epsb.B(), W=st2.B())
        k.op("act", lambda e: e.activation(out=st2.h[0:NT, 2:3], in_=st2.h[0:NT, 1:2], func=AF.Exp,
                                           scale=-0.5), R=st2.B(), W=st2.B())
        k.op("act", lambda e: e.activation(out=gnb.h[0:NT, :], in_=gall.h[0:NT, :], func=AF.Copy,
                                           scale=st2.h[0:NT, 2:3]), R=gall.B() + st2.B(), W=gnb.B())
        for j0 in range(0, 32, 4):
            pt = nxt(PSW, "w")
            pv = pt.h[:].bitcast(BF16)
            for j in range(4):
                k.op("pe", lambda e, j=j: e.transpose(
                    out=pv[:, j * NT:(j + 1) * NT], in_=gnb.h[0:NT, (j0 + j) * 128:(j0 + j + 1) * 128],
                    identity=identb.h[0:NT, 0:NT]), R=gnb.B() + identb.B(), W=pt.B(), inc=(j == 3))
            k.op("dve", lambda e: e.tensor_tensor(
                out=ymT.h[:, 16 + j0:16 + j0 + 4, 0:NT],
                in0=pv[:, 0:4 * NT].rearrange("p (j t) -> p j t", t=NT),
                in1=gsT.h[:, j0:j0 + 4].unsqueeze(2).to_broadcast([128, 4, NT]), op=ALU.mult),
                R=pt.B() + gsT.B(), W=ymT.B(range(16 + j0, 16 + j0 + 4)))

    def ssd_prompt(c, prefix):
        NT = 128
        ssd_decays(c)
        dec, eac = sm["dec"], sm["eac"]
        if not prefix:
            ssd_intra_group_prep(c)
        for g in range(8):
            gs = slice(g * 512, (g + 1) * 512)
            if not prefix:
                pd = nxt(PSW, "w")
                ssd_intra_group(c, g, pd)
                po = nxt(PSW, "w")
                sb_ = nxt(sbg, "sbg")
                k.op("act", lambda e: e.activation(out=sb_.h[:, :], in_=S.h[:, gs], func=AF.Copy),
                     R=S.B(), W=sb_.B())
                k.op("pe", lambda e: e.matmul(po.h[:, 0:512], lhsT=CT.h[:, g, :], rhs=sb_.h[:, :],
                                              start=True, stop=True), R=CT.B(g) + sb_.B(), W=po.B())
                y1 = nxt(yw, "yw")
                k.op("dve", lambda e: e.tensor_tensor(
                    out=y1.h[:, :].rearrange("t (h p) -> t h p", p=64),
                    in0=po.h[:, 0:512].rearrange("t (h p) -> t h p", p=64),
                    in1=eac.h[:, g * 8:(g + 1) * 8].unsqueeze(2).to_broadcast([128, 8, 64]), op=ALU.mult),
                    R=po.B() + eac.B(), W=y1.B())
                k.op("dve", lambda e: e.tensor_tensor(out=y1.h[:, :], in0=y1.h[:, :], in1=pd.h[:, 0:512],
                                                      op=ALU.add), R=y1.B() + pd.B(), W=y1.B())
                gate_accum(c, g, y1, y1.h[:, :])
            pst = nxt(PSW, "w")
            k.op("pe", lambda e: e.matmul(pst.h[:, 0:512], lhsT=B_tok.h[:, g * 128:(g + 1) * 128],
                                          rhs=xdte.h[:, gs], start=True, stop=True),
                 R=B_tok.B() + xdte.B(), W=pst.B())
            k.op("dve", lambda e: e.tensor_tensor(
                out=S.h[:, gs].rearrange("n (h p) -> n h p", p=64),
                in0=S.h[:, gs].rearrange("n (h p) -> n h p", p=64),
                in1=dec.h[:, g * 8:(g + 1) * 8].unsqueeze(2).to_broadcast([128, 8, 64]), op=ALU.mult),
                R=S.B() + dec.B(), W=S.B())
            k.op("dve", lambda e: e.tensor_tensor(out=S.h[:, gs], in0=S.h[:, gs], in1=pst.h[:, 0:512],
                                                  op=ALU.add), R=S.B() + pst.B(), W=S.B())
        if not prefix:
            gate_finish(c)

    def ssd_sample(c):
        NT = 64
        ssd_decays(c)
        ssd_intra_group_prep(c)
        eac, a_ = sm["eac"], sm["a"]
        dP = k.sb("dP", [128, 32, 16], F32)
        aexp = gall
        k.op("dve", lambda e: e.tensor_copy(out=v3(aexp.h[0:64, :], 64), in_=hb3(a_.h[0:64, :], 64)),
             R=a_.B(), W=aexp.B())
        for hp0 in range(0, 32, 8):
            pp = nxt(PSW, "w")
            for q in range(8):
                hp = hp0 + q
                k.op("pe", lambda e, q=q, hp=hp: e.matmul(pp.h[:, q * 16:(q + 1) * 16],
                                                          lhsT=aexp.h[0:64, hp * 128:(hp + 1) * 128],
                                                          rhs=onesS[:, 0:16], start=True, stop=True),
                     R=aexp.B() + cst.B(), W=pp.B(), inc=(q == 7))
            k.op("act", lambda e: e.activation(out=dP.h[:, hp0:hp0 + 8, :],
                                               in_=pp.h[:, 0:128].rearrange("p (q b) -> p q b", b=16),
                                               func=AF.Exp), R=pp.B(), W=dP.B())
        for g in range(8):
            pd = nxt(PSW, "w")
            ssd_intra_group(c, g, pd)
            k.op("act", lambda e, g=g, pd=pd: e.activation(out=gall.h[0:NT, g * 512:(g + 1) * 512],
                                                           in_=pd.h[0:NT, 0:512], func=AF.Copy),
                 R=pd.B(), W=gall.B())
        s0 = TT(S.h[:, :].rearrange("p (a b) -> p a b", b=128), "s0")
        s0.all = S.all
        sn = s0
        s0t = TT(hT.h[:, :, :].rearrange("p a b -> p (a b)"), "s0t")
        s0t.B = lambda i=None: hT.B()
        xmb = TT(xnb.h[0:64, :], "xmb")
        xmb.all = xnb.all
        for b in range(16):
            k.dma("sp", s0.h[:], ssm_in[b].rearrange("(hp q) n -> q hp n", q=128), W=s0.B())
            for hp0 in range(0, 32, 4):
                pt = nxt(PSW, "w")
                for q in range(4):
                    k.op("pe", lambda e, q=q: e.transpose(out=pt.h[:, q * 128:(q + 1) * 128],
                                                          in_=s0.h[:, hp0 + q, :], identity=ident),
                         R=s0.B() + cst.B(), W=pt.B(), inc=(q == 3))
                k.op("act", lambda e: e.activation(out=s0t.h[:, hp0 * 128:(hp0 + 4) * 128],
                                                   in_=pt.h[:, 0:512], func=AF.Copy), R=pt.B(), W=s0t.B())
            for g in range(8):
                po = nxt(PSW, "w")
                k.op("pe", lambda e, g=g: e.matmul(po.h[0:64, 0:512], lhsT=CT.h[:, g, 0:64],
                                                   rhs=s0t.h[:, g * 512:(g + 1) * 512], start=True, stop=True),
                     R=CT.B(g) + s0t.B(), W=po.B())
                y1 = nxt(yw, "yw")
                k.op("dve", lambda e, g=g, y1=y1, po=po: e.tensor_tensor(
                    out=y1.h[0:64, :].rearrange("t (h p) -> t h p", p=64),
                    in0=po.h[0:64, 0:512].rearrange("t (h p) -> t h p", p=64),
                    in1=eac.h[0:64, g * 8:(g + 1) * 8].unsqueeze(2).to_broadcast([64, 8, 64]), op=ALU.mult),
                    R=po.B() + eac.B(), W=y1.B())
                k.op("dve", lambda e, g=g, y1=y1: e.scalar_tensor_tensor(
                    out=gall.h[0:64, g * 512:(g + 1) * 512], in0=y1.h[0:64, :], scalar=rowmask[:, b:b + 1],
                    in1=gall.h[0:64, g * 512:(g + 1) * 512], op0=ALU.mult, op1=ALU.add),
                    R=y1.B() + cst.B() + gall.B(), W=gall.B())
            k.op("dve", lambda e: e.tensor_scalar(out=xmb.h[:, :], in0=xdte.h[0:64, :],
                                                  scalar1=rowmask[:, b:b + 1], scalar2=None, op0=ALU.mult),
                 R=xdte.B() + cst.B(), W=xmb.B())
            for hp0 in range(0, 32, 4):
                g = hp0 // 4
                pn = nxt(PSW, "w")
                for q in range(4):
                    hp = hp0 + q
                    k.op("pe", lambda e, q=q, hp=hp: e.matmul(pn.h[:, q * 128:(q + 1) * 128],
                                                              lhsT=xmb.h[:, hp * 128:(hp + 1) * 128],
                                                              rhs=B_tok.h[0:64, g * 128:(g + 1) * 128],
                                                              start=True, stop=True),
                         R=xmb.B() + B_tok.B(), W=pn.B(), inc=(q == 3))
                for q in range(4):
                    hp = hp0 + q
                    k.op("dve", lambda e, q=q, hp=hp: e.scalar_tensor_tensor(
                        out=sn.h[:, hp, :], in0=s0.h[:, hp, :], scalar=dP.h[:, hp, b:b + 1],
                        in1=pn.h[:, q * 128:(q + 1) * 128], op0=ALU.mult, op1=ALU.add),
                        R=s0.B() + dP.B() + pn.B(), W=sn.B())
            k.dma("sp", o_ssm_s[b].rearrange("(hp q) n -> q hp n", q=128), sn.h[:], R=sn.B())
        for g in range(8):
            gate_accum(c, g, gall, gall.h[0:64, g * 512:(g + 1) * 512])
        gate_finish(c)

    def proj_residual(c, Wd, kcs, rhs_tt, goff, xr):
        NT = c.NT
        stt = {"mg": None}

        def cons(tag, pt, pap):
            j = tag
            q = j % 4
            if q == 0:
                stt["mg"] = nxt(mixg, "mixg")
            mg = stt["mg"]
            dst = mg.h[:, q:q + 1, 0:NT]
            k.op("dve", lambda e: e.tensor_tensor(
                out=c.view(dst), in0=c.view(pap.unsqueeze(1)), in1=c.mod_ap(modT, goff, j, 1), op=ALU.mult),
                R=pt.B() + modT.B(), W=mg.B())
            if q == 3:
                j0 = j - 3
                p2 = nxt(PSW, "w")
                for qq in range(4):
                    k.op("pe", lambda e, qq=qq: e.transpose(out=p2.h[0:NT, qq * 128:(qq + 1) * 128],
                                                            in_=mg.h[:, qq, 0:NT], identity=ident),
                         R=mg.B() + cst.B(), W=p2.B(), inc=(qq == 3))
                k.op("dve", lambda e: e.tensor_tensor(out=xr.h[0:NT, j0 * 128:(j0 + 4) * 128],
                                                      in0=xr.h[0:NT, j0 * 128:(j0 + 4) * 128],
                                                      in1=p2.h[0:NT, 0:512], op=ALU.add),
                     R=xr.B() + p2.B(), W=xr.B())

        proj(Wd, [(j * 128, 128, j) for j in range(32)], kcs,
             lambda kc: rhs_tt.h[:, kc, 0:NT], lambda kc: rhs_tt.B(kc), NT, cons)

    def ffn_up(c):
        NT = c.NT
        stt = {}

        def cons(tag, pt, pap):
            kind, i = tag
            blk = i if kind == "g" else 86 + i
            ca = conv_block(c, pt, pap, tl_ffn, blk, wffn, 3, bffn.h[:, blk:blk + 1])
            if kind == "g":
                t = nxt(cvc, "cvc")
                k.op("act", lambda e: e.activation(out=t.h[:, 0:NT], in_=ca.h[:, 0:NT], func=AF.Silu),
                     R=ca.B(), W=t.B())
                stt["g"] = t
            else:
                t = stt["g"]
                k.op("dve", lambda e: e.tensor_tensor(out=actT.h[:, i, 0:NT], in0=t.h[:, 0:NT],
                                                      in1=ca.h[:, 0:NT], op=ALU.mult),
                     R=t.B() + ca.B(), W=actT.B(i))

        blocks = []
        for i in range(86):
            blocks += [(i * 128, 128, ("g", i)), (D_FF + i * 128, 128, ("v", i))]
        proj(w_up, blocks, 32, lambda kc: hT.h[:, kc, 0:NT], lambda kc: hT.B(kc), NT, cons)

    def final_out(c, xr, dst):
        NT = c.NT
        k.op("dve", lambda e: e.memset(st1.h[:, 4:5], 0.0), W=st1.B())
        k.op("act", lambda e: e.activation(out=xnb.h[0:NT, :], in_=xr.h[0:NT, :], func=AF.Square,
                                           accum_out=st1.h[0:NT, 4:5]),
             R=xr.B() + st1.B(), W=xnb.B() + st1.B())
        k.op("act", lambda e: e.activation(out=st1.h[0:NT, 5:6], in_=st1.h[0:NT, 4:5], func=AF.Ln,
                                           scale=1.0 / D, bias=epsb.h[0:NT, 0:1]), R=st1.B() + epsb.B(), W=st1.B())
        k.op("act", lambda e: e.activation(out=st1.h[0:NT, 6:7], in_=st1.h[0:NT, 5:6], func=AF.Exp,
                                           scale=-0.5), R=st1.B(), W=st1.B())
        k.dma("sp", gfin.h[:], gfin_d.partition_broadcast(128), W=gfin.B())
        k.op("dve", lambda e: e.scalar_tensor_tensor(out=xr.h[0:NT, :], in0=xr.h[0:NT, :],
                                                     scalar=st1.h[0:NT, 6:7], in1=gfin.h[0:NT, :],
                                                     op0=ALU.mult, op1=ALU.mult),
             R=xr.B() + st1.B() + gfin.B(), W=xr.B())
        k.dma("sp", dst, xr.h[0:NT, :], R=xr.B())

    for tl in (tl_sc, tl_ssd, tl_ffn):
        k.op("dve", lambda e, tl=tl: e.memset(tl.h[:], 0.0), W=tl.B())
    k.op("dve", lambda e: e.memset(S.h[:], 0.0), W=S.B())

    cP = mk_cfg(False)
    cS = mk_cfg(True)

    def full_tile(c, src, dst, ssd_fn):
        xr = nxt(xres, "xres")
        k.dma("sp", xr.h[0:c.NT, :], src, W=xr.B())
        norm_to_hT(c, xr, gm1, SH1)
        in_proj(c, False)
        ssd_fn()
        proj_residual(c, w_out, 48, ymT, GT1, xr)
        norm_to_hT(c, xr, gm2, SH2)
        k.barrier()
        ffn_up(c)
        proj_residual(c, w_down, 86, actT, GT2, xr)
        k.barrier()
        final_out(c, xr, dst)

    for t in range(NPRE):
        xr = nxt(xres, "xres")
        k.dma("sp", xr.h[:, :], xp[t * 128:(t + 1) * 128, :], W=xr.B())
        norm_to_hT(cP, xr, gm1, SH1)
        in_proj(cP, True)
        ssd_prompt(cP, True)

    for t in range(NMAIN):
        full_tile(cP, xm[t * 128:(t + 1) * 128, :], y_main[t * 128:(t + 1) * 128, :],
                  lambda: ssd_prompt(cP, False))
        if t == 0:
            k.op("dve", lambda e: e.tensor_scalar(out=S.h[:], in0=S.h[:], scalar1=flag.h[:, 0:1],
                                                  scalar2=None, op0=ALU.mult), R=S.B() + flag.B(), W=S.B())
            for tl in (tl_sc, tl_ssd, tl_ffn):
                k.op("dve", lambda e, tl=tl: e.tensor_scalar(out=tl.h[:], in0=tl.h[:], scalar1=flag.h[:, 0:1],
                                                             scalar2=None, op0=ALU.mult),
                     R=tl.B() + flag.B(), W=tl.B())
    k.dma("sp", o_sc_p, tl_sc.h[:, :, 0:2], R=tl_sc.B())
    k.dma("sp", o_ssd_p, tl_ssd.h[:, :, 0:3], R=tl_ssd.B())
    k.dma("sp", o_ffn_p, tl_ffn.h[:, :, 0:2], R=tl_ffn.B())
    for hp0 in range(0, 32, 4):
        pt = nxt(PSW, "w")
        for q in range(4):
            k.op("pe", lambda e, q=q: e.transpose(out=pt.h[:, q * 128:(q + 1) * 128],
                                                  in_=S.h[:, (hp0 + q) * 128:(hp0 + q + 1) * 128], identity=ident),
                 R=S.B() + cst.B(), W=pt.B(), inc=(q == 3))
        k.op("act", lambda e: e.activation(out=gall.h[:, hp0 * 128:(hp0 + 4) * 128], in_=pt.h[:, 0:512],
                                           func=AF.Copy), R=pt.B(), W=gall.B())
    k.dma("sp", o_ssm_p.rearrange("(hp q) n -> q hp n", q=128),
          gall.h[:, :].rearrange("q (hp n) -> q hp n", n=128), R=gall.B())

    if DO_SAMPLE:
        TLS["tl_sc"] = (st_sc, o_sc_s)
        TLS["tl_ssd"] = (st_ssd, o_ssd_s)
        TLS["tl_ffn"] = (st_ffn, o_ffn_s)
        full_tile(cS, xs_in, y_samp, lambda: ssd_sample(cS))
    k.finish()


def _consts():
    c = np.zeros((128, 640), np.float32)
    c[:, 0:128] = np.eye(128, dtype=np.float32)
    c[:, 128:256] = np.triu(np.ones((128, 128), np.float32))
    c[:, 256:384] = 1.0
    idx = np.arange(64)
    tt, bb = idx // 16, idx % 16
    same = (bb[:, None] == bb[None, :])
    c[0:64, 384:448] = (same & (tt[:, None] <= tt[None, :])).astype(np.float32)
    c[0:64, 448:512] = same.astype(np.float32)
    c[0:64, 512:528] = (bb[:, None] == np.arange(16)[None, :]).astype(np.float32)
    return c


def _fm(v, nblk):
    return np.ascontiguousarray(v.reshape(nblk, 128).T)


def _fmk(w, nblk):
    K = w.shape[0]
    return np.ascontiguousarray(w.T.reshape(nblk, 128, K).transpose(1, 0, 2))


def _st_in(st, nblk):
    R = st.shape[1]
    a = st.transpose(2, 1, 0).reshape(nblk, 128, R * 16)
    return np.ascontiguousarray(a.transpose(1, 0, 2))


def _st_out(a, nblk, R):
    return a.transpose(1, 0, 2).reshape(nblk * 128, R, 16).transpose(2, 1, 0)


def kernel(x_prompt, x_sample, c_prompt, c_sample, state_sc_conv, state_ssd_conv,
           state_ssm, state_ffn_conv, w_ada, b_ada, g_norm1, w_in, w_sc_conv,
           w_ssd_conv, b_ssd_conv, dt_bias, a_log, d_skip, g_ssd_norm, w_out,
           g_norm2, w_up, w_ffn_conv, b_ffn_conv, w_down, g_final):
    f = lambda a: np.ascontiguousarray(np.asarray(a, dtype=np.float32))
    x_prompt, x_sample, c_prompt, c_sample = f(x_prompt), f(x_sample), f(c_prompt), f(c_sample)
    nc = bass.Bass("TRN2", target_bir_lowering=False)
    build(nc)
    shared = {
        "w_ada": f(w_ada[0]), "b_adaT": _fm(f(b_ada[0]), 192), "g1T": _fm(f(g_norm1[0]), 32),
        "g2T": _fm(f(g_norm2[0]), 32), "gsT": _fm(f(g_ssd_norm[0]), 32), "w_in": f(w_in[0]),
        "wsc": _fmk(f(w_sc_conv[0]), 16), "wssd": _fmk(f(w_ssd_conv[0]), 48),
        "bssd": _fm(f(b_ssd_conv[0]), 48), "dtb": f(dt_bias[0]).reshape(1, 64),
        "alog": f(a_log[0]).reshape(1, 64), "dsk": f(d_skip[0]).reshape(1, 64),
        "w_out": f(w_out[0]), "w_up": f(w_up[0]), "wffn": _fmk(f(w_ffn_conv[0]), 172),
        "bffn": _fm(f(b_ffn_conv[0]), 172), "w_down": f(w_down[0]),
        "gfin": f(g_final).reshape(1, D), "consts": _consts(),
    }
    in_maps = []
    npre = max(NPRE, 1) * 128
    for c in range(NCORE):
        s, half = c // 2, c % 2
        m = dict(shared)
        xm = np.zeros((NMAIN * 128, D), np.float32)
        xp = np.zeros((npre, D), np.float32)
        if half == 0:
            xm[128:] = x_prompt[s, 0:(NMAIN - 1) * 128]
        else:
            xm[:] = x_prompt[s, 896:896 + NMAIN * 128]
            xp[:NPRE * 128] = x_prompt[s, 0:NPRE * 128]
        m["xm"], m["xp"] = xm, xp
        m["flag"] = np.full((128, 1), float(half), np.float32)
        bs = slice(16 * c, 16 * c + 16)
        m["cvec"] = np.ascontiguousarray(np.concatenate([c_prompt[s:s + 1], c_sample[bs]], 0))
        m["xs_in"] = np.ascontiguousarray(x_sample[bs].transpose(1, 0, 2).reshape(64, D))
        m["st_sc"] = _st_in(f(state_sc_conv[0, bs]), 16)
        m["st_ssd"] = _st_in(f(state_ssd_conv[0, bs]), 48)
        m["st_ffn"] = _st_in(f(state_ffn_conv[0, bs]), 172)
        m["ssm_in"] = f(state_ssm[0, bs]).reshape(16, 4096, 128)
        in_maps.append(m)
    res = run_bass_kernel_spmd(nc, in_maps, core_ids=list(range(NCORE)))
    R = res.results
    kernel.last = R
    B, L = x_prompt.shape[0], x_prompt.shape[1]
    y_prompt = np.zeros((B, L, D), np.float32)
    y_sample = np.zeros((128, 4, D), np.float32)
    p_sc = np.zeros((1, B, 2, D_SC), np.float32)
    p_ssd = np.zeros((1, B, 3, D_XBC), np.float32)
    p_ssm = np.zeros((1, B, NH, HP, NS), np.float32)
    p_ffn = np.zeros((1, B, 2, 2 * D_FF), np.float32)
    s_sc = np.zeros((1, 128, 2, D_SC), np.float32)
    s_ssd = np.zeros((1, 128, 3, D_XBC), np.float32)
    s_ssm = np.zeros((1, 128, NH, HP, NS), np.float32)
    s_ffn = np.zeros((1, 128, 2, 2 * D_FF), np.float32)
    for c in range(NCORE):
        s, half = c // 2, c % 2
        r = R[c]
        nm = (NMAIN - 1) * 128
        y_prompt[s, half * 1024:half * 1024 + nm] = r["y_main"][128:]
        bs = slice(16 * c, 16 * c + 16)
        y_sample[bs] = r["y_samp"].reshape(4, 16, D).transpose(1, 0, 2)
        if half == 1:
            p_sc[0, s] = r["o_sc_p"].transpose(1, 0, 2).reshape(D_SC, 2).T
            p_ssd[0, s] = r["o_ssd_p"].transpose(1, 0, 2).reshape(D_XBC, 3).T
            p_ffn[0, s] = r["o_ffn_p"].transpose(1, 0, 2).reshape(2 * D_FF, 2).T
            p_ssm[0, s] = r["o_ssm_p"].reshape(NH, HP, NS)
        s_sc[0, bs] = _st_out(r["o_sc_s"], 16, 2)
        s_ssd[0, bs] = _st_out(r["o_ssd_s"], 48, 3)
        s_ffn[0, bs] = _st_out(r["o_ffn_s"], 172, 2)
        s_ssm[0, bs] = r["o_ssm_s"].reshape(16, NH, HP, NS)
    return (y_prompt, y_sample, p_sc, p_ssd, p_ssm, p_ffn, s_sc, s_ssd, s_ssm, s_ffn)
```

```python
import os
import contextlib
import numpy as np
import concourse.bass as bass
import concourse.mybir as mybir
from concourse.bass_utils import run_bass_kernel_spmd

F32 = mybir.dt.float32
BF16 = mybir.dt.bfloat16
AF = mybir.ActivationFunctionType
ALU = mybir.AluOpType

D = 4096
D_SC = 2048
D_SSD = 4096
NH = 64
HP = 64
NG = 8
NS = 128
D_XBC = 6144
D_IN = 16448
D_FF = 11008
EPS = 1e-6
NCORE = 8
NPRE = int(os.environ.get("K_NPRE", "7"))
NMAIN = int(os.environ.get("K_NMAIN", "9"))
DO_SAMPLE = int(os.environ.get("K_SAMPLE", "1"))
DEBUG = int(os.environ.get("K_DEBUG", "0"))

C_SC_B, C_SC_C, C_SC_X, C_Z, C_XBC, C_DT = 0, 2048, 4096, 6144, 10240, 16384


class Buf:
    __slots__ = ("w", "r", "name")

    def __init__(self, name=""):
        self.w = None
        self.r = {}
        self.name = name


class TT:
    def __init__(self, h, name, nb=0):
        self.h = h
        self.name = name
        self.all = Buf(name)
        self.subs = [Buf(f"{name}.{i}") for i in range(nb)]

    def B(self, i=None):
        if not self.subs:
            return [self.all]
        if i is None:
            return list(self.subs)
        if isinstance(i, (list, tuple, range)):
            return [self.subs[j] for j in i]
        return [self.subs[i]]


class Eng:
    def __init__(self, name, h, sem):
        self.name, self.h, self.sem = name, h, sem
        self.count = 0
        self.waited = {}


class Slot:
    def __init__(self, sem):
        self.sem = sem
        self.count = 0


class Ker:
    def __init__(self, nc, es):
        self.nc, self.es = nc, es
        self.eng = {}
        for n, h in (("pe", nc.tensor), ("act", nc.scalar), ("dve", nc.vector),
                     ("pool", nc.gpsimd), ("sp", nc.sync)):
            self.eng[n] = Eng(n, h, es.enter_context(nc.semaphore("s_" + n)))
        self.slots = {q: [Slot(es.enter_context(nc.semaphore(f"d_{q}{i}"))) for i in range(10)]
                      for q in ("sp", "pool")}
        self.slot_i = {"sp": 0, "pool": 0}
        self.nt = 0
        self.dbg_outs = []

    def sb(self, name, shape, dt, nb=0):
        h = self.es.enter_context(self.nc.sbuf_tensor("sb_" + name, list(shape), dt))
        return TT(h, name, nb)

    def ps(self, name):
        h = self.es.enter_context(self.nc.psum_tensor(name, [128, 512], F32))
        return TT(h, name)

    def _deps(self, E, R, W):
        deps = {}
        for b in R:
            if b.w is not None:
                s, v = b.w
                if deps.get(s, (None, 0))[1] < v:
                    deps[s] = (s, v)
        for b in W:
            if b.w is not None:
                s, v = b.w
                if deps.get(s, (None, 0))[1] < v:
                    deps[s] = (s, v)
            for s, v in b.r.items():
                if deps.get(s, (None, 0))[1] < v:
                    deps[s] = (s, v)
        for s, v in deps.values():
            if s is E.sem and E.name == "pe":
                continue
            if E.waited.get(s, 0) < v:
                E.h.wait_ge(s, v)
                E.waited[s] = v

    def op(self, e, fn, R=(), W=(), inc=True):
        E = self.eng[e]
        self._deps(E, R, W)
        ins = fn(E.h)
        if inc:
            E.count += 1
            ins.then_inc(E.sem, 1)
            val = E.count
        else:
            val = E.count + 1
        for b in R:
            if b.r.get(E.sem, 0) < val:
                b.r[E.sem] = val
        for b in W:
            b.w = (E.sem, val)
            b.r = {}
        return ins

    def dma(self, q, out, in_, R=(), W=()):
        E = self.eng[q]
        self._deps(E, R, W)
        sl = self.slots[q][self.slot_i[q] % len(self.slots[q])]
        self.slot_i[q] += 1
        if sl.count and E.waited.get(sl.sem, 0) < sl.count:
            E.h.wait_ge(sl.sem, sl.count)
            E.waited[sl.sem] = sl.count
        E.h.dma_start(out=out, in_=in_).then_inc(sl.sem, 16)
        sl.count += 16
        for b in R:
            b.r[sl.sem] = sl.count
        for b in W:
            b.w = (sl.sem, sl.count)
            b.r = {}

    def barrier(self):
        for E in self.eng.values():
            for Fe in self.eng.values():
                if Fe is E or Fe.count == 0:
                    continue
                if E.waited.get(Fe.sem, 0) < Fe.count:
                    E.h.wait_ge(Fe.sem, Fe.count)
                    E.waited[Fe.sem] = Fe.count
            for q in ("sp", "pool"):
                for sl in self.slots[q]:
                    if sl.count and E.waited.get(sl.sem, 0) < sl.count:
                        E.h.wait_ge(sl.sem, sl.count)
                        E.waited[sl.sem] = sl.count

    def finish(self):
        E = self.eng["sp"]
        for q in ("sp", "pool"):
            for sl in self.slots[q]:
                if sl.count and E.waited.get(sl.sem, 0) < sl.count:
                    E.h.wait_ge(sl.sem, sl.count)
                    E.waited[sl.sem] = sl.count

    def dbg(self, name, tt, ap, shape, dt=F32):
        if not DEBUG:
            return
        d = self.nc.dram_tensor("dbg_" + name, list(shape), dt, kind="ExternalOutput").ap()
        self.dma("sp", d, ap, R=tt.B())
        self.dbg_outs.append("dbg_" + name)


def build(nc):
    es = contextlib.ExitStack()
    with es:
        _build(nc, es)
    return nc


def _build(nc, es):
    k = Ker(nc, es)

    def din(name, shape):
        return nc.dram_tensor(name, list(shape), F32, kind="ExternalInput").ap()

    def dout(name, shape):
        return nc.dram_tensor(name, list(shape), F32, kind="ExternalOutput").ap()

    NTOK_M = NMAIN * 128
    xm = din("xm", [NTOK_M, D])
    xp = din("xp", [max(NPRE, 1) * 128, D])
    flag_d = din("flag", [128, 1])
    cvec = din("cvec", [17, D])
    xs_in = din("xs_in", [64, D])
    st_sc = din("st_sc", [128, 16, 32])
    st_ssd = din("st_ssd", [128, 48, 48])
    st_ffn = din("st_ffn", [128, 172, 32])
    ssm_in = din("ssm_in", [16, 4096, 128])
    w_ada = din("w_ada", [D, 6 * D])
    b_adaT = din("b_adaT", [128, 192])
    g1T_d = din("g1T", [128, 32])
    g2T_d = din("g2T", [128, 32])
    gsT_d = din("gsT", [128, 32])
    w_in = din("w_in", [D, D_IN])
    wsc_d = din("wsc", [128, 16, 3])
    wssd_d = din("wssd", [128, 48, 4])
    bssd_d = din("bssd", [128, 48])
    dtb_d = din("dtb", [1, 64])
    alog_d = din("alog", [1, 64])
    dsk_d = din("dsk", [1, 64])
    w_out = din("w_out", [D_SC + D_SSD, D])
    w_up = din("w_up", [D, 2 * D_FF])
    wffn_d = din("wffn", [128, 172, 3])
    bffn_d = din("bffn", [128, 172])
    w_down = din("w_down", [D_FF, D])
    gfin_d = din("gfin", [1, D])
    consts_d = din("consts", [128, 640])

    y_main = dout("y_main", [NTOK_M, D])
    y_samp = dout("y_samp", [64, D])
    o_sc_p = dout("o_sc_p", [128, 16, 2])
    o_ssd_p = dout("o_ssd_p", [128, 48, 3])
    o_ffn_p = dout("o_ffn_p", [128, 172, 2])
    o_ssm_p = dout("o_ssm_p", [4096, 128])
    o_sc_s = dout("o_sc_s", [128, 16, 32])
    o_ssd_s = dout("o_ssd_s", [128, 48, 48])
    o_ffn_s = dout("o_ffn_s", [128, 172, 32])
    o_ssm_s = dout("o_ssm_s", [16, 4096, 128])

    cst = k.sb("cst", [128, 640], F32)
    identb = k.sb("identb", [128, 128], BF16)
    flag = k.sb("flag", [128, 1], F32)
    modT = k.sb("modT", [128, 192, 17], F32)
    gm1 = k.sb("gm1", [128, 32, 17], F32)
    gm2 = k.sb("gm2", [128, 32, 17], F32)
    g1T = k.sb("g1T", [128, 32], F32)
    g2T = k.sb("g2T", [128, 32], F32)
    gsT = k.sb("gsT", [128, 32], F32)
    badaT = k.sb("badaT", [128, 192], F32)
    wsc = k.sb("wsc", [128, 16, 3], F32)
    wssd = k.sb("wssd", [128, 48, 4], F32)
    bssd = k.sb("bssd", [128, 48], F32)
    wffn = k.sb("wffn", [128, 172, 3], F32)
    bffn = k.sb("bffn", [128, 172], F32)
    dtb = k.sb("dtb", [128, 64], F32)
    Abc = k.sb("Abc", [128, 64], F32)
    Dbc = k.sb("Dbc", [128, 64], F32)
    cT = k.sb("cT", [128, 32, 17], BF16)

    NWB = 5
    KP = 16
    wst = [k.sb(f"wst{i}", [128, KP, 128], BF16) for i in range(NWB)]
    xres = [k.sb(f"xres{i}", [128, D], F32) for i in range(2)]
    xnb = k.sb("xnb", [128, D], BF16)
    hTs = [k.sb(f"hT{i}", [128, 32, 128], BF16, nb=32) for i in range(2)]
    hT = hTs[0]
    arena = k.sb("arena", [128, 86, 128], BF16)
    actT = TT(arena.h[:, :, :], "actT", nb=86)
    ymT = TT(arena.h[:, 0:48, :], "ymT", nb=48)
    BT = TT(arena.h[:, 48:56, :], "BT", nb=8)
    CT = TT(arena.h[:, 56:64, :], "CT", nb=8)
    B_tok = TT(arena.h[:, 64:72, :].rearrange("p a b -> p (a b)"), "B_tok")
    arena2 = k.sb("arena2", [128, 96, 128], BF16)
    actT1 = TT(arena2.h[:, 0:86, :], "actT1", nb=86)
    xs_tok = TT(arena2.h[:, 0:32, :].rearrange("p a b -> p (a b)"), "xs_tok")
    sz_tok = TT(arena2.h[:, 32:64, :].rearrange("p a b -> p (a b)"), "sz_tok")
    actTs = [actT, actT1]
    S = k.sb("S", [128, D], F32, nb=0)
    sbg = [k.sb(f"sbg{i}", [128, 512], BF16) for i in range(2)]
    tl_sc = k.sb("tl_sc", [128, 16, 2], F32, nb=16)
    tl_ssd = k.sb("tl_ssd", [128, 48, 3], F32, nb=48)
    tl_ffn = k.sb("tl_ffn", [128, 172, 2], F32, nb=172)
    cvb = [k.sb(f"cvb{i}", [128, 128 + 48], F32) for i in range(3)]
    cva = [k.sb(f"cva{i}", [128, 128], F32) for i in range(3)]
    cvc = [k.sb(f"cvc{i}", [128, 128], F32) for i in range(2)]
    cvh = [k.sb(f"cvh{i}", [128, 128], BF16) for i in range(3)]
    mixg = [k.sb(f"mixg{i}", [128, 4, 128], F32) for i in range(2)]
    sm = {n: k.sb("sm_" + n, [128, 64], F32) for n in
          ("v", "av", "e", "l", "dt", "a", "ac", "dte", "w1", "eac", "dec", "tot")}
    epsb = k.sb("epsb", [128, 1], F32)
    k.op("dve", lambda e: e.memset(epsb.h[:], EPS), W=epsb.B())
    st1 = k.sb("st1", [128, 8], F32)
    st2 = k.sb("st2", [128, 8], F32)
    xdt = xnb
    xdte = TT(arena2.h[:, 64:96, :].rearrange("p a b -> p (a b)"), "xdte")
    xsD = xs_tok
    cbm = k.sb("cbm", [128, 8, 128], BF16)
    sgw = [k.sb(f"sgw{i}", [128, 512], F32) for i in range(2)]
    MT = [k.sb(f"MT{i}", [128, 4, 128], BF16) for i in range(2)]
    gall = k.sb("gall", [128, D], F32)
    gnb = xs_tok
    gfin = gall
    yw = [k.sb(f"yw{i}", [128, 512], F32) for i in range(1)]

    PSP = [k.ps(f"psp{i}") for i in range(4)]
    PSW = [k.ps(f"psw{i}") for i in range(4)]
    ring = {"p": 0, "w": 0, "wst": 0, "cvb": 0, "cva": 0, "cvc": 0, "cvh": 0, "mixg": 0,
            "sgw": 0, "sgx": 0, "MT": 0, "yw": 0, "xres": 0, "sbg": 0}

    def nxt(lst, key):
        t = lst[ring[key] % len(lst)]
        ring[key] += 1
        return t

    ident = cst.h[:, 0:128]
    triU = cst.h[:, 128:256]
    ones = cst.h[:, 256:384]
    triS = cst.h[0:64, 384:448]
    onesS = cst.h[0:64, 448:512]
    rowmask = cst.h[0:64, 512:528]

    def ld(tt, src):
        k.dma("sp", tt.h[:], src, W=tt.B())

    ld(cst, consts_d)
    ld(flag, flag_d)
    ld(g1T, g1T_d); ld(g2T, g2T_d); ld(gsT, gsT_d); ld(badaT, b_adaT)
    ld(wsc, wsc_d); ld(wssd, wssd_d); ld(bssd, bssd_d); ld(wffn, wffn_d); ld(bffn, bffn_d)
    k.dma("sp", dtb.h[:], dtb_d.partition_broadcast(128), W=dtb.B())
    k.dma("sp", Abc.h[:], alog_d.partition_broadcast(128), W=Abc.B())
    k.dma("sp", Dbc.h[:], dsk_d.partition_broadcast(128), W=Dbc.B())
    k.op("dve", lambda e: e.tensor_copy(out=identb.h[:], in_=ident), R=cst.B(), W=identb.B())
    k.op("act", lambda e: e.activation(out=Abc.h[:], in_=Abc.h[:], func=AF.Exp), R=Abc.B(), W=Abc.B())
    k.op("dve", lambda e: e.tensor_scalar(out=Abc.h[:], in0=Abc.h[:], scalar1=-1.0, scalar2=None,
                                          op0=ALU.mult), R=Abc.B(), W=Abc.B())

    scr_map = {}
    SCR = {}
    for nm, nblk in (("w_in", 129 * 2), ("w_up", 172 * 2), ("w_out", 32 * 3), ("w_down", 32 * 6)):
        SCR[nm] = {"n": 0, "t": nc.dram_tensor("scr_" + nm, [nblk, 128, KP * 128], BF16).ap()}

    def proj(Wd, blocks, kcs, ctxs):
        npart = -(-kcs // KP)
        bnd = [round(i * kcs / npart) for i in range(npart + 1)]
        kparts = [(bnd[i], bnd[i + 1]) for i in range(npart)]
        jobs = [(bi, kp) for bi in range(len(blocks)) for kp in kparts]
        loaded = {}
        pend_wr = {}
        wname = Wd.tensor.name
        use_scr = wname in SCR

        def issue(j):
            bi, (k0, k1) = jobs[j]
            c0, ncol, _ = blocks[bi]
            t = nxt(wst, "wst")
            key = (wname, c0, k0)
            if key in scr_map:
                sbuf_, ap = scr_map[key]
                k.dma("sp", t.h[:, 0:k1 - k0, 0:ncol], ap, R=[sbuf_], W=t.B())
            else:
                src = Wd[k0 * 128:k1 * 128, c0:c0 + ncol].rearrange("(c p) n -> p c n", p=128)
                k.dma("pool", t.h[:, 0:k1 - k0, 0:ncol], src, W=t.B())
                if use_scr:
                    st_ = SCR[wname]
                    idx = st_["n"]
                    st_["n"] += 1
                    ap = st_["t"][idx].rearrange("p (c n) -> p c n", n=128)[:, 0:k1 - k0, 0:ncol]
                    sbuf_ = Buf("scr")
                    scr_map[key] = (sbuf_, ap)
                    pend_wr[j] = (ap, t.h[:, 0:k1 - k0, 0:ncol], sbuf_)
            loaded[j] = t

        PF = NWB - 1
        for j in range(min(PF, len(jobs))):
            issue(j)
        cur = None
        for j, (bi, (k0, k1)) in enumerate(jobs):
            c0, ncol, tag = blocks[bi]
            t = loaded.pop(j)
            if k0 == 0:
                cur = [nxt(PSP, "p") for _ in ctxs]
            for ci, (rhs_fn, rhs_bufs_fn, NT, consumer) in enumerate(ctxs):
                for kc in range(k0, k1):
                    last = (kc == kcs - 1)
                    k.op("pe", lambda e, kc=kc, t=t, last=last, ci=ci, NT=NT, rhs_fn=rhs_fn: e.matmul(
                        cur[ci].h[0:ncol, 0:NT], lhsT=t.h[:, kc - k0, 0:ncol], rhs=rhs_fn(kc),
                        start=(kc == 0), stop=last),
                        R=t.B() + rhs_bufs_fn(kc), W=cur[ci].B(), inc=(kc == k1 - 1))
            if j in pend_wr:
                ap_, src_, sbuf_ = pend_wr.pop(j)
                k.dma("pool", ap_, src_, R=t.B(), W=[sbuf_])
            if j + PF < len(jobs):
                issue(j + PF)
            if k1 == kcs:
                for ci, (rhs_fn, rhs_bufs_fn, NT, consumer) in enumerate(ctxs):
                    consumer(tag, cur[ci], cur[ci].h[0:ncol, 0:NT])

    cv = nxt(xres, "xres")
    k.dma("sp", cv.h[0:17, :], cvec, W=cv.B())
    k.op("act", lambda e: e.activation(out=xnb.h[0:17, :], in_=cv.h[0:17, :], func=AF.Silu),
         R=cv.B(), W=xnb.B())
    for j0 in range(0, 32, 8):
        pt = nxt(PSW, "w")
        pv = pt.h[:].bitcast(BF16)
        for j in range(8):
            k.op("pe", lambda e, j=j: e.transpose(out=pv[:, j * 18:j * 18 + 17],
                                                  in_=xnb.h[0:17, (j0 + j) * 128:(j0 + j + 1) * 128],
                                                  identity=identb.h[0:17, 0:17]),
                 R=xnb.B() + identb.B(), W=pt.B(), inc=(j == 7))
        k.op("dve", lambda e: e.tensor_copy(
            out=cT.h[:, j0:j0 + 8, :], in_=pv[:, 0:8 * 18].rearrange("p (j r) -> p j r", r=18)[:, :, 0:17]),
            R=pt.B(), W=cT.B())

    def ada_cons(tag, pt, pap):
        k.op("act", lambda e: e.activation(out=modT.h[:, tag, :], in_=pap, func=AF.Identity,
                                           bias=badaT.h[:, tag:tag + 1], scale=1.0),
             R=pt.B() + badaT.B(), W=modT.B())

    proj(w_ada, [(i * 128, 128, i) for i in range(192)], 32,
         [(lambda kc: cT.h[:, kc, :], lambda kc: cT.B(), 17, ada_cons)])
    for (gm, gT, off) in ((gm1, g1T, 32), (gm2, g2T, 128)):
        k.op("dve", lambda e, gm=gm, off=off: e.tensor_scalar(
            out=gm.h[:], in0=modT.h[:, off:off + 32, :], scalar1=1.0, scalar2=None, op0=ALU.add),
            R=modT.B(), W=gm.B())
        k.op("dve", lambda e, gm=gm, gT=gT: e.tensor_tensor(
            out=gm.h[:], in0=gm.h[:], in1=gT.h[:].unsqueeze(2).to_broadcast([128, 32, 17]),
            op=ALU.mult), R=gm.B() + gT.B(), W=gm.B())
    SH1, GT1, SH2, GT2 = 0, 64, 96, 160

    class Cfg:
        pass

    def mk_cfg(sample):
        c = Cfg()
        c.sample = sample
        c.NT = 64 if sample else 128
        c.shift = 16 if sample else 1

        def mod_ap(tt, off, j0, n):
            if sample:
                return tt.h[:, off + j0:off + j0 + n, 1:17].unsqueeze(2).to_broadcast([128, n, 4, 16])
            return tt.h[:, off + j0:off + j0 + n, 0:1].to_broadcast([128, n, 128])

        def view(ap):
            if sample:
                return ap.rearrange("p j (t b) -> p j t b", b=16)
            return ap
        c.mod_ap, c.view = mod_ap, view
        return c

    def norm_to_hT(c, xr, gm, shoff, hT):
        NT = c.NT
        k.op("dve", lambda e: e.memset(st1.h[:, 0:1], 0.0), W=st1.B())
        k.op("act", lambda e: e.activation(out=xnb.h[0:NT, :], in_=xr.h[0:NT, :], func=AF.Square,
                                           accum_out=st1.h[0:NT, 0:1]),
             R=xr.B() + st1.B(), W=xnb.B() + st1.B())
        k.op("act", lambda e: e.activation(out=st1.h[0:NT, 1:2], in_=st1.h[0:NT, 0:1], func=AF.Ln,
                                           scale=1.0 / D, bias=epsb.h[0:NT, 0:1]), R=st1.B() + epsb.B(), W=st1.B())
        k.op("act", lambda e: e.activation(out=st1.h[0:NT, 2:3], in_=st1.h[0:NT, 1:2], func=AF.Exp,
                                           scale=-0.5), R=st1.B(), W=st1.B())
        k.op("act", lambda e: e.activation(out=xnb.h[0:NT, :], in_=xr.h[0:NT, :], func=AF.Copy,
                                           scale=st1.h[0:NT, 2:3]),
             R=xr.B() + st1.B(), W=xnb.B())
        for j0 in range(0, 32, 4):
            pt = nxt(PSW, "w")
            pv = pt.h[:].bitcast(BF16)
            for j in range(4):
                k.op("pe", lambda e, j=j: e.transpose(
                    out=pv[:, j * NT:(j + 1) * NT], in_=xnb.h[0:NT, (j0 + j) * 128:(j0 + j + 1) * 128],
                    identity=identb.h[0:NT, 0:NT]), R=xnb.B() + identb.B(), W=pt.B(), inc=(j == 3))
            src = pv[:, 0:4 * NT].rearrange("p (j t) -> p j t", t=NT)
            dst = hT.h[:, j0:j0 + 4, 0:NT]
            k.op("dve", lambda e: e.tensor_tensor(out=c.view(dst), in0=c.view(src),
                                                  in1=c.mod_ap(gm, 0, j0, 4), op=ALU.mult),
                 R=pt.B() + gm.B(), W=hT.B(range(j0, j0 + 4)))
            k.op("dve", lambda e: e.tensor_tensor(out=c.view(dst), in0=c.view(dst),
                                                  in1=c.mod_ap(modT, shoff, j0, 4), op=ALU.add),
                 R=hT.B(range(j0, j0 + 4)) + modT.B(), W=hT.B(range(j0, j0 + 4)))

    TLS = {}

    def tail_load(c, cb, TL, tl, blk):
        if c.sample:
            k.dma("sp", cb.h[:, 0:TL], TLS[tl.name][0][:, blk, :], W=cb.B())
        else:
            k.op("dve", lambda e: e.tensor_copy(out=cb.h[:, 0:TL], in_=tl.h[:, blk, 0:TL]),
                 R=tl.B(blk), W=cb.B())

    def tail_store(c, cb, TL, tl, blk):
        NT = c.NT
        if c.sample:
            k.dma("sp", TLS[tl.name][1][:, blk, :], cb.h[:, NT:NT + TL], R=cb.B())
        else:
            if getattr(c, "halo", False):
                k.op("dve", lambda e: e.tensor_scalar(out=tl.h[:, blk, 0:TL], in0=cb.h[:, NT:NT + TL],
                                                      scalar1=flag.h[:, 0:1], scalar2=None, op0=ALU.mult),
                     R=cb.B() + flag.B(), W=tl.B(blk))
            else:
                k.op("dve", lambda e: e.tensor_copy(out=tl.h[:, blk, 0:TL], in_=cb.h[:, NT:NT + TL]),
                     R=cb.B(), W=tl.B(blk))

    def conv_block(c, pt, pap, tl, blk, wt, Kw, bias_ap):
        NT, sh = c.NT, c.shift
        TL = (Kw - 1) * sh
        cb = nxt(cvb, "cvb")
        ca = nxt(cva, "cva")
        k.op("act", lambda e: e.activation(out=cb.h[:, TL:TL + NT], in_=pap, func=AF.Copy),
             R=pt.B(), W=cb.B())
        tail_load(c, cb, TL, tl, blk)
        if bias_ap is not None:
            k.op("dve", lambda e: e.tensor_scalar(out=ca.h[:, 0:NT], in0=cb.h[:, TL:TL + NT],
                                                  scalar1=wt.h[:, blk, Kw - 1:Kw], scalar2=bias_ap,
                                                  op0=ALU.mult, op1=ALU.add),
                 R=cb.B() + wt.B(), W=ca.B())
        else:
            k.op("dve", lambda e: e.tensor_scalar(out=ca.h[:, 0:NT], in0=cb.h[:, TL:TL + NT],
                                                  scalar1=wt.h[:, blk, Kw - 1:Kw], scalar2=None,
                                                  op0=ALU.mult), R=cb.B() + wt.B(), W=ca.B())
        for kk in range(Kw - 1):
            o = kk * sh
            k.op("dve", lambda e, kk=kk, o=o: e.scalar_tensor_tensor(
                out=ca.h[:, 0:NT], in0=cb.h[:, o:o + NT], scalar=wt.h[:, blk, kk:kk + 1],
                in1=ca.h[:, 0:NT], op0=ALU.mult, op1=ALU.add), R=cb.B() + wt.B() + ca.B(), W=ca.B())
        tail_store(c, cb, TL, tl, blk)
        return ca

    tq = {"pt": None, "n": 0, "dst": None, "c0": 0}

    def tr_push(c, src_tt, src_ap, dst_tt, col0):
        NT = c.NT
        if tq["n"] == 0:
            tq["pt"] = nxt(PSW, "w")
            tq["dst"], tq["c0"] = dst_tt, col0
        pt = tq["pt"]
        pv = pt.h[:].bitcast(BF16)
        n = tq["n"]
        k.op("pe", lambda e: e.transpose(out=pv[0:NT, n * 128:(n + 1) * 128], in_=src_ap,
                                         identity=identb.h[:, :]),
             R=src_tt.B() + identb.B(), W=pt.B())
        tq["n"] += 1
        if tq["n"] == 4:
            tr_flush(c)

    def tr_flush(c):
        if tq["n"] == 0:
            return
        NT = c.NT
        pt, n, dst, c0 = tq["pt"], tq["n"], tq["dst"], tq["c0"]
        pv = pt.h[:].bitcast(BF16)
        k.op("act", lambda e: e.activation(out=dst.h[0:NT, c0:c0 + n * 128], in_=pv[0:NT, 0:n * 128],
                                           func=AF.Copy), R=pt.B(), W=dst.B())
        tq["n"] = 0

    def in_proj(c, prefix, hT):
        NT = c.NT
        st = {}

        def cons(tag, pt, pap):
            kind, i = tag
            if kind == "c":
                t = nxt(cvc, "cvc")
                k.op("act", lambda e: e.activation(out=t.h[:, 0:NT], in_=pap, func=AF.Copy),
                     R=pt.B(), W=t.B())
                st["c"] = t
            elif kind == "x":
                t = st["c"]
                cb = nxt(cvb, "cvb")
                ca = nxt(cva, "cva")
                TL = 2 * c.shift
                k.op("dve", lambda e: e.tensor_tensor(out=cb.h[:, TL:TL + NT], in0=t.h[:, 0:NT],
                                                      in1=pap, op=ALU.mult), R=t.B() + pt.B(), W=cb.B())
                tail_load(c, cb, TL, tl_sc, i)
                k.op("dve", lambda e: e.tensor_scalar(out=ca.h[:, 0:NT], in0=cb.h[:, TL:TL + NT],
                                                      scalar1=wsc.h[:, i, 2:3], scalar2=None,
                                                      op0=ALU.mult), R=cb.B() + wsc.B(), W=ca.B())
                for kk in range(2):
                    o = kk * c.shift
                    k.op("dve", lambda e, kk=kk, o=o: e.scalar_tensor_tensor(
                        out=ca.h[:, 0:NT], in0=cb.h[:, o:o + NT], scalar=wsc.h[:, i, kk:kk + 1],
                        in1=ca.h[:, 0:NT], op0=ALU.mult, op1=ALU.add),
                        R=cb.B() + wsc.B() + ca.B(), W=ca.B())
                tail_store(c, cb, TL, tl_sc, i)
                st["uc"] = ca
            elif kind == "b":
                ca = st["uc"]
                k.op("dve", lambda e: e.tensor_tensor(out=ymT.h[:, i, 0:NT], in0=ca.h[:, 0:NT],
                                                      in1=pap, op=ALU.mult),
                     R=ca.B() + pt.B(), W=ymT.B(i))
            elif kind == "z":
                hb = nxt(cvh, "cvh")
                k.op("act", lambda e: e.activation(out=hb.h[:, 0:NT], in_=pap, func=AF.Silu),
                     R=pt.B(), W=hb.B())
                tr_push(c, hb, hb.h[:, 0:NT], sz_tok, i * 128)
            elif kind == "xbc":
                ca = conv_block(c, pt, pap, tl_ssd, i, wssd, 4, bssd.h[:, i:i + 1])
                if i < 32:
                    hb = nxt(cvh, "cvh")
                    k.op("act", lambda e: e.activation(out=hb.h[:, 0:NT], in_=ca.h[:, 0:NT],
                                                       func=AF.Silu), R=ca.B(), W=hb.B())
                    tr_push(c, hb, hb.h[:, 0:NT], xs_tok, i * 128)
                elif i < 40:
                    g = i - 32
                    k.op("act", lambda e: e.activation(out=BT.h[:, g, 0:NT], in_=ca.h[:, 0:NT],
                                                       func=AF.Silu), R=ca.B(), W=BT.B(g))
                    tr_push(c, BT, BT.h[:, g, 0:NT], B_tok, g * 128)
                else:
                    g = i - 40
                    k.op("act", lambda e: e.activation(out=CT.h[:, g, 0:NT], in_=ca.h[:, 0:NT],
                                                       func=AF.Silu), R=ca.B(), W=CT.B(g))

        blocks = []
        if not prefix:
            for i in range(16):
                blocks += [(C_SC_C + i * 128, 128, ("c", i)), (C_SC_X + i * 128, 128, ("x", i)),
                           (C_SC_B + i * 128, 128, ("b", i))]
            blocks += [(C_Z + i * 128, 128, ("z", i)) for i in range(32)]
        blocks += [(C_XBC + i * 128, 128, ("xbc", i)) for i in range(48)]
        proj(w_in, blocks, 32, [(lambda kc: hT.h[:, kc, 0:NT], lambda kc: hT.B(kc), NT, cons)])
        tr_flush(c)
        t = nxt(wst, "wst")
        tdt = t.h[:, :, :].rearrange("p a b -> p (a b)").rearrange("p (c n) -> p c n", n=64)
        k.dma("pool", tdt, w_in[:, C_DT:C_DT + 64].rearrange("(c p) n -> p c n", p=128), W=t.B())
        pt = nxt(PSW, "w")
        for kc in range(32):
            k.op("pe", lambda e, kc=kc: e.matmul(pt.h[0:NT, 0:64], lhsT=hT.h[:, kc, 0:NT],
                                                 rhs=tdt[:, kc, :], start=(kc == 0), stop=(kc == 31)),
                 R=t.B() + hT.B(kc), W=pt.B(), inc=(kc == 31))
        v, av, ee, ll, dt_, a_ = (sm[n] for n in ("v", "av", "e", "l", "dt", "a"))
        k.op("dve", lambda e: e.tensor_tensor(out=v.h[0:NT, :], in0=pt.h[0:NT, 0:64], in1=dtb.h[0:NT, :],
                                              op=ALU.add), R=pt.B() + dtb.B(), W=v.B())
        k.op("act", lambda e: e.activation(out=av.h[0:NT, :], in_=v.h[0:NT, :], func=AF.Abs),
             R=v.B(), W=av.B())
        k.op("act", lambda e: e.activation(out=ee.h[0:NT, :], in_=av.h[0:NT, :], func=AF.Exp, scale=-1.0),
             R=av.B(), W=ee.B())
        k.op("act", lambda e: e.activation(out=ll.h[0:NT, :], in_=ee.h[0:NT, :], func=AF.Ln, bias=1.0),
             R=ee.B(), W=ll.B())
        k.op("dve", lambda e: e.scalar_tensor_tensor(out=dt_.h[0:NT, :], in0=v.h[0:NT, :], scalar=0.0,
                                                     in1=ll.h[0:NT, :], op0=ALU.max, op1=ALU.add),
             R=v.B() + ll.B(), W=dt_.B())
        k.op("dve", lambda e: e.tensor_tensor(out=a_.h[0:NT, :], in0=dt_.h[0:NT, :], in1=Abc.h[0:NT, :],
                                              op=ALU.mult), R=dt_.B() + Abc.B(), W=a_.B())

    def hb3(ap64, NT):
        return ap64.unsqueeze(2).to_broadcast([NT, 64, 64])

    def v3(ap, NT):
        return ap.rearrange("t (h p) -> t h p", p=64)

    def ssd_decays(c):
        NT = c.NT
        tri_, ones_ = (triS, onesS) if c.sample else (triU, ones)
        a_, ac, dte, w1, eac, dec, dt_ = (sm[n] for n in ("a", "ac", "dte", "w1", "eac", "dec", "dt"))
        p1 = nxt(PSW, "w")
        k.op("pe", lambda e: e.matmul(p1.h[0:NT, 0:64], lhsT=tri_, rhs=a_.h[0:NT, :], start=True, stop=True),
             R=cst.B() + a_.B(), W=p1.B())
        k.op("pe", lambda e: e.matmul(p1.h[0:NT, 64:128], lhsT=ones_, rhs=a_.h[0:NT, :], start=True, stop=True),
             R=cst.B() + a_.B(), W=p1.B())
        k.op("act", lambda e: e.activation(out=ac.h[0:NT, :], in_=p1.h[0:NT, 0:64], func=AF.Copy),
             R=p1.B(), W=ac.B())
        k.op("act", lambda e: e.activation(out=eac.h[0:NT, :], in_=p1.h[0:NT, 0:64], func=AF.Exp),
             R=p1.B(), W=eac.B())
        k.op("act", lambda e: e.activation(out=dec.h[0:NT, :], in_=p1.h[0:NT, 64:128], func=AF.Exp),
             R=p1.B(), W=dec.B())
        k.op("dve", lambda e: e.tensor_tensor(out=dte.h[0:NT, :], in0=p1.h[0:NT, 64:128], in1=ac.h[0:NT, :],
                                              op=ALU.subtract), R=p1.B() + ac.B(), W=dte.B())
        k.op("act", lambda e: e.activation(out=dte.h[0:NT, :], in_=dte.h[0:NT, :], func=AF.Exp),
             R=dte.B(), W=dte.B())
        k.op("dve", lambda e: e.tensor_tensor(out=w1.h[0:NT, :], in0=dte.h[0:NT, :], in1=dt_.h[0:NT, :],
                                              op=ALU.mult), R=dte.B() + dt_.B(), W=w1.B())
        k.op("dve", lambda e: e.tensor_tensor(out=v3(xdte.h[0:NT, :], NT), in0=v3(xs_tok.h[0:NT, :], NT),
                                              in1=hb3(w1.h[0:NT, :], NT), op=ALU.mult),
             R=xs_tok.B() + w1.B(), W=xdte.B())

    def ssd_intra_group_prep(c):
        NT = c.NT
        tri_ = triS if c.sample else triU
        dt_ = sm["dt"]
        k.op("dve", lambda e: e.tensor_tensor(out=v3(xdt.h[0:NT, :], NT), in0=v3(xs_tok.h[0:NT, :], NT),
                                              in1=hb3(dt_.h[0:NT, :], NT), op=ALU.mult),
             R=xs_tok.B() + dt_.B(), W=xdt.B())
        k.op("dve", lambda e: e.tensor_tensor(out=v3(xsD.h[0:NT, :], NT), in0=v3(xs_tok.h[0:NT, :], NT),
                                              in1=hb3(Dbc.h[0:NT, :], NT), op=ALU.mult),
             R=xs_tok.B() + Dbc.B(), W=xsD.B())
        for g0 in (0, 4):
            pc = nxt(PSW, "w")
            for g in range(g0, g0 + 4):
                k.op("pe", lambda e, g=g: e.matmul(pc.h[0:NT, (g - g0) * NT:(g - g0 + 1) * NT],
                                                   lhsT=BT.h[:, g, 0:NT], rhs=CT.h[:, g, 0:NT],
                                                   start=True, stop=True),
                     R=BT.B(g) + CT.B(g), W=pc.B(), inc=(g == g0 + 3))
            k.op("dve", lambda e: e.tensor_tensor(
                out=cbm.h[0:NT, g0:g0 + 4, 0:NT],
                in0=pc.h[0:NT, 0:4 * NT].rearrange("p (g t) -> p g t", t=NT),
                in1=tri_.unsqueeze(1).to_broadcast([NT, 4, NT]), op=ALU.mult),
                R=pc.B() + cst.B(), W=cbm.B())

    def ssd_intra_group(c, g, pd):
        NT = c.NT
        tri_ = triS if c.sample else triU
        a_, ac = sm["a"], sm["ac"]
        k.op("pe", lambda e: e.matmul(pd.h[0:NT, 0:512], lhsT=identb.h[0:NT, 0:NT],
                                      rhs=xsD.h[0:NT, g * 512:(g + 1) * 512], start=True, stop=False),
             R=identb.B() + xsD.B(), W=pd.B(), inc=False)
        for hh in (0, 4):
            h0 = g * 8 + hh
            pa = nxt(PSW, "w")
            for q in range(4):
                k.op("pe", lambda e, q=q: e.matmul(
                    pa.h[0:NT, q * NT:(q + 1) * NT],
                    lhsT=a_.h[0:NT, h0 + q:h0 + q + 1].to_broadcast([NT, NT]), rhs=tri_,
                    start=True, stop=True), R=a_.B() + cst.B(), W=pa.B(), inc=(q == 3))
            w = nxt(sgw, "sgw")
            wv = w.h[0:NT, 0:4 * NT].rearrange("p (q t) -> p q t", t=NT)
            k.op("dve", lambda e: e.tensor_tensor(
                out=wv, in0=pa.h[0:NT, 0:4 * NT].rearrange("p (q t) -> p q t", t=NT),
                in1=ac.h[0:NT, h0:h0 + 4].unsqueeze(2).to_broadcast([NT, 4, NT]), op=ALU.subtract),
                R=pa.B() + ac.B(), W=w.B())
            k.op("dve", lambda e: e.tensor_scalar(out=w.h[0:NT, 0:4 * NT], in0=w.h[0:NT, 0:4 * NT],
                                                  scalar1=0.0, scalar2=None, op0=ALU.min),
                 R=w.B(), W=w.B())
            k.op("act", lambda e: e.activation(out=w.h[0:NT, 0:4 * NT], in_=w.h[0:NT, 0:4 * NT],
                                               func=AF.Exp), R=w.B(), W=w.B())
            m = nxt(MT, "MT")
            k.op("dve", lambda e: e.tensor_tensor(
                out=m.h[0:NT, :, 0:NT], in0=wv,
                in1=cbm.h[0:NT, g:g + 1, 0:NT].to_broadcast([NT, 4, NT]), op=ALU.mult),
                R=w.B() + cbm.B(), W=m.B())
            for q in range(4):
                hq = hh + q
                last = (hq == 7)
                k.op("pe", lambda e, q=q, hq=hq, last=last: e.matmul(
                    pd.h[0:NT, hq * 64:(hq + 1) * 64], lhsT=m.h[0:NT, q, 0:NT],
                    rhs=xdt.h[0:NT, (h0 + q) * 64:(h0 + q + 1) * 64], start=False, stop=last),
                    R=m.B() + xdt.B(), W=pd.B(), inc=last)

    def gate_accum(c, g, ysrc_tt, ysrc_ap):
        NT = c.NT
        k.op("dve", lambda e: e.tensor_tensor(out=gall.h[0:NT, g * 512:(g + 1) * 512], in0=ysrc_ap,
                                              in1=sz_tok.h[0:NT, g * 512:(g + 1) * 512], op=ALU.mult),
             R=ysrc_tt.B() + sz_tok.B(), W=gall.B())

    def gate_finish(c):
        NT = c.NT
        k.op("dve", lambda e: e.memset(st2.h[:, 0:1], 0.0), W=st2.B())
        k.op("act", lambda e: e.activation(out=gnb.h[0:NT, :], in_=gall.h[0:NT, :], func=AF.Square,
                                           accum_out=st2.h[0:NT, 0:1]),
             R=gall.B() + st2.B(), W=gnb.B() + st2.B())
        k.op("act", lambda e: e.activation(out=st2.h[0:NT, 1:2], in_=st2.h[0:NT, 0:1], func=AF.Ln,
                                           scale=1.0 / D_SSD, bias=epsb.h[0:NT, 0:1]), R=st2.B() + epsb.B(), W=st2.B())
        k.op("act", lambda e: e.activation(out=st2.h[0:NT, 2:3], in_=st2.h[0:NT, 1:2], func=AF.Exp,
                                           scale=-0.5), R=st2.B(), W=st2.B())
        k.op("act", lambda e: e.activation(out=gnb.h[0:NT, :], in_=gall.h[0:NT, :], func=AF.Copy,
                                           scale=st2.h[0:NT, 2:3]), R=gall.B() + st2.B(), W=gnb.B())
        for j0 in range(0, 32, 4):
            pt = nxt(PSW, "w")
            pv = pt.h[:].bitcast(BF16)
            for j in range(4):
                k.op("pe", lambda e, j=j: e.transpose(
                    out=pv[:, j * NT:(j + 1) * NT], in_=gnb.h[0:NT, (j0 + j) * 128:(j0 + j + 1) * 128],
                    identity=identb.h[0:NT, 0:NT]), R=gnb.B() + identb.B(), W=pt.B(), inc=(j == 3))
            k.op("dve", lambda e: e.tensor_tensor(
                out=ymT.h[:, 16 + j0:16 + j0 + 4, 0:NT],
                in0=pv[:, 0:4 * NT].rearrange("p (j t) -> p j t", t=NT),
                in1=gsT.h[:, j0:j0 + 4].unsqueeze(2).to_broadcast([128, 4, NT]), op=ALU.mult),
                R=pt.B() + gsT.B(), W=ymT.B(range(16 + j0, 16 + j0 + 4)))

    def ssd_prompt(c, prefix):
        NT = 128
        ssd_decays(c)
        dec, eac = sm["dec"], sm["eac"]
        if not prefix:
            ssd_intra_group_prep(c)
        for g in range(8):
            gs = slice(g * 512, (g + 1) * 512)
            if not prefix:
                pd = nxt(PSW, "w")
                ssd_intra_group(c, g, pd)
                po = nxt(PSW, "w")
                sb_ = nxt(sbg, "sbg")
                k.op("act", lambda e: e.activation(out=sb_.h[:, :], in_=S.h[:, gs], func=AF.Copy),
                     R=S.B(), W=sb_.B())
                k.op("pe", lambda e: e.matmul(po.h[:, 0:512], lhsT=CT.h[:, g, :], rhs=sb_.h[:, :],
                                              start=True, stop=True), R=CT.B(g) + sb_.B(), W=po.B())
                y1 = nxt(yw, "yw")
                k.op("dve", lambda e: e.tensor_tensor(
                    out=y1.h[:, :].rearrange("t (h p) -> t h p", p=64),
                    in0=po.h[:, 0:512].rearrange("t (h p) -> t h p", p=64),
                    in1=eac.h[:, g * 8:(g + 1) * 8].unsqueeze(2).to_broadcast([128, 8, 64]), op=ALU.mult),
                    R=po.B() + eac.B(), W=y1.B())
                k.op("dve", lambda e: e.tensor_tensor(out=y1.h[:, :], in0=y1.h[:, :], in1=pd.h[:, 0:512],
                                                      op=ALU.add), R=y1.B() + pd.B(), W=y1.B())
                gate_accum(c, g, y1, y1.h[:, :])
            pst = nxt(PSW, "w")
            k.op("pe", lambda e: e.matmul(pst.h[:, 0:512], lhsT=B_tok.h[:, g * 128:(g + 1) * 128],
                                          rhs=xdte.h[:, gs], start=True, stop=True),
                 R=B_tok.B() + xdte.B(), W=pst.B())
            k.op("dve", lambda e: e.tensor_tensor(
                out=S.h[:, gs].rearrange("n (h p) -> n h p", p=64),
                in0=S.h[:, gs].rearrange("n (h p) -> n h p", p=64),
                in1=dec.h[:, g * 8:(g + 1) * 8].unsqueeze(2).to_broadcast([128, 8, 64]), op=ALU.mult),
                R=S.B() + dec.B(), W=S.B())
            k.op("dve", lambda e: e.tensor_tensor(out=S.h[:, gs], in0=S.h[:, gs], in1=pst.h[:, 0:512],
                                                  op=ALU.add), R=S.B() + pst.B(), W=S.B())
        if not prefix:
            gate_finish(c)

    def ssd_sample(c, hT):
        NT = 64
        ssd_decays(c)
        ssd_intra_group_prep(c)
        eac, a_ = sm["eac"], sm["a"]
        dP = TT(mixg[0].h[:, :, :].rearrange("p a b -> p (a b)").rearrange("p (a b) -> p a b", b=16), "dP")
        dP.all = mixg[0].all
        aexp = gall
        k.op("dve", lambda e: e.tensor_copy(out=v3(aexp.h[0:64, :], 64), in_=hb3(a_.h[0:64, :], 64)),
             R=a_.B(), W=aexp.B())
        for hp0 in range(0, 32, 8):
            pp = nxt(PSW, "w")
            for q in range(8):
                hp = hp0 + q
                k.op("pe", lambda e, q=q, hp=hp: e.matmul(pp.h[:, q * 16:(q + 1) * 16],
                                                          lhsT=aexp.h[0:64, hp * 128:(hp + 1) * 128],
                                                          rhs=onesS[:, 0:16], start=True, stop=True),
                     R=aexp.B() + cst.B(), W=pp.B(), inc=(q == 7))
            k.op("act", lambda e: e.activation(out=dP.h[:, hp0:hp0 + 8, :],
                                               in_=pp.h[:, 0:128].rearrange("p (q b) -> p q b", b=16),
                                               func=AF.Exp), R=pp.B(), W=dP.B())
        for g in range(8):
            pd = nxt(PSW, "w")
            ssd_intra_group(c, g, pd)
            k.op("act", lambda e, g=g, pd=pd: e.activation(out=gall.h[0:NT, g * 512:(g + 1) * 512],
                                                           in_=pd.h[0:NT, 0:512], func=AF.Copy),
                 R=pd.B(), W=gall.B())
        s0 = TT(S.h[:, :].rearrange("p (a b) -> p a b", b=128), "s0")
        s0.all = S.all
        sn = s0
        s0t = TT(hT.h[:, :, :].rearrange("p a b -> p (a b)"), "s0t")
        s0t.B = lambda i=None: hT.B()
        xmb = TT(xnb.h[0:64, :], "xmb")
        xmb.all = xnb.all
        for b in range(16):
            k.dma("sp", s0.h[:], ssm_in[b].rearrange("(hp q) n -> q hp n", q=128), W=s0.B())
            for hp0 in range(0, 32, 4):
                pt = nxt(PSW, "w")
                for q in range(4):
                    k.op("pe", lambda e, q=q: e.transpose(out=pt.h[:, q * 128:(q + 1) * 128],
                                                          in_=s0.h[:, hp0 + q, :], identity=ident),
                         R=s0.B() + cst.B(), W=pt.B(), inc=(q == 3))
                k.op("act", lambda e: e.activation(out=s0t.h[:, hp0 * 128:(hp0 + 4) * 128],
                                                   in_=pt.h[:, 0:512], func=AF.Copy), R=pt.B(), W=s0t.B())
            for g in range(8):
                po = nxt(PSW, "w")
                k.op("pe", lambda e, g=g: e.matmul(po.h[0:64, 0:512], lhsT=CT.h[:, g, 0:64],
                                                   rhs=s0t.h[:, g * 512:(g + 1) * 512], start=True, stop=True),
                     R=CT.B(g) + s0t.B(), W=po.B())
                y1 = nxt(yw, "yw")
                k.op("dve", lambda e, g=g, y1=y1, po=po: e.tensor_tensor(
                    out=y1.h[0:64, :].rearrange("t (h p) -> t h p", p=64),
                    in0=po.h[0:64, 0:512].rearrange("t (h p) -> t h p", p=64),
                    in1=eac.h[0:64, g * 8:(g + 1) * 8].unsqueeze(2).to_broadcast([64, 8, 64]), op=ALU.mult),
                    R=po.B() + eac.B(), W=y1.B())
                k.op("dve", lambda e, g=g, y1=y1: e.scalar_tensor_tensor(
                    out=gall.h[0:64, g * 512:(g + 1) * 512], in0=y1.h[0:64, :], scalar=rowmask[:, b:b + 1],
                    in1=gall.h[0:64, g * 512:(g + 1) * 512], op0=ALU.mult, op1=ALU.add),
                    R=y1.B() + cst.B() + gall.B(), W=gall.B())
            k.op("dve", lambda e: e.tensor_scalar(out=xmb.h[:, :], in0=xdte.h[0:64, :],
                                                  scalar1=rowmask[:, b:b + 1], scalar2=None, op0=ALU.mult),
                 R=xdte.B() + cst.B(), W=xmb.B())
            for hp0 in range(0, 32, 4):
                g = hp0 // 4
                pn = nxt(PSW, "w")
                for q in range(4):
                    hp = hp0 + q
                    k.op("pe", lambda e, q=q, hp=hp: e.matmul(pn.h[:, q * 128:(q + 1) * 128],
                                                              lhsT=xmb.h[:, hp * 128:(hp + 1) * 128],
                                                              rhs=B_tok.h[0:64, g * 128:(g + 1) * 128],
                                                              start=True, stop=True),
                         R=xmb.B() + B_tok.B(), W=pn.B(), inc=(q == 3))
                for q in range(4):
                    hp = hp0 + q
                    k.op("dve", lambda e, q=q, hp=hp: e.scalar_tensor_tensor(
                        out=sn.h[:, hp, :], in0=s0.h[:, hp, :], scalar=dP.h[:, hp, b:b + 1],
                        in1=pn.h[:, q * 128:(q + 1) * 128], op0=ALU.mult, op1=ALU.add),
                        R=s0.B() + dP.B() + pn.B(), W=sn.B())
            k.dma("sp", o_ssm_s[b].rearrange("(hp q) n -> q hp n", q=128), sn.h[:], R=sn.B())
        for g in range(8):
            gate_accum(c, g, gall, gall.h[0:64, g * 512:(g + 1) * 512])
        gate_finish(c)

    def res_ctx(c, rhs_tt, goff, xr):
        NT = c.NT
        stt = {"mg": None}

        def cons(tag, pt, pap):
            j = tag
            q = j % 4
            if q == 0:
                stt["mg"] = nxt(mixg, "mixg")
            mg = stt["mg"]
            dst = mg.h[:, q:q + 1, 0:NT]
            k.op("dve", lambda e: e.tensor_tensor(
                out=c.view(dst), in0=c.view(pap.unsqueeze(1)), in1=c.mod_ap(modT, goff, j, 1), op=ALU.mult),
                R=pt.B() + modT.B(), W=mg.B())
            if q == 3:
                j0 = j - 3
                p2 = nxt(PSW, "w")
                for qq in range(4):
                    k.op("pe", lambda e, qq=qq: e.transpose(out=p2.h[0:NT, qq * 128:(qq + 1) * 128],
                                                            in_=mg.h[:, qq, 0:NT], identity=ident),
                         R=mg.B() + cst.B(), W=p2.B(), inc=(qq == 3))
                k.op("dve", lambda e: e.tensor_tensor(out=xr.h[0:NT, j0 * 128:(j0 + 4) * 128],
                                                      in0=xr.h[0:NT, j0 * 128:(j0 + 4) * 128],
                                                      in1=p2.h[0:NT, 0:512], op=ALU.add),
                     R=xr.B() + p2.B(), W=xr.B())

        return (lambda kc: rhs_tt.h[:, kc, 0:NT], lambda kc: rhs_tt.B(kc), NT, cons)

    RES_BLOCKS = [(j * 128, 128, j) for j in range(32)]

    def up_ctx(c, hT, actT):
        NT = c.NT
        stt = {}

        def cons(tag, pt, pap):
            kind, i = tag
            blk = i if kind == "g" else 86 + i
            ca = conv_block(c, pt, pap, tl_ffn, blk, wffn, 3, bffn.h[:, blk:blk + 1])
            if kind == "g":
                t = nxt(cvc, "cvc")
                k.op("act", lambda e: e.activation(out=t.h[:, 0:NT], in_=ca.h[:, 0:NT], func=AF.Silu),
                     R=ca.B(), W=t.B())
                stt["g"] = t
            else:
                t = stt["g"]
                k.op("dve", lambda e: e.tensor_tensor(out=actT.h[:, i, 0:NT], in0=t.h[:, 0:NT],
                                                      in1=ca.h[:, 0:NT], op=ALU.mult),
                     R=t.B() + ca.B(), W=actT.B(i))

        return (lambda kc: hT.h[:, kc, 0:NT], lambda kc: hT.B(kc), NT, cons)

    UP_BLOCKS = []
    for i in range(86):
        UP_BLOCKS += [(i * 128, 128, ("g", i)), (D_FF + i * 128, 128, ("v", i))]

    def final_out(c, xr, dst):
        NT = c.NT
        k.op("dve", lambda e: e.memset(st1.h[:, 4:5], 0.0), W=st1.B())
        k.op("act", lambda e: e.activation(out=xnb.h[0:NT, :], in_=xr.h[0:NT, :], func=AF.Square,
                                           accum_out=st1.h[0:NT, 4:5]),
             R=xr.B() + st1.B(), W=xnb.B() + st1.B())
        k.op("act", lambda e: e.activation(out=st1.h[0:NT, 5:6], in_=st1.h[0:NT, 4:5], func=AF.Ln,
                                           scale=1.0 / D, bias=epsb.h[0:NT, 0:1]), R=st1.B() + epsb.B(), W=st1.B())
        k.op("act", lambda e: e.activation(out=st1.h[0:NT, 6:7], in_=st1.h[0:NT, 5:6], func=AF.Exp,
                                           scale=-0.5), R=st1.B(), W=st1.B())
        k.dma("sp", gfin.h[:], gfin_d.partition_broadcast(128), W=gfin.B())
        k.op("dve", lambda e: e.scalar_tensor_tensor(out=xr.h[0:NT, :], in0=xr.h[0:NT, :],
                                                     scalar=st1.h[0:NT, 6:7], in1=gfin.h[0:NT, :],
                                                     op0=ALU.mult, op1=ALU.mult),
             R=xr.B() + st1.B() + gfin.B(), W=xr.B())
        k.dma("sp", dst, xr.h[0:NT, :], R=xr.B())

    for tl in (tl_sc, tl_ssd, tl_ffn):
        k.op("dve", lambda e, tl=tl: e.memset(tl.h[:], 0.0), W=tl.B())
    k.op("dve", lambda e: e.memset(S.h[:], 0.0), W=S.B())

    cP = mk_cfg(False)
    cH = mk_cfg(False)
    cH.halo = True
    cS = mk_cfg(True)

    def phaseA(c, ci, src, ssd_fn):
        xr, hT_ = xres[ci], hTs[ci]
        k.dma("sp", xr.h[0:c.NT, :], src, W=xr.B())
        norm_to_hT(c, xr, gm1, SH1, hT_)
        in_proj(c, False, hT_)
        ssd_fn(hT_)
        proj(w_out, RES_BLOCKS, 48, [res_ctx(c, ymT, GT1, xr)])
        norm_to_hT(c, xr, gm2, SH2, hT_)

    for t in range(NPRE):
        xr = xres[0]
        k.dma("sp", xr.h[:, :], xp[t * 128:(t + 1) * 128, :], W=xr.B())
        norm_to_hT(cP, xr, gm1, SH1, hTs[0])
        in_proj(cP, True, hTs[0])
        ssd_prompt(cP, True)

    tiles = []
    for t in range(NMAIN):
        tiles.append((cH if t == 0 else cP, xm[t * 128:(t + 1) * 128, :], y_main[t * 128:(t + 1) * 128, :],
                      (lambda c_: (lambda hT_: ssd_prompt(c_, False)))(cH if t == 0 else cP), t))
    if DO_SAMPLE:
        tiles.append((cS, xs_in, y_samp, lambda hT_: ssd_sample(cS, hT_), -1))

    def prompt_state_out():
        k.dma("sp", o_sc_p, tl_sc.h[:, :, 0:2], R=tl_sc.B())
        k.dma("sp", o_ssd_p, tl_ssd.h[:, :, 0:3], R=tl_ssd.B())
        for hp0 in range(0, 32, 4):
            pt = nxt(PSW, "w")
            for q in range(4):
                k.op("pe", lambda e, q=q: e.transpose(out=pt.h[:, q * 128:(q + 1) * 128],
                                                      in_=S.h[:, (hp0 + q) * 128:(hp0 + q + 1) * 128],
                                                      identity=ident),
                     R=S.B() + cst.B(), W=pt.B(), inc=(q == 3))
            k.op("act", lambda e: e.activation(out=gall.h[:, hp0 * 128:(hp0 + 4) * 128], in_=pt.h[:, 0:512],
                                               func=AF.Copy), R=pt.B(), W=gall.B())
        k.dma("sp", o_ssm_p.rearrange("(hp q) n -> q hp n", q=128),
              gall.h[:, :].rearrange("q (hp n) -> q hp n", n=128), R=gall.B())

    for p0 in range(0, len(tiles), 2):
        pair = tiles[p0:p0 + 2]
        for ci, (c, src, dst, ssd_fn, t) in enumerate(pair):
            if t == -1:
                prompt_state_out()
                TLS["tl_sc"] = (st_sc, o_sc_s)
                TLS["tl_ssd"] = (st_ssd, o_ssd_s)
                TLS["tl_ffn"] = (st_ffn, o_ffn_s)
            phaseA(c, ci, src, ssd_fn)
            if t == 0:
                k.op("dve", lambda e: e.tensor_scalar(out=S.h[:], in0=S.h[:], scalar1=flag.h[:, 0:1],
                                                      scalar2=None, op0=ALU.mult), R=S.B() + flag.B(), W=S.B())
        k.barrier()
        proj(w_up, UP_BLOCKS, 32, [up_ctx(c, hTs[ci], actTs[ci]) for ci, (c, _, _, _, _) in enumerate(pair)])
        proj(w_down, RES_BLOCKS, 86, [res_ctx(c, actTs[ci], GT2, xres[ci]) for ci, (c, _, _, _, _) in enumerate(pair)])
        k.barrier()
        for ci, (c, src, dst, ssd_fn, t) in enumerate(pair):
            final_out(c, xres[ci], dst)
    if not DO_SAMPLE:
        prompt_state_out()
    k.dma("sp", o_ffn_p, tl_ffn.h[:, :, 0:2], R=tl_ffn.B())
    k.finish()


def _consts():
    c = np.zeros((128, 640), np.float32)
    c[:, 0:128] = np.eye(128, dtype=np.float32)
    c[:, 128:256] = np.triu(np.ones((128, 128), np.float32))
    c[:, 256:384] = 1.0
    idx = np.arange(64)
    tt, bb = idx // 16, idx % 16
    same = (bb[:, None] == bb[None, :])
    c[0:64, 384:448] = (same & (tt[:, None] <= tt[None, :])).astype(np.float32)
    c[0:64, 448:512] = same.astype(np.float32)
    c[0:64, 512:528] = (bb[:, None] == np.arange(16)[None, :]).astype(np.float32)
    return c


def _fm(v, nblk):
    return np.ascontiguousarray(v.reshape(nblk, 128).T)


def _fmk(w, nblk):
    K = w.shape[0]
    return np.ascontiguousarray(w.T.reshape(nblk, 128, K).transpose(1, 0, 2))


def _st_in(st, nblk):
    R = st.shape[1]
    a = st.transpose(2, 1, 0).reshape(nblk, 128, R * 16)
    return np.ascontiguousarray(a.transpose(1, 0, 2))


def _st_out(a, nblk, R):
    return a.transpose(1, 0, 2).reshape(nblk * 128, R, 16).transpose(2, 1, 0)


def kernel(x_prompt, x_sample, c_prompt, c_sample, state_sc_conv, state_ssd_conv,
           state_ssm, state_ffn_conv, w_ada, b_ada, g_norm1, w_in, w_sc_conv,
           w_ssd_conv, b_ssd_conv, dt_bias, a_log, d_skip, g_ssd_norm, w_out,
           g_norm2, w_up, w_ffn_conv, b_ffn_conv, w_down, g_final):
    f = lambda a: np.ascontiguousarray(np.asarray(a, dtype=np.float32))
    x_prompt, x_sample, c_prompt, c_sample = f(x_prompt), f(x_sample), f(c_prompt), f(c_sample)
    nc = bass.Bass("TRN2", target_bir_lowering=False)
    build(nc)
    shared = {
        "w_ada": f(w_ada[0]), "b_adaT": _fm(f(b_ada[0]), 192), "g1T": _fm(f(g_norm1[0]), 32),
        "g2T": _fm(f(g_norm2[0]), 32), "gsT": _fm(f(g_ssd_norm[0]), 32), "w_in": f(w_in[0]),
        "wsc": _fmk(f(w_sc_conv[0]), 16), "wssd": _fmk(f(w_ssd_conv[0]), 48),
        "bssd": _fm(f(b_ssd_conv[0]), 48), "dtb": f(dt_bias[0]).reshape(1, 64),
        "alog": f(a_log[0]).reshape(1, 64), "dsk": f(d_skip[0]).reshape(1, 64),
        "w_out": f(w_out[0]), "w_up": f(w_up[0]), "wffn": _fmk(f(w_ffn_conv[0]), 172),
        "bffn": _fm(f(b_ffn_conv[0]), 172), "w_down": f(w_down[0]),
        "gfin": f(g_final).reshape(1, D), "consts": _consts(),
    }
    in_maps = []
    npre = max(NPRE, 1) * 128
    for c in range(NCORE):
        s, half = c // 2, c % 2
        m = dict(shared)
        xm = np.zeros((NMAIN * 128, D), np.float32)
        xp = np.zeros((npre, D), np.float32)
        if half == 0:
            xm[128:] = x_prompt[s, 0:(NMAIN - 1) * 128]
        else:
            xm[:] = x_prompt[s, 896:896 + NMAIN * 128]
            xp[:NPRE * 128] = x_prompt[s, 0:NPRE * 128]
        m["xm"], m["xp"] = xm, xp
        m["flag"] = np.full((128, 1), float(half), np.float32)
        bs = slice(16 * c, 16 * c + 16)
        m["cvec"] = np.ascontiguousarray(np.concatenate([c_prompt[s:s + 1], c_sample[bs]], 0))
        m["xs_in"] = np.ascontiguousarray(x_sample[bs].transpose(1, 0, 2).reshape(64, D))
        m["st_sc"] = _st_in(f(state_sc_conv[0, bs]), 16)
        m["st_ssd"] = _st_in(f(state_ssd_conv[0, bs]), 48)
        m["st_ffn"] = _st_in(f(state_ffn_conv[0, bs]), 172)
        m["ssm_in"] = f(state_ssm[0, bs]).reshape(16, 4096, 128)
        in_maps.append(m)
    res = run_bass_kernel_spmd(nc, in_maps, core_ids=list(range(NCORE)))
    R = res.results
    kernel.last = R
    B, L = x_prompt.shape[0], x_prompt.shape[1]
    y_prompt = np.zeros((B, L, D), np.float32)
    y_sample = np.zeros((128, 4, D), np.float32)
    p_sc = np.zeros((1, B, 2, D_SC), np.float32)
    p_ssd = np.zeros((1, B, 3, D_XBC), np.float32)
    p_ssm = np.zeros((1, B, NH, HP, NS), np.float32)
    p_ffn = np.zeros((1, B, 2, 2 * D_FF), np.float32)
    s_sc = np.zeros((1, 128, 2, D_SC), np.float32)
    s_ssd = np.zeros((1, 128, 3, D_XBC), np.float32)
    s_ssm = np.zeros((1, 128, NH, HP, NS), np.float32)
    s_ffn = np.zeros((1, 128, 2, 2 * D_FF), np.float32)
    for c in range(NCORE):
        s, half = c // 2, c % 2
        r = R[c]
        nm = (NMAIN - 1) * 128
        y_prompt[s, half * 1024:half * 1024 + nm] = r["y_main"][128:]
        bs = slice(16 * c, 16 * c + 16)
        y_sample[bs] = r["y_samp"].reshape(4, 16, D).transpose(1, 0, 2)
        if half == 1:
            p_sc[0, s] = r["o_sc_p"].transpose(1, 0, 2).reshape(D_SC, 2).T
            p_ssd[0, s] = r["o_ssd_p"].transpose(1, 0, 2).reshape(D_XBC, 3).T
            p_ffn[0, s] = r["o_ffn_p"].transpose(1, 0, 2).reshape(2 * D_FF, 2).T
            p_ssm[0, s] = r["o_ssm_p"].reshape(NH, HP, NS)
        s_sc[0, bs] = _st_out(r["o_sc_s"], 16, 2)
        s_ssd[0, bs] = _st_out(r["o_ssd_s"], 48, 3)
        s_ffn[0, bs] = _st_out(r["o_ffn_s"], 172, 2)
        s_ssm[0, bs] = r["o_ssm_s"].reshape(16, NH, HP, NS)
    return (y_prompt, y_sample, p_sc, p_ssd, p_ssm, p_ffn, s_sc, s_ssd, s_ssm, s_ffn)
```
